# Optimizing a Trainium2 kernel written in Bass

```python
import math
import jax
import jax.numpy as jnp
from jax import lax
import numpy as np

D_MODEL = 1024
BATCH = 8
SEQ = 2048
DEPTH = 2

CHUNK = 64
Q_BLOCK = 128
N_MEM = 256
A_HEADS = 8
A_HEAD_DIM = 64
A_VAL_DIM = 2 * A_HEAD_DIM
A_QK_WIDTH = A_HEADS * 2 * A_HEAD_DIM
A_V_WIDTH = A_HEADS * A_VAL_DIM
ROPE_THETA = 10000.0
POOL_WINDOWS = (2, 4, 8, 16)
POOL_GROUPS = 4
POOL_WIDTH = D_MODEL
POOL_GROUP_DIM = POOL_WIDTH // POOL_GROUPS
R_HEAD_DIM = 64
R_WIDTH = D_MODEL
R_HEADS = R_WIDTH // R_HEAD_DIM
DECAY_LORA = max(32, int(round(1.8 * D_MODEL ** 0.5 / 32)) * 32)
AAA_LORA = max(32, int(round(1.8 * D_MODEL ** 0.5 / 32)) * 32)
GATE_LORA = max(32, int(round(0.6 * D_MODEL ** 0.8 / 32)) * 32)
LORA_WIDTH = DECAY_LORA + AAA_LORA + GATE_LORA
RWKV_IN_WIDTH = 3 * R_WIDTH + LORA_WIDTH
N_BRANCH = 3
P_IN = 2 * A_QK_WIDTH + A_V_WIDTH + POOL_WIDTH + RWKV_IN_WIDTH + N_BRANCH * D_MODEL
X_HEADS = 4
X_HEAD_DIM = D_MODEL // X_HEADS
D_FF = 4 * D_MODEL
DN_ALPHA = (2 * DEPTH) ** 0.25
DN_BETA = (8 * DEPTH) ** -0.25
LN_EPS = 1e-5
RMS_EPS = 1e-5
RWKV_GN_EPS = 64e-5
NEG_INF = -1e30
F32 = jnp.float32

kernel_name = 'hybrid_diffattn_pool_rwkv7_deepnorm'


def _split_points(sizes):
    pts, acc = [], 0
    for s in sizes[:-1]:
        acc += s
        pts.append(acc)
    return pts


def layer_norm(x, g, b, eps=LN_EPS):
    xf = x.astype(F32)
    mu = jnp.mean(xf, -1, keepdims=True)
    var = jnp.mean(jnp.square(xf - mu), -1, keepdims=True)
    return ((xf - mu) * lax.rsqrt(var + eps) * g.astype(F32) + b.astype(F32)).astype(x.dtype)


def rope_tables(positions, dim):
    inv_freq = 1.0 / (ROPE_THETA ** (jnp.arange(0, dim, 2, dtype=F32) / dim))
    ang = positions.astype(F32)[..., None] * inv_freq
    return jnp.cos(ang), jnp.sin(ang)


def apply_rope(t, cos, sin):
    c = cos[:, :, None, None, :]
    s = sin[:, :, None, None, :]
    t1, t2 = jnp.split(t.astype(F32), 2, axis=-1)
    return jnp.concatenate([t1 * c - t2 * s, t2 * c + t1 * s], axis=-1)


def diff_attention(q, k, v, lam, subln_g, lam_init):
    bsz, seq = q.shape[0], q.shape[1]
    scale = A_HEAD_DIM ** -0.5
    vf = v.astype(F32)
    chunk_id = jnp.arange(seq) // CHUNK
    outs = []
    for i in range(seq // Q_BLOCK):
        lo, hi = i * Q_BLOCK, (i + 1) * Q_BLOCK
        s = jnp.einsum('bqhmd,bkhmd->bhmqk', q[:, lo:hi], k[:, :hi]) * scale
        allowed = chunk_id[None, :hi] <= chunk_id[lo:hi, None]
        s = jnp.where(allowed, s, NEG_INF)
        p = jax.nn.softmax(s, axis=-1)
        a = p[:, :, 0] - lam * p[:, :, 1]
        outs.append(jnp.einsum('bhqk,bkhd->bqhd', a, vf[:, :hi]))
    o = jnp.concatenate(outs, axis=1)
    o = o * lax.rsqrt(jnp.mean(jnp.square(o), -1, keepdims=True) + RMS_EPS) * subln_g.astype(F32)
    return (o * (1.0 - lam_init)).reshape(bsz, seq, A_V_WIDTH).astype(v.dtype)


def pool_mixer(p, w_grp, scale):
    bsz, seq, _ = p.shape
    pf = p.astype(F32).reshape(bsz, seq, POOL_GROUPS, POOL_GROUP_DIM)
    cs = jnp.concatenate([jnp.zeros((bsz, 1, POOL_GROUPS, POOL_GROUP_DIM), F32), jnp.cumsum(pf, axis=1)], axis=1)
    t = jnp.arange(seq)
    outs = []
    for g, w in enumerate(POOL_WINDOWS):
        cs_g = cs[:, :, g]
        window_sum = cs_g[:, 1:] - cs_g[:, jnp.maximum(t + 1 - w, 0)]
        count = jnp.minimum(t + 1, w).astype(F32)[None, :, None]
        outs.append(window_sum / count - pf[:, :, g])
    d = jnp.stack(outs, axis=2)
    y = jnp.einsum('bsgc,gce->bsge', d, w_grp.astype(F32)).reshape(bsz, seq, POOL_WIDTH)
    return (y * scale.astype(F32)).astype(p.dtype)


def token_shift(u, mu):
    prev = jnp.pad(u, ((0, 0), (1, 0), (0, 0)))[:, :-1]
    return u + (prev - u) * mu


def rwkv7_mixer(r, k, v, xw, xa, xg, w0, w2, a0, a2, g2, k_k, k_a, r_k, lnx_g, lnx_b):
    bsz, seq, width = r.shape
    out_dtype = r.dtype
    r, k, v, xw, xa, xg = [t.astype(F32) for t in (r, k, v, xw, xa, xg)]
    log_w = -jax.nn.softplus(-(w0.astype(F32) + jnp.tanh(xw) @ w2.astype(F32))) - 0.5
    decay = jnp.exp(-jnp.exp(log_w))
    a = jax.nn.sigmoid(a0.astype(F32) + xa @ a2.astype(F32))
    g = jax.nn.sigmoid(xg) @ g2.astype(F32)
    heads = lambda t: t.reshape(bsz, seq, R_HEADS, R_HEAD_DIM)
    kk = heads(k * k_k.astype(F32))
    kk = kk / jnp.maximum(jnp.sqrt(jnp.sum(jnp.square(kk), -1, keepdims=True)), 1e-12)
    k = k * (1.0 + (a - 1.0) * k_a.astype(F32))
    rh, kh, vh, ah = heads(r), heads(k), heads(v), heads(a)

    def step(state, inp):
        r_t, w_t, k_t, v_t, a_t, b_t = inp
        state = (state * w_t[:, :, None, :]
                 + jnp.einsum('bhij,bhj->bhi', state, a_t)[..., None] * b_t[:, :, None, :]
                 + v_t[..., None] * k_t[:, :, None, :])
        return state, jnp.einsum('bhij,bhj->bhi', state, r_t)

    tm = lambda t: jnp.swapaxes(t, 0, 1)
    s0 = jnp.zeros((bsz, R_HEADS, R_HEAD_DIM, R_HEAD_DIM), F32)
    _, y = lax.scan(step, s0, (tm(rh), tm(heads(decay)), tm(kh), tm(vh), tm(-kk), tm(kk * ah)))
    y = tm(y)
    mu = jnp.mean(y, -1, keepdims=True)
    var = jnp.mean(jnp.square(y - mu), -1, keepdims=True)
    y = ((y - mu) * lax.rsqrt(var + RWKV_GN_EPS)).reshape(bsz, seq, width)
    y = y * lnx_g.astype(F32) + lnx_b.astype(F32)
    bonus = jnp.sum(rh * kh * r_k.astype(F32), -1, keepdims=True) * vh
    y = (y + bonus.reshape(bsz, seq, width)) * g
    return y.astype(out_dtype)


def cross_attention(h, mem, w_q, w_kv, w_o):
    bsz, seq, _ = h.shape
    q = (h @ w_q).reshape(bsz, seq, X_HEADS, X_HEAD_DIM).astype(F32)
    k, v = jnp.split(mem @ w_kv, 2, axis=-1)
    k = k.reshape(bsz, -1, X_HEADS, X_HEAD_DIM).astype(F32)
    v = v.reshape(bsz, -1, X_HEADS, X_HEAD_DIM).astype(F32)
    p = jax.nn.softmax(jnp.einsum('bqhd,bkhd->bhqk', q, k) * X_HEAD_DIM ** -0.5, axis=-1)
    o = jnp.einsum('bhqk,bkhd->bqhd', p, v).reshape(bsz, seq, D_MODEL).astype(h.dtype)
    return o @ w_o


def setup_inputs(seed: int = 0) -> dict:
    key = jax.random.key(seed)
    ks = iter(jax.random.split(key, 64))
    L = DEPTH

    def nrm(shape, scale):
        return jax.random.normal(next(ks), shape, F32) * scale

    def gain(shape):
        return 1.0 + nrm(shape, 0.02)

    x = nrm((BATCH, SEQ, D_MODEL), 1.0)
    mem = nrm((BATCH, N_MEM, D_MODEL), 1.0)
    start = jax.random.randint(next(ks), (BATCH, 1), 0, 4096, dtype=jnp.int32)
    positions = start + jnp.arange(SEQ, dtype=jnp.int32)[None, :]
    ramp = (jnp.arange(R_WIDTH, dtype=F32) / (R_WIDTH - 1)) ** 0.9
    return {
        'x': x, 'mem': mem, 'positions': positions,
        'ln_in_g': gain((D_MODEL,)), 'ln_in_b': nrm((D_MODEL,), 0.02),
        'w_in': nrm((L, D_MODEL, P_IN), D_MODEL ** -0.5),
        'b_gate': nrm((L, N_BRANCH, D_MODEL), 0.02),
        'lam_q1': nrm((L, A_HEAD_DIM), 0.1), 'lam_k1': nrm((L, A_HEAD_DIM), 0.1),
        'lam_q2': nrm((L, A_HEAD_DIM), 0.1), 'lam_k2': nrm((L, A_HEAD_DIM), 0.1),
        'attn_subln_g': gain((L, A_VAL_DIM)),
        'w_br_attn': nrm((L, A_V_WIDTH, D_MODEL), A_V_WIDTH ** -0.5),
        'pool_w': nrm((L, POOL_GROUPS, POOL_GROUP_DIM, POOL_GROUP_DIM), POOL_GROUP_DIM ** -0.5),
        'pool_scale': gain((L, POOL_WIDTH)),
        'w_br_pool': nrm((L, POOL_WIDTH, D_MODEL), POOL_WIDTH ** -0.5),
        'rwkv_mu': jax.random.uniform(next(ks), (L, RWKV_IN_WIDTH), F32),
        'rwkv_w0': -6.0 + 5.0 * ramp[None, :] + nrm((L, R_WIDTH), 0.1),
        'rwkv_w2': nrm((L, DECAY_LORA, R_WIDTH), 0.1 * DECAY_LORA ** -0.5),
        'rwkv_a0': nrm((L, R_WIDTH), 0.1),
        'rwkv_a2': nrm((L, AAA_LORA, R_WIDTH), 0.1 * AAA_LORA ** -0.5),
        'rwkv_g2': nrm((L, GATE_LORA, R_WIDTH), GATE_LORA ** -0.5),
        'rwkv_k_k': 0.85 + nrm((L, R_WIDTH), 0.02),
        'rwkv_k_a': gain((L, R_WIDTH)),
        'rwkv_r_k': nrm((L, R_HEADS, R_HEAD_DIM), 0.1),
        'rwkv_lnx_g': gain((L, R_WIDTH)), 'rwkv_lnx_b': nrm((L, R_WIDTH), 0.02),
        'w_br_rwkv': nrm((L, R_WIDTH, D_MODEL), R_WIDTH ** -0.5),
        'w_out': nrm((L, D_MODEL, D_MODEL), DN_BETA * D_MODEL ** -0.5),
        'ln1_g': gain((L, D_MODEL)), 'ln1_b': nrm((L, D_MODEL), 0.02),
        'w_xq': nrm((L, D_MODEL, D_MODEL), D_MODEL ** -0.5),
        'w_xkv': nrm((L, D_MODEL, 2 * D_MODEL), D_MODEL ** -0.5),
        'w_xo': nrm((L, D_MODEL, D_MODEL), DN_BETA * D_MODEL ** -0.5),
        'ln2_g': gain((L, D_MODEL)), 'ln2_b': nrm((L, D_MODEL), 0.02),
        'w_ff1': nrm((L, D_MODEL, D_FF), D_MODEL ** -0.5),
        'w_ff2': nrm((L, D_FF, D_MODEL), DN_BETA * D_FF ** -0.5),
        'ln3_g': gain((L, D_MODEL)), 'ln3_b': nrm((L, D_MODEL), 0.02),
    }


def reference(x, mem, positions, ln_in_g, ln_in_b, w_in, b_gate,
              lam_q1, lam_k1, lam_q2, lam_k2, attn_subln_g, w_br_attn,
              pool_w, pool_scale, w_br_pool,
              rwkv_mu, rwkv_w0, rwkv_w2, rwkv_a0, rwkv_a2, rwkv_g2, rwkv_k_k, rwkv_k_a, rwkv_r_k,
              rwkv_lnx_g, rwkv_lnx_b, w_br_rwkv,
              w_out, ln1_g, ln1_b, w_xq, w_xkv, w_xo, ln2_g, ln2_b,
              w_ff1, w_ff2, ln3_g, ln3_b):
    bsz, seq, _ = x.shape
    in_pts = _split_points([A_QK_WIDTH, A_QK_WIDTH, A_V_WIDTH, POOL_WIDTH, RWKV_IN_WIDTH, N_BRANCH * D_MODEL])
    rwkv_pts = _split_points([R_WIDTH, R_WIDTH, R_WIDTH, DECAY_LORA, AAA_LORA, GATE_LORA])
    cos, sin = rope_tables(positions, A_HEAD_DIM)
    h = layer_norm(x, ln_in_g, ln_in_b)
    for l in range(DEPTH):
        u = h @ w_in[l]
        u_q, u_k, u_v, u_pool, u_rwkv, u_gate = jnp.split(u, in_pts, axis=-1)
        q = apply_rope(u_q.reshape(bsz, seq, A_HEADS, 2, A_HEAD_DIM), cos, sin)
        k = apply_rope(u_k.reshape(bsz, seq, A_HEADS, 2, A_HEAD_DIM), cos, sin)
        v = u_v.reshape(bsz, seq, A_HEADS, A_VAL_DIM)
        lam_init = 0.8 - 0.6 * math.exp(-0.3 * l)
        lam = (jnp.exp(jnp.sum(lam_q1[l].astype(F32) * lam_k1[l].astype(F32)))
               - jnp.exp(jnp.sum(lam_q2[l].astype(F32) * lam_k2[l].astype(F32))) + lam_init)
        y_a = diff_attention(q, k, v, lam, attn_subln_g[l], lam_init) @ w_br_attn[l]
        y_b = pool_mixer(u_pool, pool_w[l], pool_scale[l]) @ w_br_pool[l]
        r_r, r_k, r_v, r_xw, r_xa, r_xg = jnp.split(token_shift(u_rwkv, rwkv_mu[l]), rwkv_pts, axis=-1)
        y_c = rwkv7_mixer(r_r, r_k, r_v, r_xw, r_xa, r_xg, rwkv_w0[l], rwkv_w2[l], rwkv_a0[l], rwkv_a2[l],
                          rwkv_g2[l], rwkv_k_k[l], rwkv_k_a[l], rwkv_r_k[l], rwkv_lnx_g[l], rwkv_lnx_b[l]) @ w_br_rwkv[l]
        gates = jax.nn.sigmoid((u_gate.reshape(bsz, seq, N_BRANCH, D_MODEL) + b_gate[l]).astype(F32))
        merged = gates[:, :, 0] * y_a + gates[:, :, 1] * y_b + gates[:, :, 2] * y_c
        h = layer_norm(DN_ALPHA * h + merged.astype(h.dtype) @ w_out[l], ln1_g[l], ln1_b[l])
        h = layer_norm(DN_ALPHA * h + cross_attention(h, mem, w_xq[l], w_xkv[l], w_xo[l]), ln2_g[l], ln2_b[l])
        ff = jnp.square(jax.nn.relu(h @ w_ff1[l])) @ w_ff2[l]
        h = layer_norm(DN_ALPHA * h + ff, ln3_g[l], ln3_b[l])
    return h
```

```python
import math
import numpy as np
import concourse.bass as bass
import concourse.mybir as mybir
from concourse.bass_utils import run_bass_kernel_spmd

F32 = mybir.dt.float32
BF16 = mybir.dt.bfloat16
I32 = mybir.dt.int32
AF = mybir.ActivationFunctionType
ALU = mybir.AluOpType
AX = mybir.AxisListType

S = 2048
D = 1024
NB = 4
TB = 512
DEPTH = 2
P_IN = 10528
DN_ALPHA = (2 * DEPTH) ** 0.25
LN_EPS = 1e-5
RMS_EPS = 1e-5
GN_EPS = 64e-5
CH = 64
NCH = S // CH


class Tok:
    __slots__ = ("name", "writer", "readers")

    def __init__(self, name):
        self.name = name
        self.writer = None
        self.readers = {}


class Ins:
    __slots__ = ("eng", "fn", "deps", "signal", "dma", "sem", "semval", "sigval", "slotdep", "idx")


class Prog:
    ENGS = ("pe", "act", "dve", "pool", "sp")
    RING = 12

    def __init__(self, nc):
        self.nc = nc
        self.streams = {e: [] for e in self.ENGS}
        self.n = 0
        self.engsem = {e: nc.alloc_semaphore("cs_" + e) for e in self.ENGS}
        self.rings = {}
        self.ringpos = {}
        self.ringval = {}
        for q in ("sp", "pool", "act"):
            self.rings[q] = [nc.alloc_semaphore("dq_%s_%d" % (q, i)) for i in range(self.RING)]
            self.ringpos[q] = 0
            self.ringval[q] = [0] * self.RING
        self.toks = {}
        self.pending_dma = []

    def tok(self, name):
        t = self.toks.get(name)
        if t is None:
            t = self.toks[name] = Tok(name)
        return t

    def add(self, eng, fn, reads=(), writes=(), dma=False, extra_deps=()):
        ins = Ins()
        ins.eng = eng
        ins.fn = fn
        ins.dma = dma
        ins.signal = False
        ins.sem = None
        ins.semval = 0
        ins.sigval = 0
        ins.slotdep = None
        ins.idx = self.n
        self.n += 1
        reads = [self.tok(t) if isinstance(t, str) else t for t in reads]
        writes = [self.tok(t) if isinstance(t, str) else t for t in writes]
        deps = {}
        for t in reads:
            if t.writer is not None:
                deps[t.writer.idx] = t.writer
            if t.name.startswith(("ps", "pb")):
                for r in t.readers.values():
                    if r.eng != eng:
                        deps[r.idx] = r
        for t in writes:
            if t.writer is not None:
                deps[t.writer.idx] = t.writer
            for r in t.readers.values():
                deps[r.idx] = r
        for d in extra_deps:
            deps[d.idx] = d
        if eng == "pe" and not dma:
            deps = {k: d for k, d in deps.items() if not (d.eng == "pe" and not d.dma)}
        deps.pop(ins.idx, None)
        ins.deps = list(deps.values())
        for d in ins.deps:
            d.signal = True
        if dma:
            pos = self.ringpos[eng]
            self.ringpos[eng] = (pos + 1) % self.RING
            ins.sem = self.rings[eng][pos]
            prev = self.ringval[eng][pos]
            ins.slotdep = (ins.sem, prev) if prev > 0 else None
            ins.semval = prev + 16
            self.ringval[eng][pos] = ins.semval
            self.pending_dma.append(ins)
        for t in reads:
            key = ("dma", ins.idx) if dma else eng
            t.readers[key] = ins
        for t in writes:
            t.writer = ins
            t.readers = {}
        self.streams[eng].append(ins)
        return ins

    def barrier(self):
        last = [self.streams[e][-1] for e in self.ENGS if self.streams[e] and not self.streams[e][-1].dma]
        last = []
        for e in self.ENGS:
            for ins in reversed(self.streams[e]):
                if not ins.dma and ins.fn is not None:
                    last.append(ins)
                    break
        deps = last + self.pending_dma
        for e in self.ENGS:
            self.add(e, None, extra_deps=deps)
        self.pending_dma = []
        for t in self.toks.values():
            t.writer = None
            t.readers = {}

    def emit(self):
        nc = self.nc
        for e in self.ENGS:
            c = 0
            for ins in self.streams[e]:
                if not ins.dma and ins.signal:
                    c += 1
                    ins.sigval = c
        prog = self

        def run(e, engobj):
            waited = {}
            for ins in prog.streams[e]:
                needs = {}
                for d in ins.deps:
                    if d.dma:
                        s, v = d.sem, d.semval
                    else:
                        s, v = prog.engsem[d.eng], d.sigval
                    k = id(s)
                    if needs.get(k, (None, 0))[1] < v:
                        needs[k] = (s, v)
                if ins.slotdep is not None:
                    s, v = ins.slotdep
                    k = id(s)
                    if needs.get(k, (None, 0))[1] < v:
                        needs[k] = (s, v)
                for k, (s, v) in needs.items():
                    if waited.get(k, 0) < v:
                        engobj.wait_ge(s, v)
                        waited[k] = v
                if ins.fn is None:
                    continue
                bi = ins.fn(engobj)
                if ins.dma:
                    bi.then_inc(ins.sem, 16)
                elif ins.signal:
                    bi.then_inc(prog.engsem[e], 1)

        with nc.Block() as block:
            @block.tensor
            def _(e):
                run("pe", e)

            @block.scalar
            def _(e):
                run("act", e)

            @block.vector
            def _(e):
                run("dve", e)

            @block.gpsimd
            def _(e):
                run("pool", e)

            @block.sync
            def _(e):
                run("sp", e)


def _vec_layout():
    cols = {}
    n = 0

    def put(name, k):
        nonlocal n
        cols[name] = n
        n += k
    for nm in ("b_gate0", "b_gate1", "b_gate2", "pool_scale", "mu_r", "mu_k", "mu_v", "w0", "a0", "k_k", "k_a",
               "r_k", "lnx_g", "lnx_b", "ln1_g", "ln1_b", "ln2_g", "ln2_b", "ln3_g", "ln3_b"):
        put(nm, 8)
    put("mu_xw", 1)
    put("mu_xa", 1)
    put("mu_xg", 2)
    put("subln_g", 1)
    put("lam", 256)
    return cols, n


VCOL, NV = _vec_layout()
CCOL = {}
_n = 0
for _nm, _k in (("ident", 128), ("mstrict", 64), ("mincl", 64), ("invfreq", 1), ("shc", 1), ("shs", 1),
                ("sgn", 1), ("invcnt", 64), ("blk64", 128), ("ln_in_g", 8), ("ln_in_b", 8)):
    CCOL[_nm] = _n
    _n += _k
NCONST = _n


def _host_consts(ln_in_g, ln_in_b):
    c = np.zeros((128, NCONST), np.float32)
    c[:, CCOL["ident"]:CCOL["ident"] + 128] = np.eye(128, dtype=np.float32)
    s = np.arange(128)[:, None] % 64
    t = np.arange(64)[None, :]
    c[:, CCOL["mstrict"]:CCOL["mstrict"] + 64] = (s < t)
    c[:, CCOL["mincl"]:CCOL["mincl"] + 64] = (s <= t)
    i = np.arange(128) % 32
    inv_freq = (1.0 / (10000.0 ** (np.arange(0, 64, 2, dtype=np.float32) / np.float32(64)))).astype(np.float32)
    c[:, CCOL["invfreq"]] = inv_freq[i]
    c[:, CCOL["shc"]] = 1.5 * math.pi
    c[:, CCOL["shs"]] = math.pi
    half = (np.arange(128) // 32) % 2
    c[:, CCOL["sgn"]] = np.where(half == 0, -1.0, 1.0)
    for g in range(4):
        w = 2 ** (g + 1)
        cnt = np.minimum(np.arange(16) + 1, w).astype(np.float32)
        c[:, CCOL["invcnt"] + g * 16: CCOL["invcnt"] + (g + 1) * 16] = (1.0 / cnt)[None, :]
    blk = np.zeros((128, 128), np.float32)
    blk[:64, :64] = 1.0 / 64
    blk[64:, 64:] = 1.0 / 64
    c[:, CCOL["blk64"]:CCOL["blk64"] + 128] = blk
    c[:, CCOL["ln_in_g"]:CCOL["ln_in_g"] + 8] = ln_in_g.reshape(8, 128).T
    c[:, CCOL["ln_in_b"]:CCOL["ln_in_b"] + 8] = ln_in_b.reshape(8, 128).T
    return c


def _host_vecs(inp):
    v = np.zeros((DEPTH, 128, NV), np.float32)

    def T8(a):
        return np.ascontiguousarray(a.reshape(8, 128).T)
    for l in range(DEPTH):
        def put(nm, arr):
            k = arr.shape[1]
            v[l, :, VCOL[nm]:VCOL[nm] + k] = arr
        for j in range(3):
            put("b_gate%d" % j, T8(inp["b_gate"][l, j]))
        put("pool_scale", T8(inp["pool_scale"][l]))
        mu = inp["rwkv_mu"][l]
        put("mu_r", T8(mu[0:1024]))
        put("mu_k", T8(mu[1024:2048]))
        put("mu_v", T8(mu[2048:3072]))
        tmp = np.zeros((128, 1), np.float32)
        tmp[:64, 0] = mu[3072:3136]
        put("mu_xw", tmp)
        tmp = np.zeros((128, 1), np.float32)
        tmp[:64, 0] = mu[3136:3200]
        put("mu_xa", tmp)
        tmp = np.zeros((128, 2), np.float32)
        tmp[:, 0] = mu[3200:3328]
        tmp[:32, 1] = mu[3328:3360]
        put("mu_xg", tmp)
        put("w0", T8(inp["rwkv_w0"][l]))
        put("a0", T8(inp["rwkv_a0"][l]))
        put("k_k", T8(inp["rwkv_k_k"][l]))
        put("k_a", T8(inp["rwkv_k_a"][l]))
        put("r_k", T8(inp["rwkv_r_k"][l].reshape(-1)))
        put("lnx_g", T8(inp["rwkv_lnx_g"][l]))
        put("lnx_b", T8(inp["rwkv_lnx_b"][l]))
        for nm in ("ln1_g", "ln1_b", "ln2_g", "ln2_b", "ln3_g", "ln3_b"):
            put(nm, T8(inp[nm][l]))
        put("subln_g", inp["attn_subln_g"][l].reshape(128, 1))
        lam = np.concatenate([inp["lam_q1"][l], inp["lam_k1"][l], inp["lam_q2"][l], inp["lam_k2"][l]])
        put("lam", np.broadcast_to(lam[None, :], (128, 256)))
    return v


WEIGHTS = (("w_in", [DEPTH, D, P_IN]), ("w_br_attn", [DEPTH, D, D]), ("pool_w", [DEPTH, 4, 256, 256]),
           ("w_br_pool", [DEPTH, D, D]), ("rwkv_w2", [DEPTH, 64, D]), ("rwkv_a2", [DEPTH, 64, D]),
           ("rwkv_g2", [DEPTH, 160, D]), ("w_br_rwkv", [DEPTH, D, D]), ("w_out", [DEPTH, D, D]),
           ("w_xq", [DEPTH, D, D]), ("w_xkv", [DEPTH, D, 2 * D]), ("w_xo", [DEPTH, D, D]),
           ("w_ff1", [DEPTH, D, 4 * D]), ("w_ff2", [DEPTH, 4 * D, D]))


from contextlib import ExitStack


class K:
    def __init__(self, dbg=False, stop=None):
        nc = bass.Bass("TRN2", target_bir_lowering=False)
        self.nc = nc
        self.p = Prog(nc)
        self.stop = stop
        p = self.p
        din = lambda n, s, d=F32: nc.dram_tensor(n, s, d, kind="ExternalInput").ap()
        self.x = din("x", [S, D])
        self.mem = din("mem", [256, D])
        self.pos = din("pos", [128, S], I32)
        self.consts = din("consts", [128, NCONST])
        self.vecs = din("vecs", [DEPTH, 128, NV])
        self.w = {n: din(n, s) for n, s in WEIGHTS}
        self.out = nc.dram_tensor("out", [S, D], F32, kind="ExternalOutput").ap()
        kind = "ExternalOutput" if dbg else "Internal"
        self.h32d = nc.dram_tensor("h32d", [D, S], F32, kind=kind).ap()
        self.otd = nc.dram_tensor("otd", [D, S], BF16, kind=kind).ap()
        self.ptd = nc.dram_tensor("ptd", [D, S], BF16, kind=kind).ap()
        self.rtd = nc.dram_tensor("rtd", [D, S], BF16, kind=kind).ap()
        self.gd = nc.dram_tensor("gd", [3 * D, S], BF16, kind=kind).ap()
        self.mgd = nc.dram_tensor("mgd", [D, S], BF16, kind=kind).ap()
        ps0 = nc.alloc_psum_tensor("ps0", [128, 512], F32)
        self.psB = nc.alloc_psum_tensor("psB", [128, 3072], F32)
        ps7 = nc.alloc_psum_tensor("ps7", [128, 512], F32)
        self.ps = [ps0] + [self.psB[:, i * 512:(i + 1) * 512] for i in range(6)] + [ps7]
        sb = nc.alloc_sbuf_tensor
        self.cst = sb("cst", [128, NCONST], F32)
        self.vec = sb("vec", [128, DEPTH, NV], F32)
        self.hbT = sb("hbT", [128, 8, S], BF16)
        self.ropeC = sb("ropeC", [128, S], F32)
        self.ropeS = sb("ropeS", [128, S], F32)
        self.ident_bf = sb("ident_bf", [128, 128], BF16)
        self.ones_d = sb("ones_d", [128, 128], BF16)
        self.ones_dv = sb("ones_dv", [128, 128], BF16)
        self.ones_k = sb("ones_k", [128, 128], BF16)
        self.blk64 = sb("blk64", [128, 128], BF16)
        self.epsc = sb("epsc", [128, 4], F32)
        self.stack = None

    def cc(self, name, k=1, off=0):
        c = CCOL[name] + off
        return self.cst[:, c:c + k]

    def vc(self, l, name, k=1, off=0):
        c = VCOL[name] + off
        return self.vec[:, l, c:c + k]

    def sb(self, name, shape, dtype):
        self.uid = getattr(self, "uid", 0) + 1
        return self.stack.enter_context(self.nc.sbuf_tensor("%s_u%d" % (name, self.uid), shape, dtype))

    def begin(self):
        self.stack = ExitStack()

    def end(self):
        self.p.barrier()
        self.stack.close()
        self.stack = None

    def A(self, eng, fn, reads=(), writes=(), dma=False):
        return self.p.add(eng, fn, reads, writes, dma)

    def setup(self):
        A = self.A
        cst, vec = self.cst, self.vec
        A("sp", lambda e: e.dma_start(out=cst[:], in_=self.consts), writes=["cst"], dma=True)
        A("sp", lambda e: e.dma_start(out=vec[:], in_=self.vecs.rearrange("l p n -> p l n")), writes=["vec"], dma=True)
        A("dve", lambda e: e.tensor_copy(out=self.ident_bf[:], in_=self.cc("ident", 128)), reads=["cst"], writes=["ident_bf"])
        A("dve", lambda e: e.memset(self.ones_d[:], 1.0 / 1024), writes=["ones_d"])
        A("dve", lambda e: e.memset(self.ones_dv[:], 1.0 / 128), writes=["ones_dv"])
        A("dve", lambda e: e.memset(self.ones_k[:], 1.0), writes=["ones_k"])
        A("dve", lambda e: e.tensor_copy(out=self.blk64[:], in_=self.cc("blk64", 128)), reads=["cst"], writes=["blk64"])
        A("dve", lambda e: e.memset(self.epsc[:, 0:1], LN_EPS), writes=["epsc"])
        A("dve", lambda e: e.memset(self.epsc[:, 1:2], RMS_EPS), writes=["epsc"])
        A("dve", lambda e: e.memset(self.epsc[:, 2:3], GN_EPS), writes=["epsc"])
        A("dve", lambda e: e.memset(self.epsc[:, 3:4], 0.0), writes=["epsc"])
        self.begin()
        posi = self.sb("posi", [128, S], I32)
        ang = self.sb("ang", [128, S], F32)
        tmp = self.sb("rtmp", [128, S], F32)
        A("sp", lambda e: e.dma_start(out=posi[:], in_=self.pos), writes=["posi"], dma=True)
        A("dve", lambda e: e.tensor_copy(out=ang[:], in_=posi[:]), reads=["posi"], writes=["ang"])
        A("dve", lambda e: e.tensor_scalar(out=ang[:], in0=ang[:], scalar1=self.cc("invfreq"), scalar2=None, op0=ALU.mult),
          reads=["ang", "cst"], writes=["ang"])
        two_pi = 2.0 * math.pi
        ki = self.sb("rki", [128, S], I32)
        kf = self.sb("rkf", [128, S], F32)
        for (sh, dst, dn) in ((0.5 * math.pi, self.ropeC, "ropeC"), (0.0, self.ropeS, "ropeS")):
            A("dve", lambda e, sh=sh: e.tensor_scalar(out=tmp[:], in0=ang[:], scalar1=sh, scalar2=None, op0=ALU.add), reads=["ang"], writes=["rtmp"])
            A("dve", lambda e: e.tensor_scalar(out=ki[:], in0=tmp[:], scalar1=1.0 / two_pi, scalar2=None, op0=ALU.mult), reads=["rtmp"], writes=["rki"])
            A("dve", lambda e: e.tensor_copy(out=kf[:], in_=ki[:]), reads=["rki"], writes=["rkf"])
            A("dve", lambda e: e.scalar_tensor_tensor(out=tmp[:], in0=kf[:], scalar=-two_pi, in1=tmp[:], op0=ALU.mult, op1=ALU.add), reads=["rkf", "rtmp"], writes=["rtmp"])
            A("dve", lambda e: e.tensor_scalar(out=kf[:], in0=tmp[:], scalar1=math.pi, scalar2=-two_pi, op0=ALU.is_gt, op1=ALU.mult), reads=["rtmp"], writes=["rkf"])
            A("dve", lambda e: e.tensor_tensor(out=tmp[:], in0=tmp[:], in1=kf[:], op=ALU.add), reads=["rtmp", "rkf"], writes=["rtmp"])
            A("dve", lambda e: e.tensor_scalar(out=kf[:], in0=tmp[:], scalar1=-math.pi, scalar2=two_pi, op0=ALU.is_lt, op1=ALU.mult), reads=["rtmp"], writes=["rkf"])
            A("dve", lambda e: e.tensor_tensor(out=tmp[:], in0=tmp[:], in1=kf[:], op=ALU.add), reads=["rtmp", "rkf"], writes=["rtmp"])
            A("dve", lambda e: e.tensor_scalar(out=tmp[:], in0=tmp[:], scalar1=math.pi, scalar2=-math.pi, op0=ALU.min, op1=ALU.max), reads=["rtmp"], writes=["rtmp"])
            A("act", lambda e, dst=dst: e.activation(out=dst[:], in_=tmp[:], func=AF.Sin), reads=["rtmp"], writes=[dn])
        A("dve", lambda e: e.tensor_scalar(out=self.ropeS[:], in0=self.ropeS[:], scalar1=self.cc("sgn"), scalar2=None, op0=ALU.mult),
          reads=["ropeS", "cst"], writes=["ropeS"])
        self.end()

    def ln_T(self, zT, zname, gcol, bcol, tb, tmps, final=False):
        A = self.A
        zb, mean_sb, rstd_sb = tmps
        ps = self.ps
        A("act", lambda e: e.activation(out=zb[:], in_=zT[:], func=AF.Copy), reads=[zname], writes=["ln_zb"])
        for c in range(8):
            A("pe", lambda e, c=c: e.matmul(ps[6][:], lhsT=self.ones_d[:], rhs=zb[:, c, :], start=(c == 0), stop=(c == 7)),
              reads=["ln_zb", "ones_d"], writes=["ps6"])
        A("act", lambda e: e.activation(out=mean_sb[:], in_=ps[6][:], func=AF.Copy), reads=["ps6"], writes=["ln_mean"])
        A("dve", lambda e: e.tensor_tensor(out=zT[:], in0=zT[:], in1=mean_sb[:, None, :].to_broadcast([128, 8, TB]), op=ALU.subtract),
          reads=[zname, "ln_mean"], writes=[zname])
        A("act", lambda e: e.activation(out=zb[:], in_=zT[:], func=AF.Square), reads=[zname], writes=["ln_zb"])
        for c in range(8):
            A("pe", lambda e, c=c: e.matmul(ps[7][:], lhsT=self.ones_d[:], rhs=zb[:, c, :], start=(c == 0), stop=(c == 7)),
              reads=["ln_zb", "ones_d"], writes=["ps7"])
        A("act", lambda e: e.activation(out=rstd_sb[:], in_=ps[7][:], func=AF.Sqrt, bias=self.epsc[:, 0:1]),
          reads=["ps7", "epsc"], writes=["ln_rstd"])
        A("dve", lambda e: e.reciprocal(out=rstd_sb[:], in_=rstd_sb[:]), reads=["ln_rstd"], writes=["ln_rstd"])
        A("dve", lambda e: e.tensor_tensor(out=zT[:], in0=zT[:], in1=rstd_sb[:, None, :].to_broadcast([128, 8, TB]), op=ALU.mult),
          reads=[zname, "ln_rstd"], writes=[zname])
        for c in range(8):
            A("act", lambda e, c=c: e.activation(out=zT[:, c, :], in_=zT[:, c, :], func=AF.Identity,
                                                  scale=gcol[:, c:c + 1], bias=bcol[:, c:c + 1]),
              reads=[zname, "vec", "cst"], writes=[zname])
        if final:
            ident = self.cc("ident", 128)
            for tt in range(4):
                on_ = self.outN[tt % 2]
                onn = "outN%d" % (tt % 2)
                for cg in range(2):
                    bank = 4 + cg
                    for c4 in range(4):
                        c = cg * 4 + c4
                        A("pe", lambda e, c=c, c4=c4, tt=tt, bank=bank: e.transpose(ps[bank][:, c4 * 128:(c4 + 1) * 128], zT[:, c, tt * 128:(tt + 1) * 128], ident),
                          reads=[zname, "cst"], writes=["ps%d" % bank])
                    if cg == 0:
                        A("act", lambda e, on_=on_, bank=bank: e.activation(out=on_[:, 0:512], in_=ps[bank][:], func=AF.Copy), reads=["ps%d" % bank], writes=[onn])
                    else:
                        A("dve", lambda e, on_=on_, bank=bank: e.tensor_copy(out=on_[:, 512:1024], in_=ps[bank][:]), reads=["ps%d" % bank], writes=[onn])
                A("sp", lambda e, on_=on_, tt=tt: e.dma_start(out=self.out[tb * TB + tt * 128:tb * TB + (tt + 1) * 128, :], in_=on_[:]), reads=[onn], writes=["out%d" % tt], dma=True)
            return
        A("dve", lambda e: e.tensor_copy(out=self.hbT[:, :, tb * TB:(tb + 1) * TB], in_=zT[:]), reads=[zname], writes=["hbT"])
        A("sp", lambda e: e.dma_start(out=self.h32d.rearrange("(c p) t -> p c t", p=128)[:, :, tb * TB:(tb + 1) * TB], in_=zT[:]),
          reads=[zname], writes=["h32d"], dma=True)

    def ln_tmps(self):
        return (self.sb("ln_zb", [128, 8, TB], BF16), self.sb("ln_mean", [128, TB], F32), self.sb("ln_rstd", [128, TB], F32))

    def phase_ln_in(self):
        A = self.A
        ps = self.ps
        self.begin()
        tmps = self.ln_tmps()
        xt = [self.sb("xt%d" % i, [128, 4, D], F32) for i in range(2)]
        zT = [self.sb("zT%d" % i, [128, 8, TB], F32) for i in range(2)]
        ident = self.cc("ident", 128)
        for tb in range(NB):
            xb = xt[tb % 2]
            xn = "xt%d" % (tb % 2)
            z = zT[tb % 2]
            zn = "zT%d" % (tb % 2)
            A("sp", lambda e, xb=xb, tb=tb: e.dma_start(out=xb[:], in_=self.x[tb * TB:(tb + 1) * TB, :].rearrange("(t p) d -> p t d", p=128)),
              writes=[xn], dma=True)
            for c in range(8):
                bank = c % 2
                for tt in range(4):
                    A("pe", lambda e, xb=xb, c=c, tt=tt, bank=bank: e.transpose(ps[bank][:, tt * 128:(tt + 1) * 128], xb[:, tt, c * 128:(c + 1) * 128], ident),
                      reads=[xn, "cst"], writes=["ps%d" % bank])
                if c % 2 == 0:
                    A("act", lambda e, z=z, c=c, bank=bank: e.activation(out=z[:, c, :], in_=ps[bank][:], func=AF.Copy), reads=["ps%d" % bank], writes=[zn])
                else:
                    A("dve", lambda e, z=z, c=c, bank=bank: e.tensor_copy(out=z[:, c, :], in_=ps[bank][:]), reads=["ps%d" % bank], writes=[zn])
            self.ln_T(z, zn, self.cc("ln_in_g", 8), self.cc("ln_in_b", 8), tb, tmps)
        self.end()

    def load_w(self, dst, dname, src_rows_cols):
        self.A("pool", lambda e: e.dma_start(out=dst, in_=src_rows_cols.rearrange("(c p) n -> p c n", p=128)), writes=[dname], dma=True)

    def phase_attn(self, l):
        A = self.A
        ps = self.ps
        w_in = self.w["w_in"][l]
        self.begin()
        NWB = 2
        wq = [self.sb("wq%d" % i, [128, 8, 128], BF16) for i in range(NWB)]
        wqs = [self.sb("wqs%d" % i, [128, 8, 128], BF16) for i in range(NWB)]
        wk = [self.sb("wk%d" % i, [128, 8, 128], BF16) for i in range(NWB)]
        wks = [self.sb("wks%d" % i, [128, 8, 128], BF16) for i in range(NWB)]
        wv = [self.sb("wv%d" % i, [128, 8, 128], BF16) for i in range(NWB)]
        qT = [self.sb("qT%d" % i, [128, S], BF16) for i in range(2)]
        kT = [self.sb("kT%d" % i, [128, S], BF16) for i in range(2)]
        vS = [self.sb("vS%d" % i, [128, 16, 128], BF16) for i in range(2)]
        t1 = self.sb("rp_t1", [128, TB], F32)
        t2 = self.sb("rp_t2", [128, TB], F32)
        NE = 6
        eT = [self.sb("eT%d" % i, [128, TB], BF16) for i in range(NE)]
        r1 = self.sb("at_r1", [128, TB], F32)
        o1n = self.sb("at_o1n", [128, TB], F32)
        o2n = self.sb("at_o2n", [128, TB], F32)
        osq = self.sb("at_osq", [128, TB], BF16)
        rs = self.sb("at_rs", [128, TB], F32)
        ob = [self.sb("at_ob%d" % i, [128, TB], BF16) for i in range(2)]
        lamt = self.sb("lamt", [128, 64], F32)
        lamv = self.sb("lamv", [128, 4], F32)
        gsc = self.sb("gsc", [128, 1], F32)
        lam_init = 0.8 - 0.6 * math.exp(-0.3 * l)
        A("dve", lambda e: e.tensor_tensor(out=lamt[:], in0=self.vc(l, "lam", 64, 0), in1=self.vc(l, "lam", 64, 64), op=ALU.mult), reads=["vec"], writes=["lamt"])
        A("dve", lambda e: e.reduce_sum(out=lamv[:, 0:1], in_=lamt[:], axis=AX.X), reads=["lamt"], writes=["lamv"])
        A("dve", lambda e: e.tensor_tensor(out=lamt[:], in0=self.vc(l, "lam", 64, 128), in1=self.vc(l, "lam", 64, 192), op=ALU.mult), reads=["vec", "lamv"], writes=["lamt"])
        A("dve", lambda e: e.reduce_sum(out=lamv[:, 1:2], in_=lamt[:], axis=AX.X), reads=["lamt"], writes=["lamv"])
        A("act", lambda e: e.activation(out=lamv[:, 0:2], in_=lamv[:, 0:2], func=AF.Exp), reads=["lamv"], writes=["lamv"])
        A("dve", lambda e: e.tensor_tensor(out=lamv[:, 2:3], in0=lamv[:, 1:2], in1=lamv[:, 0:1], op=ALU.subtract), reads=["lamv"], writes=["lamv"])
        A("dve", lambda e: e.tensor_scalar(out=lamv[:, 3:4], in0=lamv[:, 2:3], scalar1=-lam_init, scalar2=None, op0=ALU.add), reads=["lamv"], writes=["lamv"])
        A("dve", lambda e: e.tensor_scalar(out=gsc[:], in0=self.vc(l, "subln_g"), scalar1=(1.0 - lam_init), scalar2=None, op0=ALU.mult), reads=["vec"], writes=["gsc"])
        neglam = lamv[:, 3:4]

        def swapped_load(dst, dname, base):
            for mm_ in range(2):
                for half in range(2):
                    so = base + mm_ * 64 + (1 - half) * 32
                    do = mm_ * 64 + half * 32
                    src = w_in[:, so:so + 32].rearrange("(c p) i -> p c i", p=128)
                    d = dst[:, :, do:do + 32]
                    A("pool", lambda e, d=d, src=src: e.dma_start(out=d, in_=src), writes=[dname], dma=True)

        def load_head(h):
            wb = h % NWB
            self.load_w(wq[wb][:], "wq%d" % wb, w_in[:, h * 128:(h + 1) * 128])
            swapped_load(wqs[wb], "wqs%d" % wb, h * 128)
            self.load_w(wk[wb][:], "wk%d" % wb, w_in[:, 1024 + h * 128:1024 + (h + 1) * 128])
            swapped_load(wks[wb], "wks%d" % wb, 1024 + h * 128)
            self.load_w(wv[wb][:], "wv%d" % wb, w_in[:, 2048 + h * 128:2048 + (h + 1) * 128])

        load_head(0)
        load_head(1)
        for h in range(8):
            wb = h % NWB
            hb = h % 2
            for (wa, wan, ws, wsn, dst, dn) in ((wq[wb], "wq%d" % wb, wqs[wb], "wqs%d" % wb, qT[hb], "qT%d" % hb),
                                                (wk[wb], "wk%d" % wb, wks[wb], "wks%d" % wb, kT[hb], "kT%d" % hb)):
                for tb in range(NB):
                    tsl = slice(tb * TB, (tb + 1) * TB)
                    for c in range(8):
                        A("pe", lambda e, wa=wa, c=c, tsl=tsl: e.matmul(ps[0][:], lhsT=wa[:, c, :], rhs=self.hbT[:, c, tsl], start=(c == 0), stop=(c == 7)),
                          reads=[wan, "hbT"], writes=["ps0"])
                    for c in range(8):
                        A("pe", lambda e, ws=ws, c=c, tsl=tsl: e.matmul(ps[1][:], lhsT=ws[:, c, :], rhs=self.hbT[:, c, tsl], start=(c == 0), stop=(c == 7)),
                          reads=[wsn, "hbT"], writes=["ps1"])
                    A("dve", lambda e, tsl=tsl: e.tensor_tensor(out=t1[:], in0=ps[0][:], in1=self.ropeC[:, tsl], op=ALU.mult), reads=["ps0", "ropeC"], writes=["rp_t1"])
                    A("dve", lambda e, tsl=tsl: e.tensor_tensor(out=t2[:], in0=ps[1][:], in1=self.ropeS[:, tsl], op=ALU.mult), reads=["ps1", "ropeS"], writes=["rp_t2"])
                    A("pool", lambda e, dst=dst, tsl=tsl: e.tensor_tensor(out=dst[:, tsl], in0=t1[:], in1=t2[:], op=ALU.add), reads=["rp_t1", "rp_t2"], writes=[dn])
            for g4 in range(4):
                bank = g4 % 2
                for t4 in range(4):
                    tt = g4 * 4 + t4
                    for c in range(8):
                        A("pe", lambda e, c=c, tt=tt, t4=t4, bank=bank, wb=wb: e.matmul(ps[bank][:, t4 * 128:(t4 + 1) * 128], lhsT=self.hbT[:, c, tt * 128:(tt + 1) * 128],
                                                                                 rhs=wv[wb][:, c, :], start=(c == 0), stop=(c == 7)),
                          reads=["wv%d" % wb, "hbT"], writes=["ps%d" % bank])
                A("act", lambda e, g4=g4, bank=bank, hb=hb: e.activation(out=vS[hb][:, g4 * 4:(g4 + 1) * 4, :], in_=ps[bank][:].rearrange("p (a b) -> p a b", a=4), func=AF.Copy),
                  reads=["ps%d" % bank], writes=["vS%d" % hb])
            if h + 2 < 8:
                load_head(h + 2)
            LOOK = 3
            sbanks = (2, 3, 0, 1)
            ecnt = 0
            scnt = 0
            for Q in range(NB):
                qsl0 = Q * TB
                its = []
                for m in range(2):
                    nk = 4 * Q + 4
                    for kt in range(nk):
                        its.append((m, kt, nk))
                pend = []
                for i in range(len(its) + LOOK):
                    if i < len(its):
                        m, kt, nk = its[i]
                        msl = slice(64 * m, 64 * m + 64)
                        qoff = max(0, kt - 4 * Q) * 128
                        diag = kt >= 4 * Q
                        sb_ = sbanks[scnt % 4]
                        scnt += 1
                        et = eT[ecnt % NE]
                        en = "eT%d" % (ecnt % NE)
                        ecnt += 1
                        A("pe", lambda e, sb_=sb_, msl=msl, kt=kt, qoff=qoff, qsl0=qsl0, hb=hb: e.matmul(
                            ps[sb_][:, qoff:TB], lhsT=kT[hb][msl, kt * 128:(kt + 1) * 128], rhs=qT[hb][msl, qsl0 + qoff:qsl0 + TB], start=True, stop=True),
                          reads=["kT%d" % hb, "qT%d" % hb], writes=["ps%d" % sb_])
                        A("act", lambda e, sb_=sb_, et=et, qoff=qoff: e.activation(out=et[:, qoff:TB], in_=ps[sb_][:, qoff:TB], func=AF.Exp, scale=0.125),
                          reads=["ps%d" % sb_], writes=[en])
                        if diag:
                            A("pool", lambda e, et=et, qoff=qoff: e.memset(et[64:128, qoff:qoff + 64], 0.0), reads=[en], writes=[en])
                        pend.append((m, kt, nk, qoff, et, en))
                    if i >= LOOK:
                        m, kt, nk, qoff, et, en = pend[i - LOOK]
                        accO = ps[4 + 2 * m]
                        accD = ps[5 + 2 * m]
                        aon = "ps%d" % (4 + 2 * m)
                        adn = "ps%d" % (5 + 2 * m)
                        A("pe", lambda e, accO=accO, et=et, kt=kt, qoff=qoff, nk=nk, hb=hb: e.matmul(
                            accO[:, qoff:TB], lhsT=vS[hb][:, kt, :], rhs=et[:, qoff:TB], start=(kt == 0), stop=(kt == nk - 1)),
                          reads=["vS%d" % hb, en], writes=[aon])
                        A("pe", lambda e, accD=accD, et=et, kt=kt, qoff=qoff, nk=nk: e.matmul(
                            accD[:, qoff:TB], lhsT=self.ones_k[:], rhs=et[:, qoff:TB], start=(kt == 0), stop=(kt == nk - 1)),
                          reads=["ones_k", en], writes=[adn])
                A("dve", lambda e: e.reciprocal(out=r1[:], in_=ps[5][:]), reads=["ps5"], writes=["at_r1"])
                A("dve", lambda e: e.tensor_tensor(out=o1n[:], in0=ps[4][:], in1=r1[:], op=ALU.mult), reads=["ps4", "at_r1"], writes=["at_o1n"])
                A("dve", lambda e: e.reciprocal(out=r1[:], in_=ps[7][:]), reads=["ps7", "at_o1n"], writes=["at_r1"])
                A("dve", lambda e: e.tensor_tensor(out=o2n[:], in0=ps[6][:], in1=r1[:], op=ALU.mult), reads=["ps6", "at_r1"], writes=["at_o2n"])
                A("dve", lambda e: e.scalar_tensor_tensor(out=o1n[:], in0=o2n[:], scalar=neglam, in1=o1n[:], op0=ALU.mult, op1=ALU.add),
                  reads=["at_o2n", "at_o1n", "lamv"], writes=["at_o1n"])
                A("act", lambda e: e.activation(out=osq[:], in_=o1n[:], func=AF.Square), reads=["at_o1n"], writes=["at_osq"])
                A("pe", lambda e: e.matmul(ps[2][:], lhsT=self.ones_dv[:], rhs=osq[:], start=True, stop=True), reads=["ones_dv", "at_osq"], writes=["ps2"])
                A("act", lambda e: e.activation(out=rs[:], in_=ps[2][:], func=AF.Sqrt, bias=self.epsc[:, 1:2]), reads=["ps2", "epsc"], writes=["at_rs"])
                A("dve", lambda e: e.reciprocal(out=rs[:], in_=rs[:]), reads=["at_rs"], writes=["at_rs"])
                A("dve", lambda e: e.tensor_tensor(out=o1n[:], in0=o1n[:], in1=rs[:], op=ALU.mult), reads=["at_o1n", "at_rs"], writes=["at_o1n"])
                obq = ob[Q % 2]
                obn = "at_ob%d" % (Q % 2)
                A("act", lambda e, obq=obq: e.activation(out=obq[:], in_=o1n[:], func=AF.Identity, scale=gsc[:, 0:1], bias=self.epsc[:, 3:4]),
                  reads=["at_o1n", "gsc", "epsc"], writes=[obn])
                A("sp", lambda e, obq=obq, h=h, Q=Q: e.dma_start(out=self.otd[h * 128:(h + 1) * 128, Q * TB:(Q + 1) * TB], in_=obq[:]),
                  reads=[obn], writes=["otd"], dma=True)
        self.end()

    def phase_gates(self, l):
        A = self.A
        ps = self.ps
        w_in = self.w["w_in"][l]
        self.begin()
        wg = [self.sb("wg%d" % i, [128, 8, 128], BF16) for i in range(3)]
        gb = [self.sb("gb%d" % i, [128, TB], BF16) for i in range(3)]
        idx = 0
        k = 0
        for j in range(3):
            for m in range(8):
                wb = idx % 3
                col = 7456 + j * 1024 + m * 128
                self.load_w(wg[wb][:], "wg%d" % wb, w_in[:, col:col + 128])
                for tb in range(NB):
                    bank = k % 2
                    g = gb[k % 3]
                    gn = "gb%d" % (k % 3)
                    k += 1
                    for c in range(8):
                        A("pe", lambda e, wb=wb, c=c, tb=tb, bank=bank: e.matmul(ps[bank][:], lhsT=wg[wb][:, c, :], rhs=self.hbT[:, c, tb * TB:(tb + 1) * TB],
                                                                               start=(c == 0), stop=(c == 7)), reads=["wg%d" % wb, "hbT"], writes=["ps%d" % bank])
                    A("act", lambda e, g=g, bank=bank, j=j, m=m: e.activation(out=g[:], in_=ps[bank][:], func=AF.Sigmoid, bias=self.vc(l, "b_gate%d" % j, 1, m)),
                      reads=["ps%d" % bank, "vec"], writes=[gn])
                    A("sp", lambda e, g=g, j=j, m=m, tb=tb: e.dma_start(out=self.gd[j * 1024 + m * 128:j * 1024 + (m + 1) * 128, tb * TB:(tb + 1) * TB], in_=g[:]),
                      reads=[gn], writes=["gd%d" % (k % 4)], dma=True)
                idx += 1
        self.end()

    def phase_pool(self, l):
        A = self.A
        ps = self.ps
        w_in = self.w["w_in"][l]
        self.begin()
        PW = 16
        wp = [self.sb("wp%d" % i, [128, 8, 128], BF16) for i in range(2)]
        pw = [self.sb("pw%d" % i, [128, 2, 256], BF16) for i in range(2)]
        Pb = self.sb("pl_P", [128, PW + S], F32)
        Xb = self.sb("pl_X", [128, PW + S], F32)
        Yb = self.sb("pl_Y", [128, PW + S], F32)
        t16 = self.sb("pl_t16", [128, 16], F32)
        dT = [self.sb("pl_d%d" % i, [128, 2, S], BF16) for i in range(2)]
        yb = [self.sb("pl_y%d" % i, [128, TB], BF16) for i in range(2)]
        for (b_, n_) in ((Pb, "pl_P"), (Xb, "pl_X"), (Yb, "pl_Y")):
            A("pool", lambda e, b_=b_: e.memset(b_[:, 0:PW], 0.0), writes=[n_])
        k = 0
        for g in range(4):
            gp = g % 2
            A("pool", lambda e, g=g, gp=gp: e.dma_start(out=pw[gp][:], in_=self.w["pool_w"][l, g].rearrange("(kc p) e -> p kc e", p=128)), writes=["pw%d" % gp], dma=True)
            win = 2 ** (g + 1)
            for cc in range(2):
                c = 2 * g + cc
                wb = c % 2
                self.load_w(wp[wb][:], "wp%d" % wb, w_in[:, 3072 + c * 128:3072 + (c + 1) * 128])
                for tb in range(NB):
                    bank = tb % 2
                    for kc in range(8):
                        A("pe", lambda e, wb=wb, kc=kc, tb=tb, bank=bank: e.matmul(ps[bank][:], lhsT=wp[wb][:, kc, :], rhs=self.hbT[:, kc, tb * TB:(tb + 1) * TB],
                                                                                 start=(kc == 0), stop=(kc == 7)), reads=["wp%d" % wb, "hbT"], writes=["ps%d" % bank])
                    A("act", lambda e, tb=tb, bank=bank: e.activation(out=Pb[:, PW + tb * TB:PW + (tb + 1) * TB], in_=ps[bank][:], func=AF.Copy),
                      reads=["ps%d" % bank], writes=["pl_P"])
                src, sn = Pb, "pl_P"
                sh = 1
                bufs = [(Xb, "pl_X"), (Yb, "pl_Y")]
                bi = 0
                while sh < win:
                    dst, dn = bufs[bi % 2]
                    bi += 1
                    A("dve", lambda e, src=src, dst=dst, sh=sh: e.tensor_tensor(out=dst[:, PW:PW + S], in0=src[:, PW:PW + S], in1=src[:, PW - sh:PW - sh + S], op=ALU.add),
                      reads=[sn], writes=[dn])
                    src, sn = dst, dn
                    sh *= 2
                A("dve", lambda e, src=src, gp=gp, cc=cc, win=win: e.scalar_tensor_tensor(out=dT[gp][:, cc, :], in0=src[:, PW:PW + S], scalar=1.0 / win, in1=Pb[:, PW:PW + S],
                                                                                   op0=ALU.mult, op1=ALU.subtract), reads=[sn, "pl_P"], writes=["pl_d%d" % gp])
                A("dve", lambda e, src=src, g=g: e.tensor_tensor(out=t16[:], in0=src[:, PW:PW + 16], in1=self.cc("invcnt", 16, g * 16), op=ALU.mult), reads=[sn, "cst"], writes=["pl_t16"])
                A("dve", lambda e, gp=gp, cc=cc: e.tensor_tensor(out=dT[gp][:, cc, 0:16], in0=t16[:], in1=Pb[:, PW:PW + 16], op=ALU.subtract), reads=["pl_t16", "pl_P"], writes=["pl_d%d" % gp])
            for ec in range(2):
                for tb in range(NB):
                    bank = 2 + (k % 2)
                    y = yb[k % 2]
                    yn = "pl_y%d" % (k % 2)
                    k += 1
                    for kc in range(2):
                        A("pe", lambda e, gp=gp, kc=kc, ec=ec, tb=tb, bank=bank: e.matmul(ps[bank][:], lhsT=pw[gp][:, kc, ec * 128:(ec + 1) * 128], rhs=dT[gp][:, kc, tb * TB:(tb + 1) * TB],
                                                                                      start=(kc == 0), stop=(kc == 1)), reads=["pw%d" % gp, "pl_d%d" % gp], writes=["ps%d" % bank])
                    A("act", lambda e, y=y, bank=bank, g=g, ec=ec: e.activation(out=y[:], in_=ps[bank][:], func=AF.Identity, scale=self.vc(l, "pool_scale", 1, 2 * g + ec), bias=self.epsc[:, 3:4]),
                      reads=["ps%d" % bank, "vec", "epsc"], writes=[yn])
                    A("sp", lambda e, y=y, g=g, ec=ec, tb=tb: e.dma_start(out=self.ptd[(2 * g + ec) * 128:(2 * g + ec + 1) * 128, tb * TB:(tb + 1) * TB], in_=y[:]),
                      reads=[yn], writes=["ptd%d" % (k % 4)], dma=True)
        self.end()

    def phase_rwkv(self, l):
        A = self.A
        ps = self.ps
        w_in = self.w["w_in"][l]
        self.begin()
        lxw = self.sb("lxw", [128, S], BF16)
        lxa = self.sb("lxa", [128, S], BF16)
        lxg = self.sb("lxg", [128, 2, S], BF16)
        w2s = self.sb("w2s", [128, D], BF16)
        a2s = self.sb("a2s", [128, D], BF16)
        g2s = self.sb("g2s", [128, 2, D], BF16)
        A("pool", lambda e: e.dma_start(out=w2s[0:64, :], in_=self.w["rwkv_w2"][l]), writes=["w2s"], dma=True)
        A("pool", lambda e: e.dma_start(out=a2s[0:64, :], in_=self.w["rwkv_a2"][l]), writes=["a2s"], dma=True)
        A("pool", lambda e: e.dma_start(out=g2s[:, 0, :], in_=self.w["rwkv_g2"][l][0:128, :]), writes=["g2s"], dma=True)
        A("pool", lambda e: e.dma_start(out=g2s[0:32, 1, :], in_=self.w["rwkv_g2"][l][128:160, :]), writes=["g2s"], dma=True)
        outer = self.stack
        self.stack = ExitStack()
        wl = self.sb("wl", [128, 8, 288], BF16)
        ushl = self.sb("ushl", [128, 1 + S], F32)
        tsl_ = self.sb("tsl", [128, S], F32)
        self.load_w(wl[:], "wl", w_in[:, 7168:7456])
        A("dve", lambda e: e.memset(ushl[:, 0:1], 0.0), writes=["ushl"])
        specs = ((0, 64, self.vc(l, "mu_xw"), AF.Tanh, lxw, "lxw", None), (64, 64, self.vc(l, "mu_xa"), AF.Copy, lxa, "lxa", None),
                 (128, 128, self.vc(l, "mu_xg", 1, 0), AF.Sigmoid, lxg, "lxg", 0), (256, 32, self.vc(l, "mu_xg", 1, 1), AF.Sigmoid, lxg, "lxg", 1))
        for (c0, nco, mu, fn, dst, dn, pl) in specs:
            for tb in range(NB):
                bank = tb % 2
                for c in range(8):
                    A("pe", lambda e, c=c, c0=c0, nco=nco, tb=tb, bank=bank: e.matmul(ps[bank][0:nco, :], lhsT=wl[:, c, c0:c0 + nco], rhs=self.hbT[:, c, tb * TB:(tb + 1) * TB],
                                                                                  start=(c == 0), stop=(c == 7)), reads=["wl", "hbT"], writes=["ps%d" % bank])
                A("act", lambda e, nco=nco, tb=tb, bank=bank: e.activation(out=ushl[0:nco, 1 + tb * TB:1 + (tb + 1) * TB], in_=ps[bank][0:nco, :], func=AF.Copy), reads=["ps%d" % bank], writes=["ushl"])
            A("dve", lambda e, nco=nco: e.tensor_tensor(out=tsl_[0:nco, :], in0=ushl[0:nco, 0:S], in1=ushl[0:nco, 1:S + 1], op=ALU.subtract), reads=["ushl"], writes=["tsl"])
            A("dve", lambda e, nco=nco, mu=mu: e.scalar_tensor_tensor(out=tsl_[0:nco, :], in0=tsl_[0:nco, :], scalar=mu[0:nco, :], in1=ushl[0:nco, 1:S + 1], op0=ALU.mult, op1=ALU.add),
              reads=["tsl", "ushl", "vec"], writes=["tsl"])
            d_ = dst[0:nco, :] if pl is None else dst[0:nco, pl, :]
            A("act", lambda e, d_=d_, nco=nco, fn=fn: e.activation(out=d_, in_=tsl_[0:nco, :], func=fn), reads=["tsl"], writes=[dn])
        self.p.barrier()
        self.stack.close()
        self.stack = outer
        import os as _os
        cut = _os.environ.get('RWKV_CUT', '')
        if cut == 'lora':
            self.end()
            return
        f32t = lambda n: self.sb(n, [128, TB], F32)
        bft = lambda n: self.sb(n, [128, TB], BF16)
        wr = [self.sb("rw_r%d" % i, [128, 8, 128], BF16) for i in range(2)]
        wk = [self.sb("rw_k%d" % i, [128, 8, 128], BF16) for i in range(2)]
        wv = [self.sb("rw_v%d" % i, [128, 8, 128], BF16) for i in range(2)]
        ush = {x: self.sb("ush_" + x, [128, 1 + TB], F32) for x in "rkv"}
        rmask = f32t("rmask")
        maskL = self.sb("maskL", [128, 64], F32)
        A("dve", lambda e: e.memset(rmask[:], 1.0), writes=["rmask"])
        A("dve", lambda e: e.memset(rmask[:].rearrange("p (c t) -> p c t", t=64)[:, :, 0:1], 0.0), writes=["rmask"])
        A("dve", lambda e: e.tensor_scalar(out=maskL[:], in0=self.cc("mincl", 64), scalar1=-1.0, scalar2=1.0, op0=ALU.mult, op1=ALU.add), reads=["cst"], writes=["maskL"])
        mstrict = self.cc("mstrict", 64)
        mincl = self.cc("mincl", 64)
        rT, kTt, vT, lwT, aT, kk, kkn, kp, Lc, eneg, eprev, tmpA = [f32t("rk_" + n) for n in ("r", "k", "v", "lw", "a", "kk", "kkn", "kp", "L", "eneg", "eprev", "tmpA")]
        kk2 = bft("rk_kk2")
        rkb = bft("rk_rkb")
        ybf = bft("rk_ybf")
        AR = [self.sb("rk_AR%d" % i, [128, 2, TB], BF16) for i in range(2)]
        btb = [bft("rk_bt%d" % i) for i in range(2)]
        ktb = [bft("rk_kt%d" % i) for i in range(2)]
        vbf = [bft("rk_vb%d" % i) for i in range(2)]
        epos = [f32t("rk_ep%d" % i) for i in range(2)]
        yT = [f32t("rk_y%d" % i) for i in range(2)]
        bonus = [f32t("rk_bo%d" % i) for i in range(2)]
        gT = [f32t("rk_g%d" % i) for i in range(2)]
        yout = [bft("rk_yo%d" % i) for i in range(2)]
        G = 2
        gt = lambda n, sh, dt=BF16: [self.sb("%s%d" % (n, i), [128] + sh, dt) for i in range(2)]
        VK = gt("g_VK", [G, 2, 2, 64])
        AX = gt("g_AX", [G, 2, 2, 64])
        BR = gt("g_BR", [G, 2, 2, 64])
        AK = gt("g_AK", [G, 2, 64])
        RK = gt("g_RK", [G, 2, 64])
        STa = gt("g_STa", [G, 2, 2, 64])
        STb = gt("g_STb", [G, 2, 2, 64])
        PTa = gt("g_PTa", [G, 2, 64])
        PTb = gt("g_PTb", [G, 2, 64])
        AU = gt("g_AU", [G, 2, 2, 64])
        QmTbd = gt("g_Qm", [G, 128])
        RhT = gt("g_Rh", [G, 64])
        Y0T = gt("g_Y0", [G, 64], F32)
        Gbd = gt("g_Gb", [G, 128], F32)
        Gam = gt("g_Gam", [G, 64], F32)
        Rtmp = gt("g_Rtmp", [G, 64], F32)
        Gtmp = gt("g_Gtmp", [G, 64], F32)
        mkS = self.sb("mkS", [128, 2 * G, 64], BF16)
        mkI = self.sb("mkI", [128, 2 * G, 64], BF16)
        mkL = self.sb("mkL", [128, 2 * G, 64], BF16)
        for i in range(2):
            A("dve", lambda e, i=i: e.memset(QmTbd[i][:], 0.0), writes=["g_Qm%d" % i])
            A("dve", lambda e, i=i: e.memset(Gbd[i][:], 0.0), writes=["g_Gb%d" % i])
        A("dve", lambda e: e.tensor_tensor(out=mkS[0:64, :, :], in0=mstrict[0:64, None, :].to_broadcast([64, 2 * G, 64]), in1=mstrict[0:64, None, :].to_broadcast([64, 2 * G, 64]), op=ALU.mult), reads=["cst"], writes=["mkS"])
        A("dve", lambda e: e.tensor_tensor(out=mkI[0:64, :, :], in0=mincl[0:64, None, :].to_broadcast([64, 2 * G, 64]), in1=mincl[0:64, None, :].to_broadcast([64, 2 * G, 64]), op=ALU.mult), reads=["cst"], writes=["mkI"])
        A("dve", lambda e: e.tensor_tensor(out=mkL[0:64, :, :], in0=maskL[0:64, None, :].to_broadcast([64, 2 * G, 64]), in1=maskL[0:64, None, :].to_broadcast([64, 2 * G, 64]), op=ALU.mult), reads=["maskL"], writes=["mkL"])
        Hbd32 = self.sb("c_H32", [128, 128], F32)
        Hbdb = self.sb("c_Hbb", [128, 128], BF16)
        tmpH = self.sb("c_tH", [128, 128], F32)
        psB = self.psB
        bk = [0]

        def pbank():
            return 0

        def load_pair(c):
            wb = c % 2
            self.load_w(wr[wb][:], "rw_r%d" % wb, w_in[:, 4096 + c * 128:4096 + (c + 1) * 128])
            self.load_w(wk[wb][:], "rw_k%d" % wb, w_in[:, 5120 + c * 128:5120 + (c + 1) * 128])
            self.load_w(wv[wb][:], "rw_v%d" % wb, w_in[:, 6144 + c * 128:6144 + (c + 1) * 128])

        def prep_block(c, tb, A):
            wb = c % 2
            if tb == 0:
                for x in "rkv":
                    A("dve", lambda e, x=x: e.memset(ush[x][:, 0:1], 0.0), writes=["ush_" + x])
            pb = (c * NB + tb) % 2
            tsl = slice(tb * TB, (tb + 1) * TB)
            P = lambda n: "%s%d" % (n, pb)
            for (x, wt, wn, mun, dst, dn) in (("r", wr[wb], "rw_r%d" % wb, "mu_r", rT, "rk_r"), ("k", wk[wb], "rw_k%d" % wb, "mu_k", kTt, "rk_k"), ("v", wv[wb], "rw_v%d" % wb, "mu_v", vT, "rk_v")):
                bank = pbank()
                u = ush[x]
                un = "ush_" + x
                for kc in range(8):
                    A("pe", lambda e, wt=wt, kc=kc, bank=bank, tsl=tsl: e.matmul(ps[bank][:], lhsT=wt[:, kc, :], rhs=self.hbT[:, kc, tsl], start=(kc == 0), stop=(kc == 7)),
                      reads=[wn, "hbT"], writes=["ps%d" % bank])
                A("act", lambda e, u=u, bank=bank: e.activation(out=u[:, 1:1 + TB], in_=ps[bank][:], func=AF.Copy), reads=["ps%d" % bank], writes=[un])
                A("dve", lambda e, u=u: e.tensor_tensor(out=tmpA[:], in0=u[:, 0:TB], in1=u[:, 1:1 + TB], op=ALU.subtract), reads=[un], writes=["rk_tmpA"])
                A("dve", lambda e, u=u, dst=dst, mun=mun, c=c: e.scalar_tensor_tensor(out=dst[:], in0=tmpA[:], scalar=self.vc(l, mun, 1, c), in1=u[:, 1:1 + TB], op0=ALU.mult, op1=ALU.add),
                  reads=["rk_tmpA", un, "vec"], writes=[dn])
                A("act", lambda e, u=u: e.activation(out=u[:, 0:1], in_=u[:, TB:TB + 1], func=AF.Copy), reads=[un, dn], writes=[un])
            A("act", lambda e, pb=pb: e.activation(out=vbf[pb][:], in_=vT[:], func=AF.Copy), reads=["rk_v"], writes=[P("rk_vb")])
            bank = pbank()
            A("pe", lambda e, bank=bank, c=c, tsl=tsl: e.matmul(ps[bank][:], lhsT=w2s[0:64, c * 128:(c + 1) * 128], rhs=lxw[0:64, tsl], start=True, stop=True), reads=["w2s", "lxw"], writes=["ps%d" % bank])
            A("act", lambda e, bank=bank, c=c: e.activation(out=lwT[:], in_=ps[bank][:], func=AF.Sigmoid, bias=self.vc(l, "w0", 1, c)), reads=["ps%d" % bank, "vec"], writes=["rk_lw"])
            A("dve", lambda e: e.tensor_scalar(out=lwT[:], in0=lwT[:], scalar1=-math.exp(-0.5), scalar2=None, op0=ALU.mult), reads=["rk_lw"], writes=["rk_lw"])
            bank = pbank()
            A("pe", lambda e, bank=bank, c=c, tsl=tsl: e.matmul(ps[bank][:], lhsT=a2s[0:64, c * 128:(c + 1) * 128], rhs=lxa[0:64, tsl], start=True, stop=True), reads=["a2s", "lxa"], writes=["ps%d" % bank])
            A("act", lambda e, bank=bank, c=c: e.activation(out=aT[:], in_=ps[bank][:], func=AF.Sigmoid, bias=self.vc(l, "a0", 1, c)), reads=["ps%d" % bank, "vec"], writes=["rk_a"])
            bank = pbank()
            A("pe", lambda e, bank=bank, c=c, tsl=tsl: e.matmul(ps[bank][:], lhsT=g2s[:, 0, c * 128:(c + 1) * 128], rhs=lxg[:, 0, tsl], start=True, stop=False), reads=["g2s", "lxg"], writes=["ps%d" % bank])
            A("pe", lambda e, bank=bank, c=c, tsl=tsl: e.matmul(ps[bank][:], lhsT=g2s[0:32, 1, c * 128:(c + 1) * 128], rhs=lxg[0:32, 1, tsl], start=False, stop=True), reads=["g2s", "lxg"], writes=["ps%d" % bank])
            A("act", lambda e, bank=bank, pb=pb: e.activation(out=gT[pb][:], in_=ps[bank][:], func=AF.Copy), reads=["ps%d" % bank], writes=[P("rk_g")])
            A("dve", lambda e, c=c: e.tensor_scalar(out=kk[:], in0=kTt[:], scalar1=self.vc(l, "k_k", 1, c), scalar2=None, op0=ALU.mult), reads=["rk_k", "vec"], writes=["rk_kk"])
            A("act", lambda e: e.activation(out=kk2[:], in_=kk[:], func=AF.Square), reads=["rk_kk"], writes=["rk_kk2"])
            bank = pbank()
            A("pe", lambda e, bank=bank: e.matmul(ps[bank][:], lhsT=self.blk64[:], rhs=kk2[:], start=True, stop=True), reads=["blk64", "rk_kk2"], writes=["ps%d" % bank])
            A("act", lambda e, bank=bank: e.activation(out=tmpA[:], in_=ps[bank][:], func=AF.Sqrt, scale=64.0), reads=["ps%d" % bank], writes=["rk_tmpA"])
            A("dve", lambda e: e.tensor_scalar(out=tmpA[:], in0=tmpA[:], scalar1=1e-12, scalar2=None, op0=ALU.max), reads=["rk_tmpA"], writes=["rk_tmpA"])
            A("dve", lambda e: e.reciprocal(out=tmpA[:], in_=tmpA[:]), reads=["rk_tmpA"], writes=["rk_tmpA"])
            A("dve", lambda e: e.tensor_tensor(out=kkn[:], in0=kk[:], in1=tmpA[:], op=ALU.mult), reads=["rk_kk", "rk_tmpA"], writes=["rk_kkn"])
            A("dve", lambda e, c=c: e.tensor_scalar(out=tmpA[:], in0=aT[:], scalar1=-1.0, scalar2=self.vc(l, "k_a", 1, c), op0=ALU.add, op1=ALU.mult), reads=["rk_a", "vec", "rk_kkn"], writes=["rk_tmpA"])
            A("dve", lambda e: e.scalar_tensor_tensor(out=kp[:], in0=tmpA[:], scalar=1.0, in1=kTt[:], op0=ALU.add, op1=ALU.mult), reads=["rk_tmpA", "rk_k"], writes=["rk_kp"])
            A("dve", lambda e, c=c: e.scalar_tensor_tensor(out=rkb[:], in0=rT[:], scalar=self.vc(l, "r_k", 1, c), in1=kp[:], op0=ALU.mult, op1=ALU.mult), reads=["rk_r", "rk_kp", "vec"], writes=["rk_rkb"])
            bank = pbank()
            A("pe", lambda e, bank=bank: e.matmul(ps[bank][:], lhsT=self.blk64[:], rhs=rkb[:], start=True, stop=True), reads=["blk64", "rk_rkb"], writes=["ps%d" % bank])
            A("dve", lambda e, bank=bank, pb=pb: e.scalar_tensor_tensor(out=bonus[pb][:], in0=ps[bank][:], scalar=64.0, in1=vT[:], op0=ALU.mult, op1=ALU.mult), reads=["ps%d" % bank, "rk_v"], writes=[P("rk_bo")])
            A("dve", lambda e: e.tensor_tensor_scan(out=Lc[:], data0=rmask[:], data1=lwT[:], initial=0.0, op0=ALU.mult, op1=ALU.add), reads=["rmask", "rk_lw"], writes=["rk_L"])
            A("act", lambda e, pb=pb: e.activation(out=epos[pb][:], in_=Lc[:], func=AF.Exp), reads=["rk_L"], writes=[P("rk_ep")])
            A("act", lambda e: e.activation(out=eneg[:], in_=Lc[:], func=AF.Exp, scale=-1.0), reads=["rk_L"], writes=["rk_eneg"])
            A("dve", lambda e: e.tensor_tensor(out=tmpA[:], in0=Lc[:], in1=lwT[:], op=ALU.subtract), reads=["rk_L", "rk_lw", "rk_kp"], writes=["rk_tmpA"])
            A("act", lambda e: e.activation(out=eprev[:], in_=tmpA[:], func=AF.Exp), reads=["rk_tmpA"], writes=["rk_eprev"])
            A("dve", lambda e, pb=pb: e.scalar_tensor_tensor(out=AR[pb][:, 0, :], in0=kkn[:], scalar=-1.0, in1=eprev[:], op0=ALU.mult, op1=ALU.mult), reads=["rk_kkn", "rk_eprev"], writes=[P("rk_AR")])
            A("dve", lambda e: e.tensor_tensor(out=tmpA[:], in0=kkn[:], in1=aT[:], op=ALU.mult), reads=["rk_kkn", "rk_a", "rk_eprev"], writes=["rk_tmpA"])
            A("dve", lambda e, pb=pb: e.tensor_tensor(out=btb[pb][:], in0=tmpA[:], in1=eneg[:], op=ALU.mult), reads=["rk_tmpA", "rk_eneg"], writes=[P("rk_bt")])
            A("dve", lambda e, pb=pb: e.tensor_tensor(out=ktb[pb][:], in0=kp[:], in1=eneg[:], op=ALU.mult), reads=["rk_kp", "rk_eneg"], writes=[P("rk_kt")])
            A("dve", lambda e, pb=pb: e.tensor_tensor(out=AR[pb][:, 1, :], in0=rT[:], in1=epos[pb][:], op=ALU.mult), reads=["rk_r", P("rk_ep")], writes=[P("rk_AR")])

        load_pair(0)
        gch = 0
        blocks = [(c_, t_) for c_ in range(8) for t_ in range(NB)]
        rec = []
        RA = lambda *a_, **k_: rec.append((a_, k_))
        prep_block(0, 0, A)
        for bi, (c, tb) in enumerate(blocks):
            wb = c % 2
            if tb == 0:
                if c + 1 < 8:
                    load_pair(c + 1)
                A("dve", lambda e: e.memset(Hbd32[:], 0.0), writes=["c_H32"])
                A("dve", lambda e: e.memset(Hbdb[:], 0.0), writes=["c_Hbb"])
            if True:
                pb = (c * NB + tb) % 2
                tsl = slice(tb * TB, (tb + 1) * TB)
                P = lambda n: "%s%d" % (n, pb)
                del rec[:]
                if bi + 1 < len(blocks):
                    prep_block(blocks[bi + 1][0], blocks[bi + 1][1], RA)
                pend = list(rec)

                def drain(n):
                    for _ in range(n):
                        if pend:
                            a_, k_ = pend.pop(0)
                            A(*a_, **k_)
                if cut == 'prep':
                    self.end()
                    return
                def half_stages(hx, q0, pb=pb):
                    X = lambda n: "%s%d" % (n, hx)
                    cs = [slice((q0 + g) * CH, (q0 + g + 1) * CH) for g in range(G)]
                    gsl = slice(q0 * CH, (q0 + G) * CH)
                    B0 = hx * 1536
                    cA = B0
                    cB = B0 + 768
                    tb0, tb1, tb2 = X("pb_0_"), X("pb_1_"), X("pb_2_")
                    allb = [tb0, tb1]
                    tA = [tb0, tb1]
                    tB = [tb1, tb2]
                    vk, ax, br, ak, rk, sta, stb, pta, ptb, au, qm, rh, y0, gb, gam_t, rtmp, gtmp = [t[hx] for t in (VK, AX, BR, AK, RK, STa, STb, PTa, PTb, AU, QmTbd, RhT, Y0T, Gbd, Gam, Rtmp, Gtmp)]
                    stages = []

                    def st1():
                        for g in range(G):
                            base = B0 + g * 512
                            for ii, src in enumerate((vbf[pb][:, cs[g]], ktb[pb][:, cs[g]], AR[pb][:, 0, cs[g]], btb[pb][:, cs[g]])):
                                A("pe", lambda e, base=base, ii=ii, src=src: e.matmul(psB[0:64, base + ii * 128:base + (ii + 1) * 128], lhsT=src, rhs=self.ident_bf[:], start=True, stop=True),
                                  reads=[P("rk_vb"), P("rk_kt"), P("rk_AR"), P("rk_bt"), "ident_bf"], writes=allb)
                        V1 = psB[0:64, B0:B0 + G * 512].rearrange("p (g x) -> p g x", g=G)
                        A("act", lambda e: e.activation(out=vk[0:64].rearrange("p g a h x -> p g (a h x)"), in_=V1[:, :, 0:256], func=AF.Copy), reads=allb, writes=[X("g_VK")])
                        A("act", lambda e: e.activation(out=ax[0:64, :, :, 0, :], in_=V1[:, :, 256:384].rearrange("p g (h x) -> p g h x", h=2), func=AF.Copy), reads=allb, writes=[X("g_AX")])
                        A("act", lambda e: e.activation(out=br[0:64, :, :, 0, :], in_=V1[:, :, 384:512].rearrange("p g (h x) -> p g h x", h=2), func=AF.Copy), reads=allb, writes=[X("g_BR")])
                    stages.append(st1)

                    def st2():
                        for h in range(2):
                            hs = slice(64 * h, 64 * h + 64)
                            tk = [tb0] if h == 0 else [tb1]
                            for g in range(G):
                                base = B0 + h * 512 + g * 256
                                A("pe", lambda e, hs=hs, base=base, g=g: e.matmul(psB[0:64, base:base + 128], lhsT=btb[pb][hs, cs[g]], rhs=AR[pb][hs, :, cs[g]], start=True, stop=True), reads=[P("rk_bt"), P("rk_AR")], writes=tk)
                                A("pe", lambda e, hs=hs, base=base, g=g: e.matmul(psB[0:64, base + 128:base + 256], lhsT=ktb[pb][hs, cs[g]], rhs=AR[pb][hs, :, cs[g]], start=True, stop=True), reads=[P("rk_kt"), P("rk_AR")], writes=tk)
                        V2 = psB[0:64, B0:B0 + 1024].rearrange("p (h g x) -> p h g x", h=2, g=G)
                        hg = lambda t: t[0:64].rearrange("p g h x -> p h g x")
                        mS = mkS[0:64].rearrange("p (h g) x -> p h g x", h=2)
                        mI = mkI[0:64].rearrange("p (h g) x -> p h g x", h=2)
                        A("dve", lambda e: e.tensor_tensor(out=sta[0:64].rearrange("p g h a x -> p h g a x")[:, :, :, 0, :], in0=V2[:, :, :, 0:64], in1=mS, op=ALU.mult), reads=allb + ["mkS"], writes=[X("g_STa")])
                        A("dve", lambda e: e.tensor_tensor(out=br[0:64].rearrange("p g h a x -> p h g a x")[:, :, :, 1, :], in0=V2[:, :, :, 64:128], in1=mI, op=ALU.mult), reads=allb + ["mkI"], writes=[X("g_BR")])
                        A("dve", lambda e: e.tensor_tensor(out=hg(ak), in0=V2[:, :, :, 128:192], in1=mS, op=ALU.mult), reads=allb + ["mkS"], writes=[X("g_AK")])
                        A("dve", lambda e: e.tensor_tensor(out=hg(rk), in0=V2[:, :, :, 192:256], in1=mI, op=ALU.mult), reads=allb + ["mkI"], writes=[X("g_RK")])
                        A("act", lambda e: e.activation(out=sta[0:64].rearrange("p g h a x -> p (g h) a x")[:, :, 1, :], in_=self.ident_bf[0:64, None, 0:64].to_broadcast([64, 2 * G, 64]), func=AF.Copy), reads=["ident_bf"], writes=[X("g_STa")])
                        for g in range(G):
                            for h in range(2):
                                o = B0 + 1024 + (g * 2 + h) * 64
                                A("pe", lambda e, g=g, h=h, o=o: e.matmul(psB[0:64, o:o + 64], lhsT=sta[0:64, g, h, 0, :], rhs=self.ident_bf[0:64, 0:64], start=True, stop=True), reads=[X("g_STa"), "ident_bf"], writes=[tb2])
                        A("act", lambda e: e.activation(out=pta[0:64].rearrange("p g h x -> p (g h x)"), in_=psB[0:64, B0 + 1024:B0 + 1024 + G * 128], func=AF.Copy), reads=[tb2], writes=[X("g_PTa")])
                    stages.append(st2)
                    STs = (sta, stb)
                    STn = (X("g_STa"), X("g_STb"))
                    PTs = (pta, ptb)
                    PTn = (X("g_PTa"), X("g_PTb"))
                    V3 = psB[0:64, cA:cA + G * 384].rearrange("p (g x) -> p g x", g=G)
                    V3a = V3[:, :, 0:256].rearrange("p g (h a x) -> p g h a x", h=2, a=2)

                    def mk_level(kk_):
                        def lv():
                            cur, nxt = STs[kk_ % 2], STs[(kk_ + 1) % 2]
                            cn, nn = STn[kk_ % 2], STn[(kk_ + 1) % 2]
                            pcur, pnxt = PTs[kk_ % 2], PTs[(kk_ + 1) % 2]
                            pcn, pnn = PTn[kk_ % 2], PTn[(kk_ + 1) % 2]
                            for g in range(G):
                                for h in range(2):
                                    o = cA + g * 384 + h * 128
                                    if kk_ < 5:
                                        A("pe", lambda e, g=g, h=h, o=o: e.matmul(psB[0:64, o:o + 128], lhsT=pcur[0:64, g, h, :], rhs=cur[0:64, g, h, :, :], start=True, stop=False), reads=[cn, pcn], writes=tA)
                                        A("pe", lambda e, g=g, h=h, o=o: e.matmul(psB[0:64, o + 64:o + 128], lhsT=self.ident_bf[0:64, 0:64], rhs=cur[0:64, g, h, 1, :], start=False, stop=True), reads=[cn, "ident_bf"], writes=tA)
                                        o2 = cA + g * 384 + 256 + h * 64
                                        A("pe", lambda e, g=g, h=h, o2=o2: e.matmul(psB[0:64, o2:o2 + 64], lhsT=cur[0:64, g, h, 0, :], rhs=pcur[0:64, g, h, :], start=True, stop=True), reads=[cn, pcn], writes=tA)
                                    else:
                                        A("pe", lambda e, g=g, h=h, o=o: e.matmul(psB[0:64, o + 64:o + 128], lhsT=pcur[0:64, g, h, :], rhs=cur[0:64, g, h, 1, :], start=True, stop=False), reads=[cn, pcn], writes=tA)
                                        A("pe", lambda e, g=g, h=h, o=o: e.matmul(psB[0:64, o + 64:o + 128], lhsT=self.ident_bf[0:64, 0:64], rhs=cur[0:64, g, h, 1, :], start=False, stop=True), reads=[cn, "ident_bf"], writes=tA)
                            if kk_ < 5:
                                A("act", lambda e: e.activation(out=nxt[0:64].rearrange("p g h a x -> p g (h a x)"), in_=V3[:, :, 0:256], func=AF.Copy), reads=tA, writes=[nn])
                                A("act", lambda e: e.activation(out=pnxt[0:64], in_=V3[:, :, 256:384].rearrange("p g (h x) -> p g h x", h=2), func=AF.Copy), reads=tA, writes=[pnn])
                            else:
                                A("act", lambda e: e.activation(out=nxt[0:64, :, :, 1, :], in_=V3a[:, :, :, 1, :], func=AF.Copy), reads=tA, writes=[nn])
                        return lv
                    for kk_ in range(6):
                        stages.append(mk_level(kk_))
                    TTf = STs[0]
                    TTn = STn[0]
                    V4 = psB[0:64, cB:cB + G * 384].rearrange("p (g x) -> p g x", g=G)

                    def st4x():
                        for g in range(G):
                            for h in range(2):
                                o = cB + g * 384 + h * 64
                                A("pe", lambda e, g=g, h=h, o=o: e.matmul(psB[0:64, o:o + 64], lhsT=ak[0:64, g, h, :], rhs=vk[0:64, g, 0, h, :], start=True, stop=True), reads=[X("g_AK"), X("g_VK")], writes=tB)
                        A("act", lambda e: e.activation(out=ax[0:64, :, :, 1, :], in_=V4[:, :, 0:128].rearrange("p g (h x) -> p g h x", h=2), func=AF.Copy), reads=tB, writes=[X("g_AX")])
                    stages.append(st4x)

                    def st4u():
                        for g in range(G):
                            for h in range(2):
                                o = cB + g * 384 + 128 + h * 128
                                A("pe", lambda e, g=g, h=h, o=o: e.matmul(psB[0:64, o:o + 128], lhsT=TTf[0:64, g, h, 1, :], rhs=ax[0:64, g, h, :, :], start=True, stop=True), reads=[TTn, X("g_AX")], writes=tB)
                        A("act", lambda e: e.activation(out=au[0:64].rearrange("p g h a x -> p g (h a x)"), in_=V4[:, :, 128:384], func=AF.Copy), reads=tB, writes=[X("g_AU")])
                    stages.append(st4u)
                    V5 = psB[:, cA:cA + G * 384].rearrange("p (g x) -> p g x", g=G)

                    def st4b():
                        for g in range(G):
                            for h in range(2):
                                hs = slice(64 * h, 64 * h + 64)
                                o = cA + g * 384
                                A("pe", lambda e, g=g, h=h, hs=hs, o=o: e.matmul(psB[hs, o:o + 128], lhsT=au[0:64, g, h, 0, :], rhs=br[0:64, g, h, :, :], start=True, stop=True), reads=[X("g_AU"), X("g_BR")], writes=tA)
                                A("pe", lambda e, g=g, h=h, hs=hs, o=o: e.matmul(psB[hs, o + 128:o + 192], lhsT=au[0:64, g, h, 1, :], rhs=br[0:64, g, h, 1, :], start=True, stop=False), reads=[X("g_AU"), X("g_BR")], writes=tA)
                                A("pe", lambda e, g=g, h=h, hs=hs, o=o: e.matmul(psB[hs, o + 128:o + 192], lhsT=vk[0:64, g, 0, h, :], rhs=rk[0:64, g, h, :], start=False, stop=True), reads=[X("g_VK"), X("g_RK")], writes=tA)
                                A("pe", lambda e, g=g, h=h, hs=hs, o=o: e.matmul(psB[hs, o + 192:o + 256], lhsT=br[0:64, g, h, 0, :], rhs=au[0:64, g, h, 1, :], start=True, stop=False), reads=[X("g_AU"), X("g_BR")], writes=tA)
                                A("pe", lambda e, g=g, h=h, hs=hs, o=o: e.matmul(psB[hs, o + 192:o + 256], lhsT=vk[0:64, g, 1, h, :], rhs=vk[0:64, g, 0, h, :], start=False, stop=True), reads=[X("g_VK")], writes=tA)
                        A("act", lambda e: e.activation(out=qm[0:64, :, 0:64], in_=V5[0:64, :, 0:64], func=AF.Copy), reads=tA, writes=[X("g_Qm")])
                        A("act", lambda e: e.activation(out=qm[64:128, :, 64:128], in_=V5[64:128, :, 0:64], func=AF.Copy), reads=tA, writes=[X("g_Qm")])
                        A("act", lambda e: e.activation(out=rtmp[:], in_=V5[:, :, 64:128], func=AF.Copy), reads=tA, writes=[X("g_Rtmp")])
                        A("dve", lambda e: e.tensor_tensor(out=rh[:], in0=rtmp[:], in1=AR[pb][:, 1, gsl].rearrange("p (g t) -> p g t", g=G), op=ALU.add), reads=[X("g_Rtmp"), P("rk_AR")], writes=[X("g_Rh")])
                        A("act", lambda e: e.activation(out=y0[:], in_=V5[:, :, 128:192], func=AF.Copy), reads=tA, writes=[X("g_Y0")])
                        A("act", lambda e: e.activation(out=gtmp[:], in_=V5[:, :, 192:256], func=AF.Copy), reads=tA, writes=[X("g_Gtmp")])
                        gam_all = epos[pb][:, gsl].rearrange("p (g t) -> p g t", g=G)[:, :, CH - 1:CH].to_broadcast([128, G, 64])
                        A("act", lambda e: e.activation(out=gam_t[:], in_=gam_all, func=AF.Copy), reads=[P("rk_ep")], writes=[X("g_Gam")])
                        for h in range(2):
                            hs = slice(64 * h, 64 * h + 64)
                            A("dve", lambda e, hs=hs, h=h: e.tensor_tensor(out=gb[hs, :, 64 * h:64 * h + 64], in0=gtmp[hs, :, :], in1=gam_t[hs, :, :], op=ALU.mult), reads=[X("g_Gtmp"), X("g_Gam")], writes=[X("g_Gb")])
                    stages.append(st4b)

                    def st5():
                        for g in range(G):
                            gam = epos[pb][:, (q0 + g) * CH + CH - 1:(q0 + g) * CH + CH]
                            A("pe", lambda e, g=g: e.matmul(ps[7][:, 0:128], lhsT=qm[:, g, :], rhs=Hbdb[:], start=True, stop=True), reads=[X("g_Qm"), "c_Hbb"], writes=["ps7"])
                            A("pe", lambda e, g=g: e.matmul(ps[7][:, 128:192], lhsT=Hbdb[:], rhs=rh[:, g, :], start=True, stop=True), reads=[X("g_Rh"), "c_Hbb"], writes=["ps7"])
                            A("dve", lambda e, g=g: e.tensor_tensor(out=yT[pb][:, cs[g]], in0=ps[7][:, 128:192], in1=y0[:, g, :], op=ALU.add), reads=["ps7", X("g_Y0")], writes=[P("rk_y")])
                            A("dve", lambda e: e.tensor_tensor(out=tmpH[:], in0=ps[7][:, 0:128], in1=Hbd32[:], op=ALU.add), reads=["ps7", "c_H32"], writes=["c_tH"])
                            A("dve", lambda e, g=g, gam=gam: e.scalar_tensor_tensor(out=Hbd32[:], in0=tmpH[:], scalar=gam, in1=gb[:, g, :], op0=ALU.mult, op1=ALU.add), reads=["c_tH", X("g_Gb"), P("rk_ep")], writes=["c_H32"])
                            A("act", lambda e: e.activation(out=Hbdb[:], in_=Hbd32[:], func=AF.Copy), reads=["c_H32"], writes=["c_Hbb"])
                    stages.append(st5)
                    return stages

                for gi in range(TB // CH // (2 * G)):
                    gch += 1
                    if cut.startswith('grp') and gch > int(cut[3:]):
                        self.end()
                        return
                    sx = half_stages(0, gi * 2 * G)
                    sy = half_stages(1, gi * 2 * G + G)
                    nst = int(_os.environ.get('RWKV_NST', '99'))
                    for si, (fx, fy) in enumerate(zip(sx, sy)):
                        if si >= nst:
                            self.end()
                            return
                        fx()
                        fy()
                        drain(4)
                drain(10 ** 6)
                if cut.startswith('prepost') and (c * NB + tb + 1) >= int(cut[7:]):
                    self.end()
                    return
                y = yT[pb]
                A("act", lambda e, y=y: e.activation(out=ybf[:], in_=y[:], func=AF.Copy), reads=[P("rk_y")], writes=["rk_ybf"])
                bank = pbank()
                A("pe", lambda e, bank=bank: e.matmul(ps[bank][:], lhsT=self.blk64[:], rhs=ybf[:], start=True, stop=True), reads=["blk64", "rk_ybf"], writes=["ps%d" % bank])
                A("dve", lambda e, y=y, bank=bank: e.tensor_tensor(out=y[:], in0=y[:], in1=ps[bank][:], op=ALU.subtract), reads=[P("rk_y"), "ps%d" % bank], writes=[P("rk_y")])
                A("act", lambda e, y=y: e.activation(out=ybf[:], in_=y[:], func=AF.Square), reads=[P("rk_y")], writes=["rk_ybf"])
                bank = pbank()
                A("pe", lambda e, bank=bank: e.matmul(ps[bank][:], lhsT=self.blk64[:], rhs=ybf[:], start=True, stop=True), reads=["blk64", "rk_ybf"], writes=["ps%d" % bank])
                A("act", lambda e, bank=bank: e.activation(out=tmpA[:], in_=ps[bank][:], func=AF.Sqrt, bias=self.epsc[:, 2:3]), reads=["ps%d" % bank, "epsc"], writes=["rk_tmpA"])
                A("dve", lambda e: e.reciprocal(out=tmpA[:], in_=tmpA[:]), reads=["rk_tmpA"], writes=["rk_tmpA"])
                A("dve", lambda e, y=y: e.tensor_tensor(out=y[:], in0=y[:], in1=tmpA[:], op=ALU.mult), reads=[P("rk_y"), "rk_tmpA"], writes=[P("rk_y")])
                A("act", lambda e, y=y, c=c: e.activation(out=y[:], in_=y[:], func=AF.Identity, scale=self.vc(l, "lnx_g", 1, c), bias=self.vc(l, "lnx_b", 1, c)), reads=[P("rk_y"), "vec"], writes=[P("rk_y")])
                A("dve", lambda e, y=y, pb=pb: e.tensor_tensor(out=y[:], in0=y[:], in1=bonus[pb][:], op=ALU.add), reads=[P("rk_y"), P("rk_bo")], writes=[P("rk_y")])
                A("dve", lambda e, y=y, pb=pb: e.tensor_tensor(out=yout[pb][:], in0=y[:], in1=gT[pb][:], op=ALU.mult), reads=[P("rk_y"), P("rk_g")], writes=[P("rk_yo")])
                A("sp", lambda e, pb=pb, c=c, tsl=tsl: e.dma_start(out=self.rtd[c * 128:(c + 1) * 128, tsl], in_=yout[pb][:]), reads=[P("rk_yo")], writes=["rtd%d" % pb], dma=True)
                if cut.startswith('post') and (c * NB + tb + 1) >= int(cut[4:]):
                    self.end()
                    return
        self.end()

    def phase_merge(self, l):
        A = self.A
        ps = self.ps
        self.begin()
        wbr = [self.sb("wbr%d" % j, [128, 8, D], BF16) for j in range(3)]
        srcw = (self.w["w_br_attn"][l], self.w["w_br_pool"][l], self.w["w_br_rwkv"][l])
        srcd = (self.otd, self.ptd, self.rtd)
        for hf in range(2):
            for j in range(3):
                A("pool", lambda e, j=j, hf=hf: e.dma_start(out=wbr[j][:, :, hf * 512:(hf + 1) * 512], in_=srcw[j][:, hf * 512:(hf + 1) * 512].rearrange("(c p) n -> p c n", p=128)),
                  writes=["wbr%d" % j], dma=True)
        ob = [self.sb("mg_o%d" % j, [128, 8, TB], BF16) for j in range(3)]
        gbuf = self.sb("mg_g", [128, 3, 8, TB], BF16)
        tt = [self.sb("mg_t%d" % j, [128, TB], F32) for j in range(3)]
        mT = [self.sb("mg_m%d" % i, [128, 8, TB], BF16) for i in range(2)]
        k = 0
        for tb in range(NB):
            tsl = slice(tb * TB, (tb + 1) * TB)
            for j in range(3):
                A("sp", lambda e, j=j, tsl=tsl: e.dma_start(out=ob[j][:], in_=srcd[j].rearrange("(c p) t -> p c t", p=128)[:, :, tsl]), writes=["mg_o%d" % j], dma=True)
                A("sp", lambda e, j=j, tsl=tsl: e.dma_start(out=gbuf[:, j, :, :], in_=self.gd[j * 1024:(j + 1) * 1024, :].rearrange("(c p) t -> p c t", p=128)[:, :, tsl]), writes=["mg_g"], dma=True)
            mt = mT[tb % 2]
            mn = "mg_m%d" % (tb % 2)
            for m in range(8):
                for j in range(3):
                    bank = (k % 2) * 3 + j
                    for c in range(8):
                        A("pe", lambda e, j=j, c=c, m=m, bank=bank: e.matmul(ps[bank][:], lhsT=wbr[j][:, c, m * 128:(m + 1) * 128], rhs=ob[j][:, c, :], start=(c == 0), stop=(c == 7)),
                          reads=["wbr%d" % j, "mg_o%d" % j], writes=["ps%d" % bank])
                    A("dve", lambda e, j=j, m=m, bank=bank: e.tensor_tensor(out=tt[j][:], in0=ps[bank][:], in1=gbuf[:, j, m, :], op=ALU.mult), reads=["ps%d" % bank, "mg_g"], writes=["mg_t%d" % j])
                k += 1
                A("pool", lambda e: e.tensor_tensor(out=tt[0][:], in0=tt[0][:], in1=tt[1][:], op=ALU.add), reads=["mg_t0", "mg_t1"], writes=["mg_t0"])
                A("pool", lambda e, mt=mt, m=m: e.tensor_tensor(out=mt[:, m, :], in0=tt[0][:], in1=tt[2][:], op=ALU.add), reads=["mg_t0", "mg_t2"], writes=[mn])
            A("sp", lambda e, mt=mt, tsl=tsl: e.dma_start(out=self.mgd.rearrange("(c p) t -> p c t", p=128)[:, :, tsl], in_=mt[:]), reads=[mn], writes=["mgd"], dma=True)
        self.end()

    def proj_res_ln(self, l, wres, wname, load_src, gname, bname, final=False, tmps=None, zT=None):
        A = self.A
        ps = self.ps
        for tb in range(NB):
            tsl = slice(tb * TB, (tb + 1) * TB)
            z = zT[tb % 2]
            zn = "zT%d" % (tb % 2)
            A("sp", lambda e, z=z, tsl=tsl: e.dma_start(out=z[:], in_=self.h32d.rearrange("(c p) t -> p c t", p=128)[:, :, tsl]), reads=["h32d"], writes=[zn], dma=True)
            src, sn = load_src(tb)
            for m in range(8):
                bank = m % 2
                for c in range(8):
                    A("pe", lambda e, src=src, c=c, m=m, bank=bank: e.matmul(ps[bank][:], lhsT=wres[:, c, m * 128:(m + 1) * 128], rhs=src[:, c, :], start=(c == 0), stop=(c == 7)),
                      reads=[wname, sn], writes=["ps%d" % bank])
                A("dve", lambda e, z=z, m=m, bank=bank: e.scalar_tensor_tensor(out=z[:, m, :], in0=z[:, m, :], scalar=DN_ALPHA, in1=ps[bank][:], op0=ALU.mult, op1=ALU.add),
                  reads=[zn, "ps%d" % bank], writes=[zn])
            self.ln_T(z, zn, self.vc(l, gname, 8), self.vc(l, bname, 8), tb, tmps, final=final)

    def phase_wout(self, l):
        A = self.A
        self.begin()
        tmps = self.ln_tmps()
        wo = self.sb("wo", [128, 8, D], BF16)
        for hf in range(2):
            A("pool", lambda e, hf=hf: e.dma_start(out=wo[:, :, hf * 512:(hf + 1) * 512], in_=self.w["w_out"][l][:, hf * 512:(hf + 1) * 512].rearrange("(c p) n -> p c n", p=128)),
              writes=["wo"], dma=True)
        zT = [self.sb("zT%d" % i, [128, 8, TB], F32) for i in range(2)]
        mi = [self.sb("mi%d" % i, [128, 8, TB], BF16) for i in range(2)]

        def load_src(tb):
            b = mi[tb % 2]
            n = "mi%d" % (tb % 2)
            A("sp", lambda e, b=b, tb=tb: e.dma_start(out=b[:], in_=self.mgd.rearrange("(c p) t -> p c t", p=128)[:, :, tb * TB:(tb + 1) * TB]), reads=["mgd"], writes=[n], dma=True)
            return b, n
        self.proj_res_ln(l, wo, "wo", load_src, "ln1_g", "ln1_b", tmps=tmps, zT=zT)
        self.end()

    def phase_xattn(self, l, b_unused=None):
        A = self.A
        ps = self.ps
        self.begin()
        tmps = self.ln_tmps()
        ident = self.cc("ident", 128)
        memN = self.sb("memN", [128, 2, D], F32)
        memT = self.sb("memT", [128, 8, 256], BF16)
        kxT = self.sb("kxT", [128, 8, 256], BF16)
        vx = self.sb("vx", [128, 2, D], BF16)
        wq = self.sb("xwq", [128, 8, D], BF16)
        wo = self.sb("xwo", [128, 8, D], BF16)
        wst = [self.sb("xws%d" % i, [128, 8, 512], BF16) for i in range(2)]
        A("sp", lambda e: e.dma_start(out=memN[:], in_=self.mem.rearrange("(t p) d -> p t d", p=128)), writes=["memN"], dma=True)
        for c in range(8):
            bank = c % 2
            for t2 in range(2):
                A("pe", lambda e, c=c, t2=t2, bank=bank: e.transpose(ps[bank][:, t2 * 128:(t2 + 1) * 128], memN[:, t2, c * 128:(c + 1) * 128], ident), reads=["memN", "cst"], writes=["ps%d" % bank])
            A("act", lambda e, c=c, bank=bank: e.activation(out=memT[:, c, :], in_=ps[bank][:, 0:256], func=AF.Copy), reads=["ps%d" % bank], writes=["memT"])
        w_xkv = self.w["w_xkv"][l]
        for blk in range(4):
            ws = wst[blk % 2]
            wn = "xws%d" % (blk % 2)
            self.load_w(ws[:], wn, w_xkv[:, blk * 512:(blk + 1) * 512])
            if blk < 2:
                for mm_ in range(4):
                    m = blk * 4 + mm_
                    bank = 2 + (m % 2)
                    for c in range(8):
                        A("pe", lambda e, ws=ws, c=c, mm_=mm_, bank=bank: e.matmul(ps[bank][:, 0:256], lhsT=ws[:, c, mm_ * 128:(mm_ + 1) * 128], rhs=memT[:, c, :], start=(c == 0), stop=(c == 7)),
                          reads=[wn, "memT"], writes=["ps%d" % bank])
                    A("act", lambda e, m=m, bank=bank: e.activation(out=kxT[:, m, :], in_=ps[bank][:, 0:256], func=AF.Copy), reads=["ps%d" % bank], writes=["kxT"])
            else:
                for kt in range(2):
                    bank = 2 + kt
                    for c in range(8):
                        A("pe", lambda e, ws=ws, c=c, kt=kt, bank=bank: e.matmul(ps[bank][:], lhsT=memT[:, c, kt * 128:(kt + 1) * 128], rhs=ws[:, c, :], start=(c == 0), stop=(c == 7)),
                          reads=[wn, "memT"], writes=["ps%d" % bank])
                    A("act", lambda e, kt=kt, bank=bank, blk=blk: e.activation(out=vx[:, kt, (blk - 2) * 512:(blk - 1) * 512], in_=ps[bank][:], func=AF.Copy), reads=["ps%d" % bank], writes=["vx"])
        for (dst, dn, src) in ((wq, "xwq", self.w["w_xq"][l]), (wo, "xwo", self.w["w_xo"][l])):
            for hf in range(2):
                A("pool", lambda e, dst=dst, src=src, hf=hf: e.dma_start(out=dst[:, :, hf * 512:(hf + 1) * 512], in_=src[:, hf * 512:(hf + 1) * 512].rearrange("(c p) n -> p c n", p=128)),
                  writes=[dn], dma=True)
        qx = [self.sb("qx%d" % i, [128, 8, TB], BF16) for i in range(1)] * 2
        ox = [self.sb("ox%d" % i, [128, 8, TB], BF16) for i in range(1)] * 2
        eT = [self.sb("xe%d" % i, [128, TB], BF16) for i in range(3)]
        rr = self.sb("xrr", [128, TB], F32)
        zT = [self.sb("zT%d" % i, [128, 8, TB], F32) for i in range(2)]
        ecnt = [0]

        def load_src(tb):
            tsl = slice(tb * TB, (tb + 1) * TB)
            q = qx[tb % 2]
            qn = "qx0"
            o = ox[tb % 2]
            on = "ox0"
            for m in range(8):
                bank = m % 2
                for c in range(8):
                    A("pe", lambda e, c=c, m=m, bank=bank, tsl=tsl: e.matmul(ps[bank][:], lhsT=wq[:, c, m * 128:(m + 1) * 128], rhs=self.hbT[:, c, tsl], start=(c == 0), stop=(c == 7)),
                      reads=["xwq", "hbT"], writes=["ps%d" % bank])
                A("act", lambda e, q=q, m=m, bank=bank: e.activation(out=q[:, m, :], in_=ps[bank][:], func=AF.Copy), reads=["ps%d" % bank], writes=[qn])
            for h in range(4):
                for kt in range(2):
                    sbk = 2 + (ecnt[0] % 2)
                    et = eT[ecnt[0] % 3]
                    en = "xe%d" % (ecnt[0] % 3)
                    ecnt[0] += 1
                    for mm_ in range(2):
                        A("pe", lambda e, q=q, h=h, kt=kt, mm_=mm_, sbk=sbk: e.matmul(ps[sbk][:], lhsT=kxT[:, 2 * h + mm_, kt * 128:(kt + 1) * 128], rhs=q[:, 2 * h + mm_, :],
                                                                                 start=(mm_ == 0), stop=(mm_ == 1)), reads=["kxT", qn], writes=["ps%d" % sbk])
                    A("act", lambda e, et=et, sbk=sbk: e.activation(out=et[:], in_=ps[sbk][:], func=AF.Exp, scale=1.0 / 16.0), reads=["ps%d" % sbk], writes=[en])
                    for mm_ in range(2):
                        A("pe", lambda e, et=et, h=h, kt=kt, mm_=mm_: e.matmul(ps[4 + mm_][:], lhsT=vx[:, kt, (2 * h + mm_) * 128:(2 * h + mm_ + 1) * 128], rhs=et[:], start=(kt == 0), stop=(kt == 1)),
                          reads=["vx", en], writes=["ps%d" % (4 + mm_)])
                    A("pe", lambda e, et=et, kt=kt: e.matmul(ps[6][:], lhsT=self.ones_k[:], rhs=et[:], start=(kt == 0), stop=(kt == 1)), reads=["ones_k", en], writes=["ps6"])
                A("dve", lambda e: e.reciprocal(out=rr[:], in_=ps[6][:]), reads=["ps6"], writes=["xrr"])
                for mm_ in range(2):
                    A("dve", lambda e, o=o, h=h, mm_=mm_: e.tensor_tensor(out=o[:, 2 * h + mm_, :], in0=ps[4 + mm_][:], in1=rr[:], op=ALU.mult), reads=["ps%d" % (4 + mm_), "xrr"], writes=[on])
            return o, on
        self.proj_res_ln(l, wo, "xwo", load_src, "ln2_g", "ln2_b", tmps=tmps, zT=zT)
        self.end()

    def phase_ffn(self, l, final=False):
        A = self.A
        ps = self.ps
        self.begin()
        tmps = self.ln_tmps()
        w1 = [self.sb("fw1_%d" % i, [128, 8, 512], BF16) for i in range(2)]
        w2 = [self.sb("fw2_%d" % i, [128, 32, 128], BF16) for i in range(2)]
        h1 = self.sb("fh1", [128, 32, 2 * TB], BF16)
        rl = [self.sb("frl%d" % i, [128, TB], F32) for i in range(2)]
        zT = [self.sb("zT%d" % i, [128, 8, TB], F32) for i in range(2)]
        if final:
            self.outN = [self.sb("outN%d" % i, [128, D], F32) for i in range(2)]
        w_ff1 = self.w["w_ff1"][l]
        w_ff2 = self.w["w_ff2"][l]
        k = 0
        for th in range(NB // 2):
            tbs = (2 * th, 2 * th + 1)
            for t2, tb in enumerate(tbs):
                tsl = slice(tb * TB, (tb + 1) * TB)
                A("sp", lambda e, t2=t2, tsl=tsl: e.dma_start(out=zT[t2][:], in_=self.h32d.rearrange("(c p) t -> p c t", p=128)[:, :, tsl]), reads=["h32d"], writes=["zT%d" % t2], dma=True)
            for fb in range(8):
                ws = w1[fb % 2]
                wn = "fw1_%d" % (fb % 2)
                self.load_w(ws[:], wn, w_ff1[:, fb * 512:(fb + 1) * 512])
                for ff in range(4):
                    f = fb * 4 + ff
                    for t2, tb in enumerate(tbs):
                        tsl = slice(tb * TB, (tb + 1) * TB)
                        bank = k % 2
                        r = rl[k % 2]
                        rn = "frl%d" % (k % 2)
                        k += 1
                        for c in range(8):
                            A("pe", lambda e, ws=ws, c=c, ff=ff, bank=bank, tsl=tsl: e.matmul(ps[bank][:], lhsT=ws[:, c, ff * 128:(ff + 1) * 128], rhs=self.hbT[:, c, tsl], start=(c == 0), stop=(c == 7)),
                              reads=[wn, "hbT"], writes=["ps%d" % bank])
                        A("act", lambda e, r=r, bank=bank: e.activation(out=r[:], in_=ps[bank][:], func=AF.Relu), reads=["ps%d" % bank], writes=[rn])
                        A("dve", lambda e, r=r, f=f, bank=bank, t2=t2: e.tensor_tensor(out=h1[:, f, t2 * TB:(t2 + 1) * TB], in0=ps[bank][:], in1=r[:], op=ALU.mult), reads=["ps%d" % bank, rn], writes=["fh1"])
            for m in range(8):
                ws = w2[m % 2]
                wn = "fw2_%d" % (m % 2)
                A("pool", lambda e, ws=ws, m=m: e.dma_start(out=ws[:], in_=w_ff2[:, m * 128:(m + 1) * 128].rearrange("(f p) n -> p f n", p=128)), writes=[wn], dma=True)
                for t2 in range(2):
                    bank = 2 + t2
                    for f in range(32):
                        A("pe", lambda e, ws=ws, f=f, bank=bank, t2=t2: e.matmul(ps[bank][:], lhsT=ws[:, f, :], rhs=h1[:, f, t2 * TB:(t2 + 1) * TB], start=(f == 0), stop=(f == 31)), reads=[wn, "fh1"], writes=["ps%d" % bank])
                    A("dve", lambda e, m=m, bank=bank, t2=t2: e.scalar_tensor_tensor(out=zT[t2][:, m, :], in0=zT[t2][:, m, :], scalar=DN_ALPHA, in1=ps[bank][:], op0=ALU.mult, op1=ALU.add),
                      reads=["zT%d" % t2, "ps%d" % bank], writes=["zT%d" % t2])
            for t2, tb in enumerate(tbs):
                self.ln_T(zT[t2], "zT%d" % t2, self.vc(l, "ln3_g", 8), self.vc(l, "ln3_b", 8), tb, tmps, final=final)
        self.end()

    def finish(self):
        self.p.add("sp", None, extra_deps=list(self.p.pending_dma))
        self.p.emit()


def build(dbg=False, stop=None, skip=()):
    k = K(dbg=dbg, stop=stop)
    k.setup()
    k.phase_ln_in()
    phases = []
    for l in range(DEPTH):
        phases += [("attn", l), ("pool", l), ("gates", l), ("rwkv", l), ("merge", l), ("wout", l), ("xattn", l), ("ffn", l)]
    for (nm, l) in phases:
        if nm not in skip:
            if nm == "ffn":
                k.phase_ffn(l, final=(l == DEPTH - 1))
            else:
                getattr(k, "phase_" + nm)(l)
        if stop == "%s%d" % (nm, l):
            break
    k.finish()
    return k.nc


def kernel(**inputs):
    inputs = {k: np.asarray(v) for k, v in inputs.items()}
    nc = build()
    consts = _host_consts(inputs["ln_in_g"], inputs["ln_in_b"])
    vecs = _host_vecs(inputs)
    in_maps = []
    for b in range(8):
        m = {"x": np.ascontiguousarray(inputs["x"][b]), "mem": np.ascontiguousarray(inputs["mem"][b]),
             "pos": np.ascontiguousarray(np.broadcast_to(inputs["positions"][b][None, :], (128, S)).astype(np.int32)),
             "consts": consts, "vecs": vecs}
        for n, _ in WEIGHTS:
            m[n] = inputs[n]
        in_maps.append(m)
    res = run_bass_kernel_spmd(nc, in_maps, core_ids=list(range(8)))
    return np.stack([r["out"] for r in res.results], 0).astype(np.float32)
```

```python
import math
import numpy as np
import concourse.bass as bass
import concourse.mybir as mybir
from concourse.bass_utils import run_bass_kernel_spmd

F32 = mybir.dt.float32
BF16 = mybir.dt.bfloat16
I32 = mybir.dt.int32
AF = mybir.ActivationFunctionType
ALU = mybir.AluOpType
AX = mybir.AxisListType

S = 2048
D = 1024
NB = 4
TB = 512
DEPTH = 2
P_IN = 10528
DN_ALPHA = (2 * DEPTH) ** 0.25
LN_EPS = 1e-5
RMS_EPS = 1e-5
GN_EPS = 64e-5
CH = 64
NCH = S // CH


class Tok:
    __slots__ = ("name", "writer", "readers")

    def __init__(self, name):
        self.name = name
        self.writer = None
        self.readers = {}


class Ins:
    __slots__ = ("eng", "fn", "deps", "signal", "dma", "sem", "semval", "sigval", "slotdep", "idx")


class Prog:
    ENGS = ("pe", "act", "dve", "pool", "sp")
    RING = 12

    def __init__(self, nc):
        self.nc = nc
        self.streams = {e: [] for e in self.ENGS}
        self.n = 0
        self.engsem = {e: nc.alloc_semaphore("cs_" + e) for e in self.ENGS}
        self.rings = {}
        self.ringpos = {}
        self.ringval = {}
        for q in ("sp", "pool", "act"):
            self.rings[q] = [nc.alloc_semaphore("dq_%s_%d" % (q, i)) for i in range(self.RING)]
            self.ringpos[q] = 0
            self.ringval[q] = [0] * self.RING
        self.toks = {}
        self.pending_dma = []

    def tok(self, name):
        t = self.toks.get(name)
        if t is None:
            t = self.toks[name] = Tok(name)
        return t

    def add(self, eng, fn, reads=(), writes=(), dma=False, extra_deps=()):
        ins = Ins()
        ins.eng = eng
        ins.fn = fn
        ins.dma = dma
        ins.signal = False
        ins.sem = None
        ins.semval = 0
        ins.sigval = 0
        ins.slotdep = None
        ins.idx = self.n
        self.n += 1
        reads = [self.tok(t) if isinstance(t, str) else t for t in reads]
        writes = [self.tok(t) if isinstance(t, str) else t for t in writes]
        deps = {}
        for t in reads:
            if t.writer is not None:
                deps[t.writer.idx] = t.writer
            if t.name.startswith(("ps", "pb")):
                for r in t.readers.values():
                    if r.eng != eng:
                        deps[r.idx] = r
        for t in writes:
            if t.writer is not None:
                deps[t.writer.idx] = t.writer
            for r in t.readers.values():
                deps[r.idx] = r
        for d in extra_deps:
            deps[d.idx] = d
        if eng == "pe" and not dma:
            deps = {k: d for k, d in deps.items() if not (d.eng == "pe" and not d.dma)}
        deps.pop(ins.idx, None)
        ins.deps = list(deps.values())
        for d in ins.deps:
            d.signal = True
        if dma:
            pos = self.ringpos[eng]
            self.ringpos[eng] = (pos + 1) % self.RING
            ins.sem = self.rings[eng][pos]
            prev = self.ringval[eng][pos]
            ins.slotdep = (ins.sem, prev) if prev > 0 else None
            ins.semval = prev + 16
            self.ringval[eng][pos] = ins.semval
            self.pending_dma.append(ins)
        for t in reads:
            key = ("dma", ins.idx) if dma else eng
            t.readers[key] = ins
        for t in writes:
            t.writer = ins
            t.readers = {}
        self.streams[eng].append(ins)
        return ins

    def barrier(self):
        last = [self.streams[e][-1] for e in self.ENGS if self.streams[e] and not self.streams[e][-1].dma]
        last = []
        for e in self.ENGS:
            for ins in reversed(self.streams[e]):
                if not ins.dma and ins.fn is not None:
                    last.append(ins)
                    break
        deps = last + self.pending_dma
        for e in self.ENGS:
            self.add(e, None, extra_deps=deps)
        self.pending_dma = []
        for t in self.toks.values():
            t.writer = None
            t.readers = {}

    def emit(self):
        nc = self.nc
        for e in self.ENGS:
            c = 0
            for ins in self.streams[e]:
                if not ins.dma and ins.signal:
                    c += 1
                    ins.sigval = c
        prog = self

        def run(e, engobj):
            waited = {}
            for ins in prog.streams[e]:
                needs = {}
                for d in ins.deps:
                    if d.dma:
                        s, v = d.sem, d.semval
                    else:
                        s, v = prog.engsem[d.eng], d.sigval
                    k = id(s)
                    if needs.get(k, (None, 0))[1] < v:
                        needs[k] = (s, v)
                if ins.slotdep is not None:
                    s, v = ins.slotdep
                    k = id(s)
                    if needs.get(k, (None, 0))[1] < v:
                        needs[k] = (s, v)
                for k, (s, v) in needs.items():
                    if waited.get(k, 0) < v:
                        engobj.wait_ge(s, v)
                        waited[k] = v
                if ins.fn is None:
                    continue
                bi = ins.fn(engobj)
                if ins.dma:
                    bi.then_inc(ins.sem, 16)
                elif ins.signal:
                    bi.then_inc(prog.engsem[e], 1)

        with nc.Block() as block:
            @block.tensor
            def _(e):
                run("pe", e)

            @block.scalar
            def _(e):
                run("act", e)

            @block.vector
            def _(e):
                run("dve", e)

            @block.gpsimd
            def _(e):
                run("pool", e)

            @block.sync
            def _(e):
                run("sp", e)


def _vec_layout():
    cols = {}
    n = 0

    def put(name, k):
        nonlocal n
        cols[name] = n
        n += k
    for nm in ("b_gate0", "b_gate1", "b_gate2", "pool_scale", "mu_r", "mu_k", "mu_v", "w0", "a0", "k_k", "k_a",
               "r_k", "lnx_g", "lnx_b", "ln1_g", "ln1_b", "ln2_g", "ln2_b", "ln3_g", "ln3_b"):
        put(nm, 8)
    put("mu_xw", 1)
    put("mu_xa", 1)
    put("mu_xg", 2)
    put("subln_g", 1)
    put("lam", 256)
    return cols, n


VCOL, NV = _vec_layout()
CCOL = {}
_n = 0
for _nm, _k in (("ident", 128), ("mstrict", 64), ("mincl", 64), ("invfreq", 1), ("shc", 1), ("shs", 1),
                ("sgn", 1), ("invcnt", 64), ("blk64", 128), ("ln_in_g", 8), ("ln_in_b", 8)):
    CCOL[_nm] = _n
    _n += _k
NCONST = _n


def _host_consts(ln_in_g, ln_in_b):
    c = np.zeros((128, NCONST), np.float32)
    c[:, CCOL["ident"]:CCOL["ident"] + 128] = np.eye(128, dtype=np.float32)
    s = np.arange(128)[:, None] % 64
    t = np.arange(64)[None, :]
    c[:, CCOL["mstrict"]:CCOL["mstrict"] + 64] = (s < t)
    c[:, CCOL["mincl"]:CCOL["mincl"] + 64] = (s <= t)
    i = np.arange(128) % 32
    inv_freq = (1.0 / (10000.0 ** (np.arange(0, 64, 2, dtype=np.float32) / np.float32(64)))).astype(np.float32)
    c[:, CCOL["invfreq"]] = inv_freq[i]
    c[:, CCOL["shc"]] = 1.5 * math.pi
    c[:, CCOL["shs"]] = math.pi
    half = (np.arange(128) // 32) % 2
    c[:, CCOL["sgn"]] = np.where(half == 0, -1.0, 1.0)
    for g in range(4):
        w = 2 ** (g + 1)
        cnt = np.minimum(np.arange(16) + 1, w).astype(np.float32)
        c[:, CCOL["invcnt"] + g * 16: CCOL["invcnt"] + (g + 1) * 16] = (1.0 / cnt)[None, :]
    blk = np.zeros((128, 128), np.float32)
    blk[:64, :64] = 1.0 / 64
    blk[64:, 64:] = 1.0 / 64
    c[:, CCOL["blk64"]:CCOL["blk64"] + 128] = blk
    c[:, CCOL["ln_in_g"]:CCOL["ln_in_g"] + 8] = ln_in_g.reshape(8, 128).T
    c[:, CCOL["ln_in_b"]:CCOL["ln_in_b"] + 8] = ln_in_b.reshape(8, 128).T
    return c


def _host_vecs(inp):
    v = np.zeros((DEPTH, 128, NV), np.float32)

    def T8(a):
        return np.ascontiguousarray(a.reshape(8, 128).T)
    for l in range(DEPTH):
        def put(nm, arr):
            k = arr.shape[1]
            v[l, :, VCOL[nm]:VCOL[nm] + k] = arr
        for j in range(3):
            put("b_gate%d" % j, T8(inp["b_gate"][l, j]))
        put("pool_scale", T8(inp["pool_scale"][l]))
        mu = inp["rwkv_mu"][l]
        put("mu_r", T8(mu[0:1024]))
        put("mu_k", T8(mu[1024:2048]))
        put("mu_v", T8(mu[2048:3072]))
        tmp = np.zeros((128, 1), np.float32)
        tmp[:64, 0] = mu[3072:3136]
        put("mu_xw", tmp)
        tmp = np.zeros((128, 1), np.float32)
        tmp[:64, 0] = mu[3136:3200]
        put("mu_xa", tmp)
        tmp = np.zeros((128, 2), np.float32)
        tmp[:, 0] = mu[3200:3328]
        tmp[:32, 1] = mu[3328:3360]
        put("mu_xg", tmp)
        put("w0", T8(inp["rwkv_w0"][l]))
        put("a0", T8(inp["rwkv_a0"][l]))
        put("k_k", T8(inp["rwkv_k_k"][l]))
        put("k_a", T8(inp["rwkv_k_a"][l]))
        put("r_k", T8(inp["rwkv_r_k"][l].reshape(-1)))
        put("lnx_g", T8(inp["rwkv_lnx_g"][l]))
        put("lnx_b", T8(inp["rwkv_lnx_b"][l]))
        for nm in ("ln1_g", "ln1_b", "ln2_g", "ln2_b", "ln3_g", "ln3_b"):
            put(nm, T8(inp[nm][l]))
        put("subln_g", inp["attn_subln_g"][l].reshape(128, 1))
        lam = np.concatenate([inp["lam_q1"][l], inp["lam_k1"][l], inp["lam_q2"][l], inp["lam_k2"][l]])
        put("lam", np.broadcast_to(lam[None, :], (128, 256)))
    return v


WEIGHTS = (("w_in", [DEPTH, D, P_IN]), ("w_br_attn", [DEPTH, D, D]), ("pool_w", [DEPTH, 4, 256, 256]),
           ("w_br_pool", [DEPTH, D, D]), ("rwkv_w2", [DEPTH, 64, D]), ("rwkv_a2", [DEPTH, 64, D]),
           ("rwkv_g2", [DEPTH, 160, D]), ("w_br_rwkv", [DEPTH, D, D]), ("w_out", [DEPTH, D, D]),
           ("w_xq", [DEPTH, D, D]), ("w_xkv", [DEPTH, D, 2 * D]), ("w_xo", [DEPTH, D, D]),
           ("w_ff1", [DEPTH, D, 4 * D]), ("w_ff2", [DEPTH, 4 * D, D]))


from contextlib import ExitStack


class K:
    def __init__(self, dbg=False, stop=None):
        nc = bass.Bass("TRN2", target_bir_lowering=False)
        self.nc = nc
        self.p = Prog(nc)
        self.stop = stop
        p = self.p
        din = lambda n, s, d=F32: nc.dram_tensor(n, s, d, kind="ExternalInput").ap()
        self.x = din("x", [S, D])
        self.mem = din("mem", [256, D])
        self.pos = din("pos", [128, S], I32)
        self.consts = din("consts", [128, NCONST])
        self.vecs = din("vecs", [DEPTH, 128, NV])
        self.w = {n: din(n, s) for n, s in WEIGHTS}
        self.out = nc.dram_tensor("out", [S, D], F32, kind="ExternalOutput").ap()
        kind = "ExternalOutput" if dbg else "Internal"
        self.h32d = nc.dram_tensor("h32d", [D, S], F32, kind=kind).ap()
        self.otd = nc.dram_tensor("otd", [D, S], BF16, kind=kind).ap()
        self.ptd = nc.dram_tensor("ptd", [D, S], BF16, kind=kind).ap()
        self.rtd = nc.dram_tensor("rtd", [D, S], BF16, kind=kind).ap()
        self.gd = nc.dram_tensor("gd", [3 * D, S], BF16, kind=kind).ap()
        self.mgd = nc.dram_tensor("mgd", [D, S], BF16, kind=kind).ap()
        ps0 = nc.alloc_psum_tensor("ps0", [128, 512], F32)
        self.psB = nc.alloc_psum_tensor("psB", [128, 3072], F32)
        ps7 = nc.alloc_psum_tensor("ps7", [128, 512], F32)
        self.ps = [ps0] + [self.psB[:, i * 512:(i + 1) * 512] for i in range(6)] + [ps7]
        sb = nc.alloc_sbuf_tensor
        self.cst = sb("cst", [128, NCONST], F32)
        self.vec = sb("vec", [128, DEPTH, NV], F32)
        self.hbT = sb("hbT", [128, 8, S], BF16)
        self.ropeC = sb("ropeC", [128, S], F32)
        self.ropeS = sb("ropeS", [128, S], F32)
        self.ident_bf = sb("ident_bf", [128, 128], BF16)
        self.ones_d = sb("ones_d", [128, 128], BF16)
        self.ones_dv = sb("ones_dv", [128, 128], BF16)
        self.ones_k = sb("ones_k", [128, 128], BF16)
        self.blk64 = sb("blk64", [128, 128], BF16)
        self.epsc = sb("epsc", [128, 4], F32)
        self.stack = None

    def cc(self, name, k=1, off=0):
        c = CCOL[name] + off
        return self.cst[:, c:c + k]

    def vc(self, l, name, k=1, off=0):
        c = VCOL[name] + off
        return self.vec[:, l, c:c + k]

    def sb(self, name, shape, dtype):
        self.uid = getattr(self, "uid", 0) + 1
        return self.stack.enter_context(self.nc.sbuf_tensor("%s_u%d" % (name, self.uid), shape, dtype))

    def begin(self):
        self.stack = ExitStack()

    def end(self):
        self.p.barrier()
        self.stack.close()
        self.stack = None

    def A(self, eng, fn, reads=(), writes=(), dma=False):
        return self.p.add(eng, fn, reads, writes, dma)

    def setup(self):
        A = self.A
        cst, vec = self.cst, self.vec
        A("sp", lambda e: e.dma_start(out=cst[:], in_=self.consts), writes=["cst"], dma=True)
        A("sp", lambda e: e.dma_start(out=vec[:], in_=self.vecs.rearrange("l p n -> p l n")), writes=["vec"], dma=True)
        A("dve", lambda e: e.tensor_copy(out=self.ident_bf[:], in_=self.cc("ident", 128)), reads=["cst"], writes=["ident_bf"])
        A("dve", lambda e: e.memset(self.ones_d[:], 1.0 / 1024), writes=["ones_d"])
        A("dve", lambda e: e.memset(self.ones_dv[:], 1.0 / 128), writes=["ones_dv"])
        A("dve", lambda e: e.memset(self.ones_k[:], 1.0), writes=["ones_k"])
        A("dve", lambda e: e.tensor_copy(out=self.blk64[:], in_=self.cc("blk64", 128)), reads=["cst"], writes=["blk64"])
        A("dve", lambda e: e.memset(self.epsc[:, 0:1], LN_EPS), writes=["epsc"])
        A("dve", lambda e: e.memset(self.epsc[:, 1:2], RMS_EPS), writes=["epsc"])
        A("dve", lambda e: e.memset(self.epsc[:, 2:3], GN_EPS), writes=["epsc"])
        A("dve", lambda e: e.memset(self.epsc[:, 3:4], 0.0), writes=["epsc"])
        self.begin()
        posi = self.sb("posi", [128, S], I32)
        ang = self.sb("ang", [128, S], F32)
        tmp = self.sb("rtmp", [128, S], F32)
        A("sp", lambda e: e.dma_start(out=posi[:], in_=self.pos), writes=["posi"], dma=True)
        A("dve", lambda e: e.tensor_copy(out=ang[:], in_=posi[:]), reads=["posi"], writes=["ang"])
        A("dve", lambda e: e.tensor_scalar(out=ang[:], in0=ang[:], scalar1=self.cc("invfreq"), scalar2=None, op0=ALU.mult),
          reads=["ang", "cst"], writes=["ang"])
        two_pi = 2.0 * math.pi
        ki = self.sb("rki", [128, S], I32)
        kf = self.sb("rkf", [128, S], F32)
        for (sh, dst, dn) in ((0.5 * math.pi, self.ropeC, "ropeC"), (0.0, self.ropeS, "ropeS")):
            A("dve", lambda e, sh=sh: e.tensor_scalar(out=tmp[:], in0=ang[:], scalar1=sh, scalar2=None, op0=ALU.add), reads=["ang"], writes=["rtmp"])
            A("dve", lambda e: e.tensor_scalar(out=ki[:], in0=tmp[:], scalar1=1.0 / two_pi, scalar2=None, op0=ALU.mult), reads=["rtmp"], writes=["rki"])
            A("dve", lambda e: e.tensor_copy(out=kf[:], in_=ki[:]), reads=["rki"], writes=["rkf"])
            A("dve", lambda e: e.scalar_tensor_tensor(out=tmp[:], in0=kf[:], scalar=-two_pi, in1=tmp[:], op0=ALU.mult, op1=ALU.add), reads=["rkf", "rtmp"], writes=["rtmp"])
            A("dve", lambda e: e.tensor_scalar(out=kf[:], in0=tmp[:], scalar1=math.pi, scalar2=-two_pi, op0=ALU.is_gt, op1=ALU.mult), reads=["rtmp"], writes=["rkf"])
            A("dve", lambda e: e.tensor_tensor(out=tmp[:], in0=tmp[:], in1=kf[:], op=ALU.add), reads=["rtmp", "rkf"], writes=["rtmp"])
            A("dve", lambda e: e.tensor_scalar(out=kf[:], in0=tmp[:], scalar1=-math.pi, scalar2=two_pi, op0=ALU.is_lt, op1=ALU.mult), reads=["rtmp"], writes=["rkf"])
            A("dve", lambda e: e.tensor_tensor(out=tmp[:], in0=tmp[:], in1=kf[:], op=ALU.add), reads=["rtmp", "rkf"], writes=["rtmp"])
            A("dve", lambda e: e.tensor_scalar(out=tmp[:], in0=tmp[:], scalar1=math.pi, scalar2=-math.pi, op0=ALU.min, op1=ALU.max), reads=["rtmp"], writes=["rtmp"])
            A("act", lambda e, dst=dst: e.activation(out=dst[:], in_=tmp[:], func=AF.Sin), reads=["rtmp"], writes=[dn])
        A("dve", lambda e: e.tensor_scalar(out=self.ropeS[:], in0=self.ropeS[:], scalar1=self.cc("sgn"), scalar2=None, op0=ALU.mult),
          reads=["ropeS", "cst"], writes=["ropeS"])
        self.end()

    def ln_T(self, zT, zname, gcol, bcol, tb, tmps, final=False):
        A = self.A
        zb, mean_sb, rstd_sb = tmps
        ps = self.ps
        A("act", lambda e: e.activation(out=zb[:], in_=zT[:], func=AF.Copy), reads=[zname], writes=["ln_zb"])
        for c in range(8):
            A("pe", lambda e, c=c: e.matmul(ps[6][:], lhsT=self.ones_d[:], rhs=zb[:, c, :], start=(c == 0), stop=(c == 7)),
              reads=["ln_zb", "ones_d"], writes=["ps6"])
        A("act", lambda e: e.activation(out=mean_sb[:], in_=ps[6][:], func=AF.Copy), reads=["ps6"], writes=["ln_mean"])
        A("dve", lambda e: e.tensor_tensor(out=zT[:], in0=zT[:], in1=mean_sb[:, None, :].to_broadcast([128, 8, TB]), op=ALU.subtract),
          reads=[zname, "ln_mean"], writes=[zname])
        A("act", lambda e: e.activation(out=zb[:], in_=zT[:], func=AF.Square), reads=[zname], writes=["ln_zb"])
        for c in range(8):
            A("pe", lambda e, c=c: e.matmul(ps[7][:], lhsT=self.ones_d[:], rhs=zb[:, c, :], start=(c == 0), stop=(c == 7)),
              reads=["ln_zb", "ones_d"], writes=["ps7"])
        A("act", lambda e: e.activation(out=rstd_sb[:], in_=ps[7][:], func=AF.Sqrt, bias=self.epsc[:, 0:1]),
          reads=["ps7", "epsc"], writes=["ln_rstd"])
        A("dve", lambda e: e.reciprocal(out=rstd_sb[:], in_=rstd_sb[:]), reads=["ln_rstd"], writes=["ln_rstd"])
        A("dve", lambda e: e.tensor_tensor(out=zT[:], in0=zT[:], in1=rstd_sb[:, None, :].to_broadcast([128, 8, TB]), op=ALU.mult),
          reads=[zname, "ln_rstd"], writes=[zname])
        for c in range(8):
            A("act", lambda e, c=c: e.activation(out=zT[:, c, :], in_=zT[:, c, :], func=AF.Identity,
                                                  scale=gcol[:, c:c + 1], bias=bcol[:, c:c + 1]),
              reads=[zname, "vec", "cst"], writes=[zname])
        if final:
            ident = self.cc("ident", 128)
            for tt in range(4):
                on_ = self.outN[tt % 2]
                onn = "outN%d" % (tt % 2)
                for cg in range(2):
                    bank = 4 + cg
                    for c4 in range(4):
                        c = cg * 4 + c4
                        A("pe", lambda e, c=c, c4=c4, tt=tt, bank=bank: e.transpose(ps[bank][:, c4 * 128:(c4 + 1) * 128], zT[:, c, tt * 128:(tt + 1) * 128], ident),
                          reads=[zname, "cst"], writes=["ps%d" % bank])
                    if cg == 0:
                        A("act", lambda e, on_=on_, bank=bank: e.activation(out=on_[:, 0:512], in_=ps[bank][:], func=AF.Copy), reads=["ps%d" % bank], writes=[onn])
                    else:
                        A("dve", lambda e, on_=on_, bank=bank: e.tensor_copy(out=on_[:, 512:1024], in_=ps[bank][:]), reads=["ps%d" % bank], writes=[onn])
                A("sp", lambda e, on_=on_, tt=tt: e.dma_start(out=self.out[tb * TB + tt * 128:tb * TB + (tt + 1) * 128, :], in_=on_[:]), reads=[onn], writes=["out%d" % tt], dma=True)
            return
        A("dve", lambda e: e.tensor_copy(out=self.hbT[:, :, tb * TB:(tb + 1) * TB], in_=zT[:]), reads=[zname], writes=["hbT"])
        A("sp", lambda e: e.dma_start(out=self.h32d.rearrange("(c p) t -> p c t", p=128)[:, :, tb * TB:(tb + 1) * TB], in_=zT[:]),
          reads=[zname], writes=["h32d"], dma=True)

    def ln_tmps(self):
        return (self.sb("ln_zb", [128, 8, TB], BF16), self.sb("ln_mean", [128, TB], F32), self.sb("ln_rstd", [128, TB], F32))

    def phase_ln_in(self):
        A = self.A
        ps = self.ps
        self.begin()
        tmps = self.ln_tmps()
        xt = [self.sb("xt%d" % i, [128, 4, D], F32) for i in range(2)]
        zT = [self.sb("zT%d" % i, [128, 8, TB], F32) for i in range(2)]
        ident = self.cc("ident", 128)
        for tb in range(NB):
            xb = xt[tb % 2]
            xn = "xt%d" % (tb % 2)
            z = zT[tb % 2]
            zn = "zT%d" % (tb % 2)
            A("sp", lambda e, xb=xb, tb=tb: e.dma_start(out=xb[:], in_=self.x[tb * TB:(tb + 1) * TB, :].rearrange("(t p) d -> p t d", p=128)),
              writes=[xn], dma=True)
            for c in range(8):
                bank = c % 2
                for tt in range(4):
                    A("pe", lambda e, xb=xb, c=c, tt=tt, bank=bank: e.transpose(ps[bank][:, tt * 128:(tt + 1) * 128], xb[:, tt, c * 128:(c + 1) * 128], ident),
                      reads=[xn, "cst"], writes=["ps%d" % bank])
                if c % 2 == 0:
                    A("act", lambda e, z=z, c=c, bank=bank: e.activation(out=z[:, c, :], in_=ps[bank][:], func=AF.Copy), reads=["ps%d" % bank], writes=[zn])
                else:
                    A("dve", lambda e, z=z, c=c, bank=bank: e.tensor_copy(out=z[:, c, :], in_=ps[bank][:]), reads=["ps%d" % bank], writes=[zn])
            self.ln_T(z, zn, self.cc("ln_in_g", 8), self.cc("ln_in_b", 8), tb, tmps)
        self.end()

    def load_w(self, dst, dname, src_rows_cols):
        self.A("pool", lambda e: e.dma_start(out=dst, in_=src_rows_cols.rearrange("(c p) n -> p c n", p=128)), writes=[dname], dma=True)

    def phase_attn(self, l):
        A = self.A
        ps = self.ps
        w_in = self.w["w_in"][l]
        self.begin()
        NWB = 2
        wq = [self.sb("wq%d" % i, [128, 8, 128], BF16) for i in range(NWB)]
        wqs = [self.sb("wqs%d" % i, [128, 8, 128], BF16) for i in range(NWB)]
        wk = [self.sb("wk%d" % i, [128, 8, 128], BF16) for i in range(NWB)]
        wks = [self.sb("wks%d" % i, [128, 8, 128], BF16) for i in range(NWB)]
        wv = [self.sb("wv%d" % i, [128, 8, 128], BF16) for i in range(NWB)]
        qT = [self.sb("qT%d" % i, [128, S], BF16) for i in range(2)]
        kT = [self.sb("kT%d" % i, [128, S], BF16) for i in range(2)]
        vS = [self.sb("vS%d" % i, [128, 16, 128], BF16) for i in range(2)]
        t1 = self.sb("rp_t1", [128, TB], F32)
        t2 = self.sb("rp_t2", [128, TB], F32)
        NE = 6
        eT = [self.sb("eT%d" % i, [128, TB], BF16) for i in range(NE)]
        r1 = self.sb("at_r1", [128, TB], F32)
        o1n = self.sb("at_o1n", [128, TB], F32)
        o2n = self.sb("at_o2n", [128, TB], F32)
        osq = self.sb("at_osq", [128, TB], BF16)
        rs = self.sb("at_rs", [128, TB], F32)
        ob = [self.sb("at_ob%d" % i, [128, TB], BF16) for i in range(2)]
        lamt = self.sb("lamt", [128, 64], F32)
        lamv = self.sb("lamv", [128, 4], F32)
        gsc = self.sb("gsc", [128, 1], F32)
        lam_init = 0.8 - 0.6 * math.exp(-0.3 * l)
        A("dve", lambda e: e.tensor_tensor(out=lamt[:], in0=self.vc(l, "lam", 64, 0), in1=self.vc(l, "lam", 64, 64), op=ALU.mult), reads=["vec"], writes=["lamt"])
        A("dve", lambda e: e.reduce_sum(out=lamv[:, 0:1], in_=lamt[:], axis=AX.X), reads=["lamt"], writes=["lamv"])
        A("dve", lambda e: e.tensor_tensor(out=lamt[:], in0=self.vc(l, "lam", 64, 128), in1=self.vc(l, "lam", 64, 192), op=ALU.mult), reads=["vec", "lamv"], writes=["lamt"])
        A("dve", lambda e: e.reduce_sum(out=lamv[:, 1:2], in_=lamt[:], axis=AX.X), reads=["lamt"], writes=["lamv"])
        A("act", lambda e: e.activation(out=lamv[:, 0:2], in_=lamv[:, 0:2], func=AF.Exp), reads=["lamv"], writes=["lamv"])
        A("dve", lambda e: e.tensor_tensor(out=lamv[:, 2:3], in0=lamv[:, 1:2], in1=lamv[:, 0:1], op=ALU.subtract), reads=["lamv"], writes=["lamv"])
        A("dve", lambda e: e.tensor_scalar(out=lamv[:, 3:4], in0=lamv[:, 2:3], scalar1=-lam_init, scalar2=None, op0=ALU.add), reads=["lamv"], writes=["lamv"])
        A("dve", lambda e: e.tensor_scalar(out=gsc[:], in0=self.vc(l, "subln_g"), scalar1=(1.0 - lam_init), scalar2=None, op0=ALU.mult), reads=["vec"], writes=["gsc"])
        neglam = lamv[:, 3:4]

        def swapped_load(dst, dname, base):
            for mm_ in range(2):
                for half in range(2):
                    so = base + mm_ * 64 + (1 - half) * 32
                    do = mm_ * 64 + half * 32
                    src = w_in[:, so:so + 32].rearrange("(c p) i -> p c i", p=128)
                    d = dst[:, :, do:do + 32]
                    A("pool", lambda e, d=d, src=src: e.dma_start(out=d, in_=src), writes=[dname], dma=True)

        def load_head(h):
            wb = h % NWB
            self.load_w(wq[wb][:], "wq%d" % wb, w_in[:, h * 128:(h + 1) * 128])
            swapped_load(wqs[wb], "wqs%d" % wb, h * 128)
            self.load_w(wk[wb][:], "wk%d" % wb, w_in[:, 1024 + h * 128:1024 + (h + 1) * 128])
            swapped_load(wks[wb], "wks%d" % wb, 1024 + h * 128)
            self.load_w(wv[wb][:], "wv%d" % wb, w_in[:, 2048 + h * 128:2048 + (h + 1) * 128])

        load_head(0)
        load_head(1)
        for h in range(8):
            wb = h % NWB
            hb = h % 2
            for (wa, wan, ws, wsn, dst, dn) in ((wq[wb], "wq%d" % wb, wqs[wb], "wqs%d" % wb, qT[hb], "qT%d" % hb),
                                                (wk[wb], "wk%d" % wb, wks[wb], "wks%d" % wb, kT[hb], "kT%d" % hb)):
                for tb in range(NB):
                    tsl = slice(tb * TB, (tb + 1) * TB)
                    for c in range(8):
                        A("pe", lambda e, wa=wa, c=c, tsl=tsl: e.matmul(ps[0][:], lhsT=wa[:, c, :], rhs=self.hbT[:, c, tsl], start=(c == 0), stop=(c == 7)),
                          reads=[wan, "hbT"], writes=["ps0"])
                    for c in range(8):
                        A("pe", lambda e, ws=ws, c=c, tsl=tsl: e.matmul(ps[1][:], lhsT=ws[:, c, :], rhs=self.hbT[:, c, tsl], start=(c == 0), stop=(c == 7)),
                          reads=[wsn, "hbT"], writes=["ps1"])
                    A("dve", lambda e, tsl=tsl: e.tensor_tensor(out=t1[:], in0=ps[0][:], in1=self.ropeC[:, tsl], op=ALU.mult), reads=["ps0", "ropeC"], writes=["rp_t1"])
                    A("dve", lambda e, tsl=tsl: e.tensor_tensor(out=t2[:], in0=ps[1][:], in1=self.ropeS[:, tsl], op=ALU.mult), reads=["ps1", "ropeS"], writes=["rp_t2"])
                    A("pool", lambda e, dst=dst, tsl=tsl: e.tensor_tensor(out=dst[:, tsl], in0=t1[:], in1=t2[:], op=ALU.add), reads=["rp_t1", "rp_t2"], writes=[dn])
            for g4 in range(4):
                bank = g4 % 2
                for t4 in range(4):
                    tt = g4 * 4 + t4
                    for c in range(8):
                        A("pe", lambda e, c=c, tt=tt, t4=t4, bank=bank, wb=wb: e.matmul(ps[bank][:, t4 * 128:(t4 + 1) * 128], lhsT=self.hbT[:, c, tt * 128:(tt + 1) * 128],
                                                                                 rhs=wv[wb][:, c, :], start=(c == 0), stop=(c == 7)),
                          reads=["wv%d" % wb, "hbT"], writes=["ps%d" % bank])
                A("act", lambda e, g4=g4, bank=bank, hb=hb: e.activation(out=vS[hb][:, g4 * 4:(g4 + 1) * 4, :], in_=ps[bank][:].rearrange("p (a b) -> p a b", a=4), func=AF.Copy),
                  reads=["ps%d" % bank], writes=["vS%d" % hb])
            if h + 2 < 8:
                load_head(h + 2)
            LOOK = 3
            sbanks = (2, 3, 0, 1)
            ecnt = 0
            scnt = 0
            for Q in range(NB):
                qsl0 = Q * TB
                its = []
                for m in range(2):
                    nk = 4 * Q + 4
                    for kt in range(nk):
                        its.append((m, kt, nk))
                pend = []
                for i in range(len(its) + LOOK):
                    if i < len(its):
                        m, kt, nk = its[i]
                        msl = slice(64 * m, 64 * m + 64)
                        qoff = max(0, kt - 4 * Q) * 128
                        diag = kt >= 4 * Q
                        sb_ = sbanks[scnt % 4]
                        scnt += 1
                        et = eT[ecnt % NE]
                        en = "eT%d" % (ecnt % NE)
                        ecnt += 1
                        A("pe", lambda e, sb_=sb_, msl=msl, kt=kt, qoff=qoff, qsl0=qsl0, hb=hb: e.matmul(
                            ps[sb_][:, qoff:TB], lhsT=kT[hb][msl, kt * 128:(kt + 1) * 128], rhs=qT[hb][msl, qsl0 + qoff:qsl0 + TB], start=True, stop=True),
                          reads=["kT%d" % hb, "qT%d" % hb], writes=["ps%d" % sb_])
                        A("act", lambda e, sb_=sb_, et=et, qoff=qoff: e.activation(out=et[:, qoff:TB], in_=ps[sb_][:, qoff:TB], func=AF.Exp, scale=0.125),
                          reads=["ps%d" % sb_], writes=[en])
                        if diag:
                            A("pool", lambda e, et=et, qoff=qoff: e.memset(et[64:128, qoff:qoff + 64], 0.0), reads=[en], writes=[en])
                        pend.append((m, kt, nk, qoff, et, en))
                    if i >= LOOK:
                        m, kt, nk, qoff, et, en = pend[i - LOOK]
                        accO = ps[4 + 2 * m]
                        accD = ps[5 + 2 * m]
                        aon = "ps%d" % (4 + 2 * m)
                        adn = "ps%d" % (5 + 2 * m)
                        A("pe", lambda e, accO=accO, et=et, kt=kt, qoff=qoff, nk=nk, hb=hb: e.matmul(
                            accO[:, qoff:TB], lhsT=vS[hb][:, kt, :], rhs=et[:, qoff:TB], start=(kt == 0), stop=(kt == nk - 1)),
                          reads=["vS%d" % hb, en], writes=[aon])
                        A("pe", lambda e, accD=accD, et=et, kt=kt, qoff=qoff, nk=nk: e.matmul(
                            accD[:, qoff:TB], lhsT=self.ones_k[:], rhs=et[:, qoff:TB], start=(kt == 0), stop=(kt == nk - 1)),
                          reads=["ones_k", en], writes=[adn])
                A("dve", lambda e: e.reciprocal(out=r1[:], in_=ps[5][:]), reads=["ps5"], writes=["at_r1"])
                A("dve", lambda e: e.tensor_tensor(out=o1n[:], in0=ps[4][:], in1=r1[:], op=ALU.mult), reads=["ps4", "at_r1"], writes=["at_o1n"])
                A("dve", lambda e: e.reciprocal(out=r1[:], in_=ps[7][:]), reads=["ps7", "at_o1n"], writes=["at_r1"])
                A("dve", lambda e: e.tensor_tensor(out=o2n[:], in0=ps[6][:], in1=r1[:], op=ALU.mult), reads=["ps6", "at_r1"], writes=["at_o2n"])
                A("dve", lambda e: e.scalar_tensor_tensor(out=o1n[:], in0=o2n[:], scalar=neglam, in1=o1n[:], op0=ALU.mult, op1=ALU.add),
                  reads=["at_o2n", "at_o1n", "lamv"], writes=["at_o1n"])
                A("act", lambda e: e.activation(out=osq[:], in_=o1n[:], func=AF.Square), reads=["at_o1n"], writes=["at_osq"])
                A("pe", lambda e: e.matmul(ps[2][:], lhsT=self.ones_dv[:], rhs=osq[:], start=True, stop=True), reads=["ones_dv", "at_osq"], writes=["ps2"])
                A("act", lambda e: e.activation(out=rs[:], in_=ps[2][:], func=AF.Sqrt, bias=self.epsc[:, 1:2]), reads=["ps2", "epsc"], writes=["at_rs"])
                A("dve", lambda e: e.reciprocal(out=rs[:], in_=rs[:]), reads=["at_rs"], writes=["at_rs"])
                A("dve", lambda e: e.tensor_tensor(out=o1n[:], in0=o1n[:], in1=rs[:], op=ALU.mult), reads=["at_o1n", "at_rs"], writes=["at_o1n"])
                obq = ob[Q % 2]
                obn = "at_ob%d" % (Q % 2)
                A("act", lambda e, obq=obq: e.activation(out=obq[:], in_=o1n[:], func=AF.Identity, scale=gsc[:, 0:1], bias=self.epsc[:, 3:4]),
                  reads=["at_o1n", "gsc", "epsc"], writes=[obn])
                A("sp", lambda e, obq=obq, h=h, Q=Q: e.dma_start(out=self.otd[h * 128:(h + 1) * 128, Q * TB:(Q + 1) * TB], in_=obq[:]),
                  reads=[obn], writes=["otd"], dma=True)
        self.end()

    def phase_gates(self, l):
        A = self.A
        ps = self.ps
        w_in = self.w["w_in"][l]
        self.begin()
        wg = [self.sb("wg%d" % i, [128, 8, 128], BF16) for i in range(3)]
        gb = [self.sb("gb%d" % i, [128, TB], BF16) for i in range(3)]
        idx = 0
        k = 0
        for j in range(3):
            for m in range(8):
                wb = idx % 3
                col = 7456 + j * 1024 + m * 128
                self.load_w(wg[wb][:], "wg%d" % wb, w_in[:, col:col + 128])
                for tb in range(NB):
                    bank = k % 2
                    g = gb[k % 3]
                    gn = "gb%d" % (k % 3)
                    k += 1
                    for c in range(8):
                        A("pe", lambda e, wb=wb, c=c, tb=tb, bank=bank: e.matmul(ps[bank][:], lhsT=wg[wb][:, c, :], rhs=self.hbT[:, c, tb * TB:(tb + 1) * TB],
                                                                               start=(c == 0), stop=(c == 7)), reads=["wg%d" % wb, "hbT"], writes=["ps%d" % bank])
                    A("act", lambda e, g=g, bank=bank, j=j, m=m: e.activation(out=g[:], in_=ps[bank][:], func=AF.Sigmoid, bias=self.vc(l, "b_gate%d" % j, 1, m)),
                      reads=["ps%d" % bank, "vec"], writes=[gn])
                    A("sp", lambda e, g=g, j=j, m=m, tb=tb: e.dma_start(out=self.gd[j * 1024 + m * 128:j * 1024 + (m + 1) * 128, tb * TB:(tb + 1) * TB], in_=g[:]),
                      reads=[gn], writes=["gd%d" % (k % 4)], dma=True)
                idx += 1
        self.end()

    def phase_pool(self, l):
        A = self.A
        ps = self.ps
        w_in = self.w["w_in"][l]
        self.begin()
        PW = 16
        wp = [self.sb("wp%d" % i, [128, 8, 128], BF16) for i in range(2)]
        pw = [self.sb("pw%d" % i, [128, 2, 256], BF16) for i in range(2)]
        Pb = self.sb("pl_P", [128, PW + S], F32)
        Xb = self.sb("pl_X", [128, PW + S], F32)
        Yb = self.sb("pl_Y", [128, PW + S], F32)
        t16 = self.sb("pl_t16", [128, 16], F32)
        dT = [self.sb("pl_d%d" % i, [128, 2, S], BF16) for i in range(2)]
        yb = [self.sb("pl_y%d" % i, [128, TB], BF16) for i in range(2)]
        for (b_, n_) in ((Pb, "pl_P"), (Xb, "pl_X"), (Yb, "pl_Y")):
            A("pool", lambda e, b_=b_: e.memset(b_[:, 0:PW], 0.0), writes=[n_])
        k = 0
        for g in range(4):
            gp = g % 2
            A("pool", lambda e, g=g, gp=gp: e.dma_start(out=pw[gp][:], in_=self.w["pool_w"][l, g].rearrange("(kc p) e -> p kc e", p=128)), writes=["pw%d" % gp], dma=True)
            win = 2 ** (g + 1)
            for cc in range(2):
                c = 2 * g + cc
                wb = c % 2
                self.load_w(wp[wb][:], "wp%d" % wb, w_in[:, 3072 + c * 128:3072 + (c + 1) * 128])
                for tb in range(NB):
                    bank = tb % 2
                    for kc in range(8):
                        A("pe", lambda e, wb=wb, kc=kc, tb=tb, bank=bank: e.matmul(ps[bank][:], lhsT=wp[wb][:, kc, :], rhs=self.hbT[:, kc, tb * TB:(tb + 1) * TB],
                                                                                 start=(kc == 0), stop=(kc == 7)), reads=["wp%d" % wb, "hbT"], writes=["ps%d" % bank])
                    A("act", lambda e, tb=tb, bank=bank: e.activation(out=Pb[:, PW + tb * TB:PW + (tb + 1) * TB], in_=ps[bank][:], func=AF.Copy),
                      reads=["ps%d" % bank], writes=["pl_P"])
                src, sn = Pb, "pl_P"
                sh = 1
                bufs = [(Xb, "pl_X"), (Yb, "pl_Y")]
                bi = 0
                while sh < win:
                    dst, dn = bufs[bi % 2]
                    bi += 1
                    A("dve", lambda e, src=src, dst=dst, sh=sh: e.tensor_tensor(out=dst[:, PW:PW + S], in0=src[:, PW:PW + S], in1=src[:, PW - sh:PW - sh + S], op=ALU.add),
                      reads=[sn], writes=[dn])
                    src, sn = dst, dn
                    sh *= 2
                A("dve", lambda e, src=src, gp=gp, cc=cc, win=win: e.scalar_tensor_tensor(out=dT[gp][:, cc, :], in0=src[:, PW:PW + S], scalar=1.0 / win, in1=Pb[:, PW:PW + S],
                                                                                   op0=ALU.mult, op1=ALU.subtract), reads=[sn, "pl_P"], writes=["pl_d%d" % gp])
                A("dve", lambda e, src=src, g=g: e.tensor_tensor(out=t16[:], in0=src[:, PW:PW + 16], in1=self.cc("invcnt", 16, g * 16), op=ALU.mult), reads=[sn, "cst"], writes=["pl_t16"])
                A("dve", lambda e, gp=gp, cc=cc: e.tensor_tensor(out=dT[gp][:, cc, 0:16], in0=t16[:], in1=Pb[:, PW:PW + 16], op=ALU.subtract), reads=["pl_t16", "pl_P"], writes=["pl_d%d" % gp])
            for ec in range(2):
                for tb in range(NB):
                    bank = 2 + (k % 2)
                    y = yb[k % 2]
                    yn = "pl_y%d" % (k % 2)
                    k += 1
                    for kc in range(2):
                        A("pe", lambda e, gp=gp, kc=kc, ec=ec, tb=tb, bank=bank: e.matmul(ps[bank][:], lhsT=pw[gp][:, kc, ec * 128:(ec + 1) * 128], rhs=dT[gp][:, kc, tb * TB:(tb + 1) * TB],
                                                                                      start=(kc == 0), stop=(kc == 1)), reads=["pw%d" % gp, "pl_d%d" % gp], writes=["ps%d" % bank])
                    A("act", lambda e, y=y, bank=bank, g=g, ec=ec: e.activation(out=y[:], in_=ps[bank][:], func=AF.Identity, scale=self.vc(l, "pool_scale", 1, 2 * g + ec), bias=self.epsc[:, 3:4]),
                      reads=["ps%d" % bank, "vec", "epsc"], writes=[yn])
                    A("sp", lambda e, y=y, g=g, ec=ec, tb=tb: e.dma_start(out=self.ptd[(2 * g + ec) * 128:(2 * g + ec + 1) * 128, tb * TB:(tb + 1) * TB], in_=y[:]),
                      reads=[yn], writes=["ptd%d" % (k % 4)], dma=True)
        self.end()

    def phase_rwkv(self, l):
        A = self.A
        ps = self.ps
        w_in = self.w["w_in"][l]
        self.begin()
        lxw = self.sb("lxw", [128, S], BF16)
        lxa = self.sb("lxa", [128, S], BF16)
        lxg = self.sb("lxg", [128, 2, S], BF16)
        w2s = self.sb("w2s", [128, D], BF16)
        a2s = self.sb("a2s", [128, D], BF16)
        g2s = self.sb("g2s", [128, 2, D], BF16)
        A("pool", lambda e: e.dma_start(out=w2s[0:64, :], in_=self.w["rwkv_w2"][l]), writes=["w2s"], dma=True)
        A("pool", lambda e: e.dma_start(out=a2s[0:64, :], in_=self.w["rwkv_a2"][l]), writes=["a2s"], dma=True)
        A("pool", lambda e: e.dma_start(out=g2s[:, 0, :], in_=self.w["rwkv_g2"][l][0:128, :]), writes=["g2s"], dma=True)
        A("pool", lambda e: e.dma_start(out=g2s[0:32, 1, :], in_=self.w["rwkv_g2"][l][128:160, :]), writes=["g2s"], dma=True)
        outer = self.stack
        self.stack = ExitStack()
        wl = self.sb("wl", [128, 8, 288], BF16)
        ushl = self.sb("ushl", [128, 1 + S], F32)
        tsl_ = self.sb("tsl", [128, S], F32)
        self.load_w(wl[:], "wl", w_in[:, 7168:7456])
        A("dve", lambda e: e.memset(ushl[:, 0:1], 0.0), writes=["ushl"])
        specs = ((0, 64, self.vc(l, "mu_xw"), AF.Tanh, lxw, "lxw", None), (64, 64, self.vc(l, "mu_xa"), AF.Copy, lxa, "lxa", None),
                 (128, 128, self.vc(l, "mu_xg", 1, 0), AF.Sigmoid, lxg, "lxg", 0), (256, 32, self.vc(l, "mu_xg", 1, 1), AF.Sigmoid, lxg, "lxg", 1))
        for (c0, nco, mu, fn, dst, dn, pl) in specs:
            for tb in range(NB):
                bank = tb % 2
                for c in range(8):
                    A("pe", lambda e, c=c, c0=c0, nco=nco, tb=tb, bank=bank: e.matmul(ps[bank][0:nco, :], lhsT=wl[:, c, c0:c0 + nco], rhs=self.hbT[:, c, tb * TB:(tb + 1) * TB],
                                                                                  start=(c == 0), stop=(c == 7)), reads=["wl", "hbT"], writes=["ps%d" % bank])
                A("act", lambda e, nco=nco, tb=tb, bank=bank: e.activation(out=ushl[0:nco, 1 + tb * TB:1 + (tb + 1) * TB], in_=ps[bank][0:nco, :], func=AF.Copy), reads=["ps%d" % bank], writes=["ushl"])
            A("dve", lambda e, nco=nco: e.tensor_tensor(out=tsl_[0:nco, :], in0=ushl[0:nco, 0:S], in1=ushl[0:nco, 1:S + 1], op=ALU.subtract), reads=["ushl"], writes=["tsl"])
            A("dve", lambda e, nco=nco, mu=mu: e.scalar_tensor_tensor(out=tsl_[0:nco, :], in0=tsl_[0:nco, :], scalar=mu[0:nco, :], in1=ushl[0:nco, 1:S + 1], op0=ALU.mult, op1=ALU.add),
              reads=["tsl", "ushl", "vec"], writes=["tsl"])
            d_ = dst[0:nco, :] if pl is None else dst[0:nco, pl, :]
            A("act", lambda e, d_=d_, nco=nco, fn=fn: e.activation(out=d_, in_=tsl_[0:nco, :], func=fn), reads=["tsl"], writes=[dn])
        self.p.barrier()
        self.stack.close()
        self.stack = outer
        import os as _os
        cut = _os.environ.get('RWKV_CUT', '')
        if cut == 'lora':
            self.end()
            return
        f32t = lambda n: self.sb(n, [128, TB], F32)
        bft = lambda n: self.sb(n, [128, TB], BF16)
        wr = [self.sb("rw_r%d" % i, [128, 8, 128], BF16) for i in range(2)]
        wk = [self.sb("rw_k%d" % i, [128, 8, 128], BF16) for i in range(2)]
        wv = [self.sb("rw_v%d" % i, [128, 8, 128], BF16) for i in range(2)]
        ush = {x: self.sb("ush_" + x, [128, 1 + TB], F32) for x in "rkv"}
        rmask = f32t("rmask")
        maskL = self.sb("maskL", [128, 64], F32)
        A("dve", lambda e: e.memset(rmask[:], 1.0), writes=["rmask"])
        A("dve", lambda e: e.memset(rmask[:].rearrange("p (c t) -> p c t", t=64)[:, :, 0:1], 0.0), writes=["rmask"])
        A("dve", lambda e: e.tensor_scalar(out=maskL[:], in0=self.cc("mincl", 64), scalar1=-1.0, scalar2=1.0, op0=ALU.mult, op1=ALU.add), reads=["cst"], writes=["maskL"])
        mstrict = self.cc("mstrict", 64)
        mincl = self.cc("mincl", 64)
        rT, kTt, vT, lwT, aT, kk, kkn, kp, Lc, eneg, eprev, tmpA = [f32t("rk_" + n) for n in ("r", "k", "v", "lw", "a", "kk", "kkn", "kp", "L", "eneg", "eprev", "tmpA")]
        kk2 = bft("rk_kk2")
        rkb = bft("rk_rkb")
        ybf = bft("rk_ybf")
        AR = [self.sb("rk_AR%d" % i, [128, 2, TB], BF16) for i in range(2)]
        btb = [bft("rk_bt%d" % i) for i in range(2)]
        ktb = [bft("rk_kt%d" % i) for i in range(2)]
        vbf = [bft("rk_vb%d" % i) for i in range(2)]
        epos = [f32t("rk_ep%d" % i) for i in range(2)]
        yT = [f32t("rk_y%d" % i) for i in range(2)]
        bonus = [f32t("rk_bo%d" % i) for i in range(2)]
        gT = [f32t("rk_g%d" % i) for i in range(2)]
        yout = [bft("rk_yo%d" % i) for i in range(2)]
        G = 2
        gt = lambda n, sh, dt=BF16: [self.sb("%s%d" % (n, i), [128] + sh, dt) for i in range(2)]
        VK = gt("g_VK", [G, 2, 2, 64])
        AX = gt("g_AX", [G, 2, 2, 64])
        BR = gt("g_BR", [G, 2, 2, 64])
        AK = gt("g_AK", [G, 2, 64])
        RK = gt("g_RK", [G, 2, 64])
        STa = gt("g_STa", [G, 2, 2, 64])
        STb = gt("g_STb", [G, 2, 2, 64])
        PTa = gt("g_PTa", [G, 2, 64])
        PTb = gt("g_PTb", [G, 2, 64])
        AU = gt("g_AU", [G, 2, 2, 64])
        QmTbd = gt("g_Qm", [G, 128])
        RhT = gt("g_Rh", [G, 64])
        Y0T = gt("g_Y0", [G, 64], F32)
        Gbd = gt("g_Gb", [G, 128], F32)
        Gam = gt("g_Gam", [G, 64], F32)
        Rtmp = gt("g_Rtmp", [G, 64], F32)
        Gtmp = gt("g_Gtmp", [G, 64], F32)
        mkS = self.sb("mkS", [128, 2 * G, 64], BF16)
        mkI = self.sb("mkI", [128, 2 * G, 64], BF16)
        mkL = self.sb("mkL", [128, 2 * G, 64], BF16)
        for i in range(2):
            A("dve", lambda e, i=i: e.memset(QmTbd[i][:], 0.0), writes=["g_Qm%d" % i])
            A("dve", lambda e, i=i: e.memset(Gbd[i][:], 0.0), writes=["g_Gb%d" % i])
        A("dve", lambda e: e.tensor_tensor(out=mkS[0:64, :, :], in0=mstrict[0:64, None, :].to_broadcast([64, 2 * G, 64]), in1=mstrict[0:64, None, :].to_broadcast([64, 2 * G, 64]), op=ALU.mult), reads=["cst"], writes=["mkS"])
        A("dve", lambda e: e.tensor_tensor(out=mkI[0:64, :, :], in0=mincl[0:64, None, :].to_broadcast([64, 2 * G, 64]), in1=mincl[0:64, None, :].to_broadcast([64, 2 * G, 64]), op=ALU.mult), reads=["cst"], writes=["mkI"])
        A("dve", lambda e: e.tensor_tensor(out=mkL[0:64, :, :], in0=maskL[0:64, None, :].to_broadcast([64, 2 * G, 64]), in1=maskL[0:64, None, :].to_broadcast([64, 2 * G, 64]), op=ALU.mult), reads=["maskL"], writes=["mkL"])
        Hbd32 = self.sb("c_H32", [128, 128], F32)
        Hbdb = self.sb("c_Hbb", [128, 128], BF16)
        tmpH = self.sb("c_tH", [128, 128], F32)
        psB = self.psB
        bk = [0]

        def pbank():
            return 0

        def load_pair(c):
            wb = c % 2
            self.load_w(wr[wb][:], "rw_r%d" % wb, w_in[:, 4096 + c * 128:4096 + (c + 1) * 128])
            self.load_w(wk[wb][:], "rw_k%d" % wb, w_in[:, 5120 + c * 128:5120 + (c + 1) * 128])
            self.load_w(wv[wb][:], "rw_v%d" % wb, w_in[:, 6144 + c * 128:6144 + (c + 1) * 128])

        def prep_block(c, tb, A):
            wb = c % 2
            if tb == 0:
                for x in "rkv":
                    A("dve", lambda e, x=x: e.memset(ush[x][:, 0:1], 0.0), writes=["ush_" + x])
            pb = (c * NB + tb) % 2
            tsl = slice(tb * TB, (tb + 1) * TB)
            P = lambda n: "%s%d" % (n, pb)
            for (x, wt, wn, mun, dst, dn) in (("r", wr[wb], "rw_r%d" % wb, "mu_r", rT, "rk_r"), ("k", wk[wb], "rw_k%d" % wb, "mu_k", kTt, "rk_k"), ("v", wv[wb], "rw_v%d" % wb, "mu_v", vT, "rk_v")):
                bank = pbank()
                u = ush[x]
                un = "ush_" + x
                for kc in range(8):
                    A("pe", lambda e, wt=wt, kc=kc, bank=bank, tsl=tsl: e.matmul(ps[bank][:], lhsT=wt[:, kc, :], rhs=self.hbT[:, kc, tsl], start=(kc == 0), stop=(kc == 7)),
                      reads=[wn, "hbT"], writes=["ps%d" % bank])
                A("act", lambda e, u=u, bank=bank: e.activation(out=u[:, 1:1 + TB], in_=ps[bank][:], func=AF.Copy), reads=["ps%d" % bank], writes=[un])
                A("dve", lambda e, u=u: e.tensor_tensor(out=tmpA[:], in0=u[:, 0:TB], in1=u[:, 1:1 + TB], op=ALU.subtract), reads=[un], writes=["rk_tmpA"])
                A("dve", lambda e, u=u, dst=dst, mun=mun, c=c: e.scalar_tensor_tensor(out=dst[:], in0=tmpA[:], scalar=self.vc(l, mun, 1, c), in1=u[:, 1:1 + TB], op0=ALU.mult, op1=ALU.add),
                  reads=["rk_tmpA", un, "vec"], writes=[dn])
                A("act", lambda e, u=u: e.activation(out=u[:, 0:1], in_=u[:, TB:TB + 1], func=AF.Copy), reads=[un, dn], writes=[un])
            A("act", lambda e, pb=pb: e.activation(out=vbf[pb][:], in_=vT[:], func=AF.Copy), reads=["rk_v"], writes=[P("rk_vb")])
            bank = pbank()
            A("pe", lambda e, bank=bank, c=c, tsl=tsl: e.matmul(ps[bank][:], lhsT=w2s[0:64, c * 128:(c + 1) * 128], rhs=lxw[0:64, tsl], start=True, stop=True), reads=["w2s", "lxw"], writes=["ps%d" % bank])
            A("act", lambda e, bank=bank, c=c: e.activation(out=lwT[:], in_=ps[bank][:], func=AF.Sigmoid, bias=self.vc(l, "w0", 1, c)), reads=["ps%d" % bank, "vec"], writes=["rk_lw"])
            A("dve", lambda e: e.tensor_scalar(out=lwT[:], in0=lwT[:], scalar1=-math.exp(-0.5), scalar2=None, op0=ALU.mult), reads=["rk_lw"], writes=["rk_lw"])
            bank = pbank()
            A("pe", lambda e, bank=bank, c=c, tsl=tsl: e.matmul(ps[bank][:], lhsT=a2s[0:64, c * 128:(c + 1) * 128], rhs=lxa[0:64, tsl], start=True, stop=True), reads=["a2s", "lxa"], writes=["ps%d" % bank])
            A("act", lambda e, bank=bank, c=c: e.activation(out=aT[:], in_=ps[bank][:], func=AF.Sigmoid, bias=self.vc(l, "a0", 1, c)), reads=["ps%d" % bank, "vec"], writes=["rk_a"])
            bank = pbank()
            A("pe", lambda e, bank=bank, c=c, tsl=tsl: e.matmul(ps[bank][:], lhsT=g2s[:, 0, c * 128:(c + 1) * 128], rhs=lxg[:, 0, tsl], start=True, stop=False), reads=["g2s", "lxg"], writes=["ps%d" % bank])
            A("pe", lambda e, bank=bank, c=c, tsl=tsl: e.matmul(ps[bank][:], lhsT=g2s[0:32, 1, c * 128:(c + 1) * 128], rhs=lxg[0:32, 1, tsl], start=False, stop=True), reads=["g2s", "lxg"], writes=["ps%d" % bank])
            A("act", lambda e, bank=bank, pb=pb: e.activation(out=gT[pb][:], in_=ps[bank][:], func=AF.Copy), reads=["ps%d" % bank], writes=[P("rk_g")])
            A("dve", lambda e, c=c: e.tensor_scalar(out=kk[:], in0=kTt[:], scalar1=self.vc(l, "k_k", 1, c), scalar2=None, op0=ALU.mult), reads=["rk_k", "vec"], writes=["rk_kk"])
            A("act", lambda e: e.activation(out=kk2[:], in_=kk[:], func=AF.Square), reads=["rk_kk"], writes=["rk_kk2"])
            bank = pbank()
            A("pe", lambda e, bank=bank: e.matmul(ps[bank][:], lhsT=self.blk64[:], rhs=kk2[:], start=True, stop=True), reads=["blk64", "rk_kk2"], writes=["ps%d" % bank])
            A("act", lambda e, bank=bank: e.activation(out=tmpA[:], in_=ps[bank][:], func=AF.Sqrt, scale=64.0), reads=["ps%d" % bank], writes=["rk_tmpA"])
            A("dve", lambda e: e.tensor_scalar(out=tmpA[:], in0=tmpA[:], scalar1=1e-12, scalar2=None, op0=ALU.max), reads=["rk_tmpA"], writes=["rk_tmpA"])
            A("dve", lambda e: e.reciprocal(out=tmpA[:], in_=tmpA[:]), reads=["rk_tmpA"], writes=["rk_tmpA"])
            A("dve", lambda e: e.tensor_tensor(out=kkn[:], in0=kk[:], in1=tmpA[:], op=ALU.mult), reads=["rk_kk", "rk_tmpA"], writes=["rk_kkn"])
            A("dve", lambda e, c=c: e.tensor_scalar(out=tmpA[:], in0=aT[:], scalar1=-1.0, scalar2=self.vc(l, "k_a", 1, c), op0=ALU.add, op1=ALU.mult), reads=["rk_a", "vec", "rk_kkn"], writes=["rk_tmpA"])
            A("dve", lambda e: e.scalar_tensor_tensor(out=kp[:], in0=tmpA[:], scalar=1.0, in1=kTt[:], op0=ALU.add, op1=ALU.mult), reads=["rk_tmpA", "rk_k"], writes=["rk_kp"])
            A("dve", lambda e, c=c: e.scalar_tensor_tensor(out=rkb[:], in0=rT[:], scalar=self.vc(l, "r_k", 1, c), in1=kp[:], op0=ALU.mult, op1=ALU.mult), reads=["rk_r", "rk_kp", "vec"], writes=["rk_rkb"])
            bank = pbank()
            A("pe", lambda e, bank=bank: e.matmul(ps[bank][:], lhsT=self.blk64[:], rhs=rkb[:], start=True, stop=True), reads=["blk64", "rk_rkb"], writes=["ps%d" % bank])
            A("dve", lambda e, bank=bank, pb=pb: e.scalar_tensor_tensor(out=bonus[pb][:], in0=ps[bank][:], scalar=64.0, in1=vT[:], op0=ALU.mult, op1=ALU.mult), reads=["ps%d" % bank, "rk_v"], writes=[P("rk_bo")])
            A("dve", lambda e: e.tensor_tensor_scan(out=Lc[:], data0=rmask[:], data1=lwT[:], initial=0.0, op0=ALU.mult, op1=ALU.add), reads=["rmask", "rk_lw"], writes=["rk_L"])
            A("act", lambda e, pb=pb: e.activation(out=epos[pb][:], in_=Lc[:], func=AF.Exp), reads=["rk_L"], writes=[P("rk_ep")])
            A("act", lambda e: e.activation(out=eneg[:], in_=Lc[:], func=AF.Exp, scale=-1.0), reads=["rk_L"], writes=["rk_eneg"])
            A("dve", lambda e: e.tensor_tensor(out=tmpA[:], in0=Lc[:], in1=lwT[:], op=ALU.subtract), reads=["rk_L", "rk_lw", "rk_kp"], writes=["rk_tmpA"])
            A("act", lambda e: e.activation(out=eprev[:], in_=tmpA[:], func=AF.Exp), reads=["rk_tmpA"], writes=["rk_eprev"])
            A("dve", lambda e, pb=pb: e.scalar_tensor_tensor(out=AR[pb][:, 0, :], in0=kkn[:], scalar=-1.0, in1=eprev[:], op0=ALU.mult, op1=ALU.mult), reads=["rk_kkn", "rk_eprev"], writes=[P("rk_AR")])
            A("dve", lambda e: e.tensor_tensor(out=tmpA[:], in0=kkn[:], in1=aT[:], op=ALU.mult), reads=["rk_kkn", "rk_a", "rk_eprev"], writes=["rk_tmpA"])
            A("dve", lambda e, pb=pb: e.tensor_tensor(out=btb[pb][:], in0=tmpA[:], in1=eneg[:], op=ALU.mult), reads=["rk_tmpA", "rk_eneg"], writes=[P("rk_bt")])
            A("dve", lambda e, pb=pb: e.tensor_tensor(out=ktb[pb][:], in0=kp[:], in1=eneg[:], op=ALU.mult), reads=["rk_kp", "rk_eneg"], writes=[P("rk_kt")])
            A("dve", lambda e, pb=pb: e.tensor_tensor(out=AR[pb][:, 1, :], in0=rT[:], in1=epos[pb][:], op=ALU.mult), reads=["rk_r", P("rk_ep")], writes=[P("rk_AR")])

        load_pair(0)
        gch = 0
        blocks = [(c_, t_) for c_ in range(8) for t_ in range(NB)]
        rec = []
        RA = lambda *a_, **k_: rec.append((a_, k_))
        prep_block(0, 0, A)
        for bi, (c, tb) in enumerate(blocks):
            wb = c % 2
            if tb == 0:
                if c + 1 < 8:
                    load_pair(c + 1)
                A("dve", lambda e: e.memset(Hbd32[:], 0.0), writes=["c_H32"])
                A("dve", lambda e: e.memset(Hbdb[:], 0.0), writes=["c_Hbb"])
            if True:
                pb = (c * NB + tb) % 2
                tsl = slice(tb * TB, (tb + 1) * TB)
                P = lambda n: "%s%d" % (n, pb)
                del rec[:]
                if bi + 1 < len(blocks):
                    prep_block(blocks[bi + 1][0], blocks[bi + 1][1], RA)
                pend = list(rec)

                def drain(n):
                    for _ in range(n):
                        if pend:
                            a_, k_ = pend.pop(0)
                            A(*a_, **k_)
                if cut == 'prep':
                    self.end()
                    return
                def half_stages(hx, q0, pb=pb):
                    X = lambda n: "%s%d" % (n, hx)
                    cs = [slice((q0 + g) * CH, (q0 + g + 1) * CH) for g in range(G)]
                    gsl = slice(q0 * CH, (q0 + G) * CH)
                    B0 = hx * 1536
                    cA = B0
                    cB = B0 + 768
                    tb0, tb1, tb2 = X("pb_0_"), X("pb_1_"), X("pb_2_")
                    allb = [tb0, tb1]
                    tA = [tb0, tb1]
                    tB = [tb1, tb2]
                    vk, ax, br, ak, rk, sta, stb, pta, ptb, au, qm, rh, y0, gb, gam_t, rtmp, gtmp = [t[hx] for t in (VK, AX, BR, AK, RK, STa, STb, PTa, PTb, AU, QmTbd, RhT, Y0T, Gbd, Gam, Rtmp, Gtmp)]
                    stages = []

                    def st1():
                        for g in range(G):
                            base = B0 + g * 512
                            for ii, src in enumerate((vbf[pb][:, cs[g]], ktb[pb][:, cs[g]], AR[pb][:, 0, cs[g]], btb[pb][:, cs[g]])):
                                A("pe", lambda e, base=base, ii=ii, src=src: e.matmul(psB[0:64, base + ii * 128:base + (ii + 1) * 128], lhsT=src, rhs=self.ident_bf[:], start=True, stop=True),
                                  reads=[P("rk_vb"), P("rk_kt"), P("rk_AR"), P("rk_bt"), "ident_bf"], writes=allb)
                        V1 = psB[0:64, B0:B0 + G * 512].rearrange("p (g x) -> p g x", g=G)
                        A("act", lambda e: e.activation(out=vk[0:64].rearrange("p g a h x -> p g (a h x)"), in_=V1[:, :, 0:256], func=AF.Copy), reads=allb, writes=[X("g_VK")])
                        A("act", lambda e: e.activation(out=ax[0:64, :, :, 0, :], in_=V1[:, :, 256:384].rearrange("p g (h x) -> p g h x", h=2), func=AF.Copy), reads=allb, writes=[X("g_AX")])
                        A("act", lambda e: e.activation(out=br[0:64, :, :, 0, :], in_=V1[:, :, 384:512].rearrange("p g (h x) -> p g h x", h=2), func=AF.Copy), reads=allb, writes=[X("g_BR")])
                    stages.append(st1)

                    def st2():
                        for h in range(2):
                            hs = slice(64 * h, 64 * h + 64)
                            tk = [tb0] if h == 0 else [tb1]
                            for g in range(G):
                                base = B0 + h * 512 + g * 256
                                A("pe", lambda e, hs=hs, base=base, g=g: e.matmul(psB[0:64, base:base + 128], lhsT=btb[pb][hs, cs[g]], rhs=AR[pb][hs, :, cs[g]], start=True, stop=True), reads=[P("rk_bt"), P("rk_AR")], writes=tk)
                                A("pe", lambda e, hs=hs, base=base, g=g: e.matmul(psB[0:64, base + 128:base + 256], lhsT=ktb[pb][hs, cs[g]], rhs=AR[pb][hs, :, cs[g]], start=True, stop=True), reads=[P("rk_kt"), P("rk_AR")], writes=tk)
                        V2 = psB[0:64, B0:B0 + 1024].rearrange("p (h g x) -> p h g x", h=2, g=G)
                        hg = lambda t: t[0:64].rearrange("p g h x -> p h g x")
                        mS = mkS[0:64].rearrange("p (h g) x -> p h g x", h=2)
                        mI = mkI[0:64].rearrange("p (h g) x -> p h g x", h=2)
                        A("dve", lambda e: e.tensor_tensor(out=sta[0:64].rearrange("p g h a x -> p h g a x")[:, :, :, 0, :], in0=V2[:, :, :, 0:64], in1=mS, op=ALU.mult), reads=allb + ["mkS"], writes=[X("g_STa")])
                        A("dve", lambda e: e.tensor_tensor(out=br[0:64].rearrange("p g h a x -> p h g a x")[:, :, :, 1, :], in0=V2[:, :, :, 64:128], in1=mI, op=ALU.mult), reads=allb + ["mkI"], writes=[X("g_BR")])
                        A("dve", lambda e: e.tensor_tensor(out=hg(ak), in0=V2[:, :, :, 128:192], in1=mS, op=ALU.mult), reads=allb + ["mkS"], writes=[X("g_AK")])
                        A("dve", lambda e: e.tensor_tensor(out=hg(rk), in0=V2[:, :, :, 192:256], in1=mI, op=ALU.mult), reads=allb + ["mkI"], writes=[X("g_RK")])
                        A("act", lambda e: e.activation(out=sta[0:64].rearrange("p g h a x -> p (g h) a x")[:, :, 1, :], in_=self.ident_bf[0:64, None, 0:64].to_broadcast([64, 2 * G, 64]), func=AF.Copy), reads=["ident_bf"], writes=[X("g_STa")])
                        for g in range(G):
                            for h in range(2):
                                o = B0 + 1024 + (g * 2 + h) * 64
                                A("pe", lambda e, g=g, h=h, o=o: e.matmul(psB[0:64, o:o + 64], lhsT=sta[0:64, g, h, 0, :], rhs=self.ident_bf[0:64, 0:64], start=True, stop=True), reads=[X("g_STa"), "ident_bf"], writes=[tb2])
                        A("act", lambda e: e.activation(out=pta[0:64].rearrange("p g h x -> p (g h x)"), in_=psB[0:64, B0 + 1024:B0 + 1024 + G * 128], func=AF.Copy), reads=[tb2], writes=[X("g_PTa")])
                    stages.append(st2)
                    STs = (sta, stb)
                    STn = (X("g_STa"), X("g_STb"))
                    PTs = (pta, ptb)
                    PTn = (X("g_PTa"), X("g_PTb"))
                    V3 = psB[0:64, cA:cA + G * 384].rearrange("p (g x) -> p g x", g=G)
                    V3a = V3[:, :, 0:256].rearrange("p g (h a x) -> p g h a x", h=2, a=2)

                    def mk_level(kk_):
                        def lv():
                            cur, nxt = STs[kk_ % 2], STs[(kk_ + 1) % 2]
                            cn, nn = STn[kk_ % 2], STn[(kk_ + 1) % 2]
                            pcur, pnxt = PTs[kk_ % 2], PTs[(kk_ + 1) % 2]
                            pcn, pnn = PTn[kk_ % 2], PTn[(kk_ + 1) % 2]
                            for g in range(G):
                                for h in range(2):
                                    o = cA + g * 384 + h * 128
                                    if kk_ < 5:
                                        A("pe", lambda e, g=g, h=h, o=o: e.matmul(psB[0:64, o:o + 128], lhsT=pcur[0:64, g, h, :], rhs=cur[0:64, g, h, :, :], start=True, stop=False), reads=[cn, pcn], writes=tA)
                                        A("pe", lambda e, g=g, h=h, o=o: e.matmul(psB[0:64, o + 64:o + 128], lhsT=self.ident_bf[0:64, 0:64], rhs=cur[0:64, g, h, 1, :], start=False, stop=True), reads=[cn, "ident_bf"], writes=tA)
                                        o2 = cA + g * 384 + 256 + h * 64
                                        A("pe", lambda e, g=g, h=h, o2=o2: e.matmul(psB[0:64, o2:o2 + 64], lhsT=cur[0:64, g, h, 0, :], rhs=pcur[0:64, g, h, :], start=True, stop=True), reads=[cn, pcn], writes=tA)
                                    else:
                                        A("pe", lambda e, g=g, h=h, o=o: e.matmul(psB[0:64, o + 64:o + 128], lhsT=pcur[0:64, g, h, :], rhs=cur[0:64, g, h, 1, :], start=True, stop=False), reads=[cn, pcn], writes=tA)
                                        A("pe", lambda e, g=g, h=h, o=o: e.matmul(psB[0:64, o + 64:o + 128], lhsT=self.ident_bf[0:64, 0:64], rhs=cur[0:64, g, h, 1, :], start=False, stop=True), reads=[cn, "ident_bf"], writes=tA)
                            if kk_ < 5:
                                A("act", lambda e: e.activation(out=nxt[0:64].rearrange("p g h a x -> p g (h a x)"), in_=V3[:, :, 0:256], func=AF.Copy), reads=tA, writes=[nn])
                                A("act", lambda e: e.activation(out=pnxt[0:64], in_=V3[:, :, 256:384].rearrange("p g (h x) -> p g h x", h=2), func=AF.Copy), reads=tA, writes=[pnn])
                            else:
                                A("act", lambda e: e.activation(out=nxt[0:64, :, :, 1, :], in_=V3a[:, :, :, 1, :], func=AF.Copy), reads=tA, writes=[nn])
                        return lv
                    for kk_ in range(6):
                        stages.append(mk_level(kk_))
                    TTf = STs[0]
                    TTn = STn[0]
                    V4 = psB[0:64, cB:cB + G * 384].rearrange("p (g x) -> p g x", g=G)

                    def st4x():
                        for g in range(G):
                            for h in range(2):
                                o = cB + g * 384 + h * 64
                                A("pe", lambda e, g=g, h=h, o=o: e.matmul(psB[0:64, o:o + 64], lhsT=ak[0:64, g, h, :], rhs=vk[0:64, g, 0, h, :], start=True, stop=True), reads=[X("g_AK"), X("g_VK")], writes=tB)
                        A("act", lambda e: e.activation(out=ax[0:64, :, :, 1, :], in_=V4[:, :, 0:128].rearrange("p g (h x) -> p g h x", h=2), func=AF.Copy), reads=tB, writes=[X("g_AX")])
                    stages.append(st4x)

                    def st4u():
                        for g in range(G):
                            for h in range(2):
                                o = cB + g * 384 + 128 + h * 128
                                A("pe", lambda e, g=g, h=h, o=o: e.matmul(psB[0:64, o:o + 128], lhsT=TTf[0:64, g, h, 1, :], rhs=ax[0:64, g, h, :, :], start=True, stop=True), reads=[TTn, X("g_AX")], writes=tB)
                        A("act", lambda e: e.activation(out=au[0:64].rearrange("p g h a x -> p g (h a x)"), in_=V4[:, :, 128:384], func=AF.Copy), reads=tB, writes=[X("g_AU")])
                    stages.append(st4u)
                    V5 = psB[:, cA:cA + G * 384].rearrange("p (g x) -> p g x", g=G)

                    def st4b():
                        for g in range(G):
                            for h in range(2):
                                hs = slice(64 * h, 64 * h + 64)
                                o = cA + g * 384
                                A("pe", lambda e, g=g, h=h, hs=hs, o=o: e.matmul(psB[hs, o:o + 128], lhsT=au[0:64, g, h, 0, :], rhs=br[0:64, g, h, :, :], start=True, stop=True), reads=[X("g_AU"), X("g_BR")], writes=tA)
                                A("pe", lambda e, g=g, h=h, hs=hs, o=o: e.matmul(psB[hs, o + 128:o + 192], lhsT=au[0:64, g, h, 1, :], rhs=br[0:64, g, h, 1, :], start=True, stop=False), reads=[X("g_AU"), X("g_BR")], writes=tA)
                                A("pe", lambda e, g=g, h=h, hs=hs, o=o: e.matmul(psB[hs, o + 128:o + 192], lhsT=vk[0:64, g, 0, h, :], rhs=rk[0:64, g, h, :], start=False, stop=True), reads=[X("g_VK"), X("g_RK")], writes=tA)
                                A("pe", lambda e, g=g, h=h, hs=hs, o=o: e.matmul(psB[hs, o + 192:o + 256], lhsT=br[0:64, g, h, 0, :], rhs=au[0:64, g, h, 1, :], start=True, stop=False), reads=[X("g_AU"), X("g_BR")], writes=tA)
                                A("pe", lambda e, g=g, h=h, hs=hs, o=o: e.matmul(psB[hs, o + 192:o + 256], lhsT=vk[0:64, g, 1, h, :], rhs=vk[0:64, g, 0, h, :], start=False, stop=True), reads=[X("g_VK")], writes=tA)
                        A("act", lambda e: e.activation(out=qm[0:64, :, 0:64], in_=V5[0:64, :, 0:64], func=AF.Copy), reads=tA, writes=[X("g_Qm")])
                        A("act", lambda e: e.activation(out=qm[64:128, :, 64:128], in_=V5[64:128, :, 0:64], func=AF.Copy), reads=tA, writes=[X("g_Qm")])
                        A("act", lambda e: e.activation(out=rtmp[:], in_=V5[:, :, 64:128], func=AF.Copy), reads=tA, writes=[X("g_Rtmp")])
                        A("dve", lambda e: e.tensor_tensor(out=rh[:], in0=rtmp[:], in1=AR[pb][:, 1, gsl].rearrange("p (g t) -> p g t", g=G), op=ALU.add), reads=[X("g_Rtmp"), P("rk_AR")], writes=[X("g_Rh")])
                        A("act", lambda e: e.activation(out=y0[:], in_=V5[:, :, 128:192], func=AF.Copy), reads=tA, writes=[X("g_Y0")])
                        A("act", lambda e: e.activation(out=gtmp[:], in_=V5[:, :, 192:256], func=AF.Copy), reads=tA, writes=[X("g_Gtmp")])
                        gam_all = epos[pb][:, gsl].rearrange("p (g t) -> p g t", g=G)[:, :, CH - 1:CH].to_broadcast([128, G, 64])
                        A("act", lambda e: e.activation(out=gam_t[:], in_=gam_all, func=AF.Copy), reads=[P("rk_ep")], writes=[X("g_Gam")])
                        for h in range(2):
                            hs = slice(64 * h, 64 * h + 64)
                            A("dve", lambda e, hs=hs, h=h: e.tensor_tensor(out=gb[hs, :, 64 * h:64 * h + 64], in0=gtmp[hs, :, :], in1=gam_t[hs, :, :], op=ALU.mult), reads=[X("g_Gtmp"), X("g_Gam")], writes=[X("g_Gb")])
                    stages.append(st4b)

                    def st5():
                        for g in range(G):
                            gam = epos[pb][:, (q0 + g) * CH + CH - 1:(q0 + g) * CH + CH]
                            A("pe", lambda e, g=g: e.matmul(ps[7][:, 0:128], lhsT=qm[:, g, :], rhs=Hbdb[:], start=True, stop=True), reads=[X("g_Qm"), "c_Hbb"], writes=["ps7"])
                            A("pe", lambda e, g=g: e.matmul(ps[7][:, 128:192], lhsT=Hbdb[:], rhs=rh[:, g, :], start=True, stop=True), reads=[X("g_Rh"), "c_Hbb"], writes=["ps7"])
                            A("dve", lambda e, g=g: e.tensor_tensor(out=yT[pb][:, cs[g]], in0=ps[7][:, 128:192], in1=y0[:, g, :], op=ALU.add), reads=["ps7", X("g_Y0")], writes=[P("rk_y")])
                            A("dve", lambda e: e.tensor_tensor(out=tmpH[:], in0=ps[7][:, 0:128], in1=Hbd32[:], op=ALU.add), reads=["ps7", "c_H32"], writes=["c_tH"])
                            A("dve", lambda e, g=g, gam=gam: e.scalar_tensor_tensor(out=Hbd32[:], in0=tmpH[:], scalar=gam, in1=gb[:, g, :], op0=ALU.mult, op1=ALU.add), reads=["c_tH", X("g_Gb"), P("rk_ep")], writes=["c_H32"])
                            A("act", lambda e: e.activation(out=Hbdb[:], in_=Hbd32[:], func=AF.Copy), reads=["c_H32"], writes=["c_Hbb"])
                    stages.append(st5)
                    return stages

                for gi in range(TB // CH // (2 * G)):
                    gch += 1
                    if cut.startswith('grp') and gch > int(cut[3:]):
                        self.end()
                        return
                    sx = half_stages(0, gi * 2 * G)
                    sy = half_stages(1, gi * 2 * G + G)
                    nst = int(_os.environ.get('RWKV_NST', '99'))
                    for si, (fx, fy) in enumerate(zip(sx, sy)):
                        if si >= nst:
                            self.end()
                            return
                        fx()
                        fy()
                        drain(4)
                drain(10 ** 6)
                if cut.startswith('prepost') and (c * NB + tb + 1) >= int(cut[7:]):
                    self.end()
                    return
                y = yT[pb]
                A("act", lambda e, y=y: e.activation(out=ybf[:], in_=y[:], func=AF.Copy), reads=[P("rk_y")], writes=["rk_ybf"])
                bank = pbank()
                A("pe", lambda e, bank=bank: e.matmul(ps[bank][:], lhsT=self.blk64[:], rhs=ybf[:], start=True, stop=True), reads=["blk64", "rk_ybf"], writes=["ps%d" % bank])
                A("dve", lambda e, y=y, bank=bank: e.tensor_tensor(out=y[:], in0=y[:], in1=ps[bank][:], op=ALU.subtract), reads=[P("rk_y"), "ps%d" % bank], writes=[P("rk_y")])
                A("act", lambda e, y=y: e.activation(out=ybf[:], in_=y[:], func=AF.Square), reads=[P("rk_y")], writes=["rk_ybf"])
                bank = pbank()
                A("pe", lambda e, bank=bank: e.matmul(ps[bank][:], lhsT=self.blk64[:], rhs=ybf[:], start=True, stop=True), reads=["blk64", "rk_ybf"], writes=["ps%d" % bank])
                A("act", lambda e, bank=bank: e.activation(out=tmpA[:], in_=ps[bank][:], func=AF.Sqrt, bias=self.epsc[:, 2:3]), reads=["ps%d" % bank, "epsc"], writes=["rk_tmpA"])
                A("dve", lambda e: e.reciprocal(out=tmpA[:], in_=tmpA[:]), reads=["rk_tmpA"], writes=["rk_tmpA"])
                A("dve", lambda e, y=y: e.tensor_tensor(out=y[:], in0=y[:], in1=tmpA[:], op=ALU.mult), reads=[P("rk_y"), "rk_tmpA"], writes=[P("rk_y")])
                A("act", lambda e, y=y, c=c: e.activation(out=y[:], in_=y[:], func=AF.Identity, scale=self.vc(l, "lnx_g", 1, c), bias=self.vc(l, "lnx_b", 1, c)), reads=[P("rk_y"), "vec"], writes=[P("rk_y")])
                A("dve", lambda e, y=y, pb=pb: e.tensor_tensor(out=y[:], in0=y[:], in1=bonus[pb][:], op=ALU.add), reads=[P("rk_y"), P("rk_bo")], writes=[P("rk_y")])
                A("dve", lambda e, y=y, pb=pb: e.tensor_tensor(out=yout[pb][:], in0=y[:], in1=gT[pb][:], op=ALU.mult), reads=[P("rk_y"), P("rk_g")], writes=[P("rk_yo")])
                A("sp", lambda e, pb=pb, c=c, tsl=tsl: e.dma_start(out=self.rtd[c * 128:(c + 1) * 128, tsl], in_=yout[pb][:]), reads=[P("rk_yo")], writes=["rtd%d" % pb], dma=True)
                if cut.startswith('post') and (c * NB + tb + 1) >= int(cut[4:]):
                    self.end()
                    return
        self.end()

    def phase_merge(self, l):
        A = self.A
        ps = self.ps
        self.begin()
        wbr = [self.sb("wbr%d" % j, [128, 8, D], BF16) for j in range(3)]
        srcw = (self.w["w_br_attn"][l], self.w["w_br_pool"][l], self.w["w_br_rwkv"][l])
        srcd = (self.otd, self.ptd, self.rtd)
        for hf in range(2):
            for j in range(3):
                A("pool", lambda e, j=j, hf=hf: e.dma_start(out=wbr[j][:, :, hf * 512:(hf + 1) * 512], in_=srcw[j][:, hf * 512:(hf + 1) * 512].rearrange("(c p) n -> p c n", p=128)),
                  writes=["wbr%d" % j], dma=True)
        ob = [self.sb("mg_o%d" % j, [128, 8, TB], BF16) for j in range(3)]
        gbuf = self.sb("mg_g", [128, 3, 8, TB], BF16)
        tt = [self.sb("mg_t%d" % j, [128, TB], F32) for j in range(3)]
        mT = [self.sb("mg_m%d" % i, [128, 8, TB], BF16) for i in range(2)]
        k = 0
        for tb in range(NB):
            tsl = slice(tb * TB, (tb + 1) * TB)
            for j in range(3):
                A("sp", lambda e, j=j, tsl=tsl: e.dma_start(out=ob[j][:], in_=srcd[j].rearrange("(c p) t -> p c t", p=128)[:, :, tsl]), writes=["mg_o%d" % j], dma=True)
                A("sp", lambda e, j=j, tsl=tsl: e.dma_start(out=gbuf[:, j, :, :], in_=self.gd[j * 1024:(j + 1) * 1024, :].rearrange("(c p) t -> p c t", p=128)[:, :, tsl]), writes=["mg_g"], dma=True)
            mt = mT[tb % 2]
            mn = "mg_m%d" % (tb % 2)
            for m in range(8):
                for j in range(3):
                    bank = (k % 2) * 3 + j
                    for c in range(8):
                        A("pe", lambda e, j=j, c=c, m=m, bank=bank: e.matmul(ps[bank][:], lhsT=wbr[j][:, c, m * 128:(m + 1) * 128], rhs=ob[j][:, c, :], start=(c == 0), stop=(c == 7)),
                          reads=["wbr%d" % j, "mg_o%d" % j], writes=["ps%d" % bank])
                    A("dve", lambda e, j=j, m=m, bank=bank: e.tensor_tensor(out=tt[j][:], in0=ps[bank][:], in1=gbuf[:, j, m, :], op=ALU.mult), reads=["ps%d" % bank, "mg_g"], writes=["mg_t%d" % j])
                k += 1
                A("pool", lambda e: e.tensor_tensor(out=tt[0][:], in0=tt[0][:], in1=tt[1][:], op=ALU.add), reads=["mg_t0", "mg_t1"], writes=["mg_t0"])
                A("pool", lambda e, mt=mt, m=m: e.tensor_tensor(out=mt[:, m, :], in0=tt[0][:], in1=tt[2][:], op=ALU.add), reads=["mg_t0", "mg_t2"], writes=[mn])
            A("sp", lambda e, mt=mt, tsl=tsl: e.dma_start(out=self.mgd.rearrange("(c p) t -> p c t", p=128)[:, :, tsl], in_=mt[:]), reads=[mn], writes=["mgd"], dma=True)
        self.end()

    def proj_res_ln(self, l, wres, wname, load_src, gname, bname, final=False, tmps=None, zT=None):
        A = self.A
        ps = self.ps
        for tb in range(NB):
            tsl = slice(tb * TB, (tb + 1) * TB)
            z = zT[tb % 2]
            zn = "zT%d" % (tb % 2)
            A("sp", lambda e, z=z, tsl=tsl: e.dma_start(out=z[:], in_=self.h32d.rearrange("(c p) t -> p c t", p=128)[:, :, tsl]), reads=["h32d"], writes=[zn], dma=True)
            src, sn = load_src(tb)
            for m in range(8):
                bank = m % 2
                for c in range(8):
                    A("pe", lambda e, src=src, c=c, m=m, bank=bank: e.matmul(ps[bank][:], lhsT=wres[:, c, m * 128:(m + 1) * 128], rhs=src[:, c, :], start=(c == 0), stop=(c == 7)),
                      reads=[wname, sn], writes=["ps%d" % bank])
                A("dve", lambda e, z=z, m=m, bank=bank: e.scalar_tensor_tensor(out=z[:, m, :], in0=z[:, m, :], scalar=DN_ALPHA, in1=ps[bank][:], op0=ALU.mult, op1=ALU.add),
                  reads=[zn, "ps%d" % bank], writes=[zn])
            self.ln_T(z, zn, self.vc(l, gname, 8), self.vc(l, bname, 8), tb, tmps, final=final)

    def phase_wout(self, l):
        A = self.A
        self.begin()
        tmps = self.ln_tmps()
        wo = self.sb("wo", [128, 8, D], BF16)
        for hf in range(2):
            A("pool", lambda e, hf=hf: e.dma_start(out=wo[:, :, hf * 512:(hf + 1) * 512], in_=self.w["w_out"][l][:, hf * 512:(hf + 1) * 512].rearrange("(c p) n -> p c n", p=128)),
              writes=["wo"], dma=True)
        zT = [self.sb("zT%d" % i, [128, 8, TB], F32) for i in range(2)]
        mi = [self.sb("mi%d" % i, [128, 8, TB], BF16) for i in range(2)]

        def load_src(tb):
            b = mi[tb % 2]
            n = "mi%d" % (tb % 2)
            A("sp", lambda e, b=b, tb=tb: e.dma_start(out=b[:], in_=self.mgd.rearrange("(c p) t -> p c t", p=128)[:, :, tb * TB:(tb + 1) * TB]), reads=["mgd"], writes=[n], dma=True)
            return b, n
        self.proj_res_ln(l, wo, "wo", load_src, "ln1_g", "ln1_b", tmps=tmps, zT=zT)
        self.end()

    def phase_xattn(self, l, b_unused=None):
        A = self.A
        ps = self.ps
        self.begin()
        tmps = self.ln_tmps()
        ident = self.cc("ident", 128)
        memN = self.sb("memN", [128, 2, D], F32)
        memT = self.sb("memT", [128, 8, 256], BF16)
        kxT = self.sb("kxT", [128, 8, 256], BF16)
        vx = self.sb("vx", [128, 2, D], BF16)
        wq = self.sb("xwq", [128, 8, D], BF16)
        wo = self.sb("xwo", [128, 8, D], BF16)
        wst = [self.sb("xws%d" % i, [128, 8, 512], BF16) for i in range(2)]
        A("sp", lambda e: e.dma_start(out=memN[:], in_=self.mem.rearrange("(t p) d -> p t d", p=128)), writes=["memN"], dma=True)
        for c in range(8):
            bank = c % 2
            for t2 in range(2):
                A("pe", lambda e, c=c, t2=t2, bank=bank: e.transpose(ps[bank][:, t2 * 128:(t2 + 1) * 128], memN[:, t2, c * 128:(c + 1) * 128], ident), reads=["memN", "cst"], writes=["ps%d" % bank])
            A("act", lambda e, c=c, bank=bank: e.activation(out=memT[:, c, :], in_=ps[bank][:, 0:256], func=AF.Copy), reads=["ps%d" % bank], writes=["memT"])
        w_xkv = self.w["w_xkv"][l]
        for blk in range(4):
            ws = wst[blk % 2]
            wn = "xws%d" % (blk % 2)
            self.load_w(ws[:], wn, w_xkv[:, blk * 512:(blk + 1) * 512])
            if blk < 2:
                for mm_ in range(4):
                    m = blk * 4 + mm_
                    bank = 2 + (m % 2)
                    for c in range(8):
                        A("pe", lambda e, ws=ws, c=c, mm_=mm_, bank=bank: e.matmul(ps[bank][:, 0:256], lhsT=ws[:, c, mm_ * 128:(mm_ + 1) * 128], rhs=memT[:, c, :], start=(c == 0), stop=(c == 7)),
                          reads=[wn, "memT"], writes=["ps%d" % bank])
                    A("act", lambda e, m=m, bank=bank: e.activation(out=kxT[:, m, :], in_=ps[bank][:, 0:256], func=AF.Copy), reads=["ps%d" % bank], writes=["kxT"])
            else:
                for kt in range(2):
                    bank = 2 + kt
                    for c in range(8):
                        A("pe", lambda e, ws=ws, c=c, kt=kt, bank=bank: e.matmul(ps[bank][:], lhsT=memT[:, c, kt * 128:(kt + 1) * 128], rhs=ws[:, c, :], start=(c == 0), stop=(c == 7)),
                          reads=[wn, "memT"], writes=["ps%d" % bank])
                    A("act", lambda e, kt=kt, bank=bank, blk=blk: e.activation(out=vx[:, kt, (blk - 2) * 512:(blk - 1) * 512], in_=ps[bank][:], func=AF.Copy), reads=["ps%d" % bank], writes=["vx"])
        for (dst, dn, src) in ((wq, "xwq", self.w["w_xq"][l]), (wo, "xwo", self.w["w_xo"][l])):
            for hf in range(2):
                A("pool", lambda e, dst=dst, src=src, hf=hf: e.dma_start(out=dst[:, :, hf * 512:(hf + 1) * 512], in_=src[:, hf * 512:(hf + 1) * 512].rearrange("(c p) n -> p c n", p=128)),
                  writes=[dn], dma=True)
        qx = [self.sb("qx%d" % i, [128, 8, TB], BF16) for i in range(1)] * 2
        ox = [self.sb("ox%d" % i, [128, 8, TB], BF16) for i in range(1)] * 2
        eT = [self.sb("xe%d" % i, [128, TB], BF16) for i in range(3)]
        rr = self.sb("xrr", [128, TB], F32)
        zT = [self.sb("zT%d" % i, [128, 8, TB], F32) for i in range(2)]
        ecnt = [0]

        def load_src(tb):
            tsl = slice(tb * TB, (tb + 1) * TB)
            q = qx[tb % 2]
            qn = "qx0"
            o = ox[tb % 2]
            on = "ox0"
            for m in range(8):
                bank = m % 2
                for c in range(8):
                    A("pe", lambda e, c=c, m=m, bank=bank, tsl=tsl: e.matmul(ps[bank][:], lhsT=wq[:, c, m * 128:(m + 1) * 128], rhs=self.hbT[:, c, tsl], start=(c == 0), stop=(c == 7)),
                      reads=["xwq", "hbT"], writes=["ps%d" % bank])
                A("act", lambda e, q=q, m=m, bank=bank: e.activation(out=q[:, m, :], in_=ps[bank][:], func=AF.Copy), reads=["ps%d" % bank], writes=[qn])
            for h in range(4):
                ets = []
                for kt in range(2):
                    sbk = 2 + (ecnt[0] % 2)
                    et = eT[ecnt[0] % 3]
                    en = "xe%d" % (ecnt[0] % 3)
                    ecnt[0] += 1
                    ets.append((et, en))
                    for mm_ in range(2):
                        A("pe", lambda e, q=q, h=h, kt=kt, mm_=mm_, sbk=sbk: e.matmul(ps[sbk][:], lhsT=kxT[:, 2 * h + mm_, kt * 128:(kt + 1) * 128], rhs=q[:, 2 * h + mm_, :],
                                                                                 start=(mm_ == 0), stop=(mm_ == 1)), reads=["kxT", qn], writes=["ps%d" % sbk])
                    A("act", lambda e, et=et, sbk=sbk: e.activation(out=et[:], in_=ps[sbk][:], func=AF.Exp, scale=1.0 / 16.0), reads=["ps%d" % sbk], writes=[en])
                for kt in range(2):
                    et, en = ets[kt]
                    for mm_ in range(2):
                        A("pe", lambda e, et=et, h=h, kt=kt, mm_=mm_: e.matmul(ps[4 + mm_][:], lhsT=vx[:, kt, (2 * h + mm_) * 128:(2 * h + mm_ + 1) * 128], rhs=et[:], start=(kt == 0), stop=(kt == 1)),
                          reads=["vx", en], writes=["ps%d" % (4 + mm_)])
                    A("pe", lambda e, et=et, kt=kt: e.matmul(ps[6][:], lhsT=self.ones_k[:], rhs=et[:], start=(kt == 0), stop=(kt == 1)), reads=["ones_k", en], writes=["ps6"])
                A("dve", lambda e: e.reciprocal(out=rr[:], in_=ps[6][:]), reads=["ps6"], writes=["xrr"])
                for mm_ in range(2):
                    A("dve", lambda e, o=o, h=h, mm_=mm_: e.tensor_tensor(out=o[:, 2 * h + mm_, :], in0=ps[4 + mm_][:], in1=rr[:], op=ALU.mult), reads=["ps%d" % (4 + mm_), "xrr"], writes=[on])
            return o, on
        self.proj_res_ln(l, wo, "xwo", load_src, "ln2_g", "ln2_b", tmps=tmps, zT=zT)
        self.end()

    def phase_ffn(self, l, final=False):
        A = self.A
        ps = self.ps
        self.begin()
        tmps = self.ln_tmps()
        w1 = [self.sb("fw1_%d" % i, [128, 8, 512], BF16) for i in range(2)]
        w2 = [self.sb("fw2_%d" % i, [128, 32, 128], BF16) for i in range(2)]
        h1 = self.sb("fh1", [128, 32, 2 * TB], BF16)
        rl = [self.sb("frl%d" % i, [128, TB], F32) for i in range(2)]
        zT = [self.sb("zT%d" % i, [128, 8, TB], F32) for i in range(2)]
        if final:
            self.outN = [self.sb("outN%d" % i, [128, D], F32) for i in range(2)]
        w_ff1 = self.w["w_ff1"][l]
        w_ff2 = self.w["w_ff2"][l]
        k = 0
        for th in range(NB // 2):
            tbs = (2 * th, 2 * th + 1)
            for t2, tb in enumerate(tbs):
                tsl = slice(tb * TB, (tb + 1) * TB)
                A("sp", lambda e, t2=t2, tsl=tsl: e.dma_start(out=zT[t2][:], in_=self.h32d.rearrange("(c p) t -> p c t", p=128)[:, :, tsl]), reads=["h32d"], writes=["zT%d" % t2], dma=True)
            for fb in range(8):
                ws = w1[fb % 2]
                wn = "fw1_%d" % (fb % 2)
                self.load_w(ws[:], wn, w_ff1[:, fb * 512:(fb + 1) * 512])
                for ff in range(4):
                    f = fb * 4 + ff
                    for t2, tb in enumerate(tbs):
                        tsl = slice(tb * TB, (tb + 1) * TB)
                        bank = k % 2
                        r = rl[k % 2]
                        rn = "frl%d" % (k % 2)
                        k += 1
                        for c in range(8):
                            A("pe", lambda e, ws=ws, c=c, ff=ff, bank=bank, tsl=tsl: e.matmul(ps[bank][:], lhsT=ws[:, c, ff * 128:(ff + 1) * 128], rhs=self.hbT[:, c, tsl], start=(c == 0), stop=(c == 7)),
                              reads=[wn, "hbT"], writes=["ps%d" % bank])
                        A("act", lambda e, r=r, bank=bank: e.activation(out=r[:], in_=ps[bank][:], func=AF.Relu), reads=["ps%d" % bank], writes=[rn])
                        A("dve", lambda e, r=r, f=f, bank=bank, t2=t2: e.tensor_tensor(out=h1[:, f, t2 * TB:(t2 + 1) * TB], in0=ps[bank][:], in1=r[:], op=ALU.mult), reads=["ps%d" % bank, rn], writes=["fh1"])
            for m in range(8):
                ws = w2[m % 2]
                wn = "fw2_%d" % (m % 2)
                A("pool", lambda e, ws=ws, m=m: e.dma_start(out=ws[:], in_=w_ff2[:, m * 128:(m + 1) * 128].rearrange("(f p) n -> p f n", p=128)), writes=[wn], dma=True)
                for t2 in range(2):
                    bank = 2 + t2
                    for f in range(32):
                        A("pe", lambda e, ws=ws, f=f, bank=bank, t2=t2: e.matmul(ps[bank][:], lhsT=ws[:, f, :], rhs=h1[:, f, t2 * TB:(t2 + 1) * TB], start=(f == 0), stop=(f == 31)), reads=[wn, "fh1"], writes=["ps%d" % bank])
                    A("dve", lambda e, m=m, bank=bank, t2=t2: e.scalar_tensor_tensor(out=zT[t2][:, m, :], in0=zT[t2][:, m, :], scalar=DN_ALPHA, in1=ps[bank][:], op0=ALU.mult, op1=ALU.add),
                      reads=["zT%d" % t2, "ps%d" % bank], writes=["zT%d" % t2])
            for t2, tb in enumerate(tbs):
                self.ln_T(zT[t2], "zT%d" % t2, self.vc(l, "ln3_g", 8), self.vc(l, "ln3_b", 8), tb, tmps, final=final)
        self.end()

    def finish(self):
        self.p.add("sp", None, extra_deps=list(self.p.pending_dma))
        self.p.emit()


def build(dbg=False, stop=None, skip=()):
    k = K(dbg=dbg, stop=stop)
    k.setup()
    k.phase_ln_in()
    phases = []
    for l in range(DEPTH):
        phases += [("attn", l), ("pool", l), ("gates", l), ("rwkv", l), ("merge", l), ("wout", l), ("xattn", l), ("ffn", l)]
    for (nm, l) in phases:
        if nm not in skip:
            if nm == "ffn":
                k.phase_ffn(l, final=(l == DEPTH - 1))
            else:
                getattr(k, "phase_" + nm)(l)
        if stop == "%s%d" % (nm, l):
            break
    k.finish()
    return k.nc


def kernel(**inputs):
    inputs = {k: np.asarray(v) for k, v in inputs.items()}
    nc = build()
    consts = _host_consts(inputs["ln_in_g"], inputs["ln_in_b"])
    vecs = _host_vecs(inputs)
    in_maps = []
    for b in range(8):
        m = {"x": np.ascontiguousarray(inputs["x"][b]), "mem": np.ascontiguousarray(inputs["mem"][b]),
             "pos": np.ascontiguousarray(np.broadcast_to(inputs["positions"][b][None, :], (128, S)).astype(np.int32)),
             "consts": consts, "vecs": vecs}
        for n, _ in WEIGHTS:
            m[n] = inputs[n]
        in_maps.append(m)
    res = run_bass_kernel_spmd(nc, in_maps, core_ids=list(range(8)))
    return np.stack([r["out"] for r in res.results], 0).astype(np.float32)
```
